# Optimizing a Trainium2 kernel written in Bass

```python
import jax, jax.numpy as jnp
from jax import lax
import numpy as np

D_MODEL = 1024
BATCH = 8
SEQ = 2048
DEPTH = 2
DEC_BATCH = 128
DEC_SEQ = 8
PAST_LEN = 16384
PAGE_SIZE = 128

SSD_HEADS = 16
SSD_HEAD_DIM = 64
SSD_INNER = SSD_HEADS * SSD_HEAD_DIM
SSD_GROUPS = 4
SSD_STATE = 128
SSD_CONV = 4
SSD_CHUNK = 128
SSD_CONV_DIM = SSD_INNER + 2 * SSD_GROUPS * SSD_STATE
CF_DIM = D_MODEL
CF_CONV = 31
PLE_DIM = 256
PEER_HEADS = 8
PEER_KEYS = 128
PEER_EXPERTS = PEER_KEYS * PEER_KEYS
PEER_QDIM = 256
PEER_HALF = PEER_QDIM // 2
PEER_TOPK = 16
PEER_BLOCK = 256
COL_Z = SSD_INNER
COL_XBC = COL_Z + SSD_CONV_DIM
COL_DT = COL_XBC + SSD_HEADS
COL_GLU = COL_DT + 2 * CF_DIM
IN_COLS = COL_GLU + 2 * D_MODEL
EPS = 1e-6

kernel_name = 'ssd_conformer_peer_hybrid_step'


def rmsnorm(x, g):
    xf = x.astype(jnp.float32)
    y = xf * lax.rsqrt(jnp.mean(xf * xf, axis=-1, keepdims=True) + EPS)
    return (y * g.astype(jnp.float32)).astype(x.dtype)


def layernorm(x, g, b):
    xf = x.astype(jnp.float32)
    mu = jnp.mean(xf, axis=-1, keepdims=True)
    xc = xf - mu
    y = xc * lax.rsqrt(jnp.mean(xc * xc, axis=-1, keepdims=True) + EPS)
    return (y * g.astype(jnp.float32) + b.astype(jnp.float32)).astype(x.dtype)


def causal_dwconv(buf, u, w, b):
    width = w.shape[0]
    xp = jnp.concatenate([buf.astype(u.dtype), u], axis=1)
    out = lax.conv_general_dilated(xp, w.astype(u.dtype)[:, None, :], window_strides=(1,),
                                   padding='VALID', dimension_numbers=('NWC', 'WIO', 'NWC'),
                                   feature_group_count=u.shape[-1])
    return out + b.astype(u.dtype), xp[:, xp.shape[1] - (width - 1):]


def segsum(a):
    T = a.shape[-1]
    rep = jnp.broadcast_to(a[..., :, None], a.shape + (T,))
    strict = jnp.tril(jnp.ones((T, T), dtype=bool), -1)
    cs = jnp.cumsum(jnp.where(strict, rep, 0.0), axis=-2)
    return jnp.where(jnp.tril(jnp.ones((T, T), dtype=bool)), cs, -jnp.inf)


def ssd_scan(xh, dt, A, Bm, Cm, h0):
    f32 = jnp.float32
    b, l = xh.shape[0], xh.shape[1]
    cl = min(SSD_CHUNK, l)
    pad = (-l) % cl
    if pad:
        padl = lambda t: jnp.pad(t, [(0, 0), (0, pad)] + [(0, 0)] * (t.ndim - 2))
        xh, dt, Bm, Cm = padl(xh), padl(dt), padl(Bm), padl(Cm)
    nc = (l + pad) // cl
    G, R, P, N = SSD_GROUPS, SSD_HEADS // SSD_GROUPS, SSD_HEAD_DIM, SSD_STATE
    X = (xh.astype(f32) * dt[..., None]).reshape(b, nc, cl, G, R, P)
    dA = (dt * A).reshape(b, nc, cl, G, R).transpose(0, 3, 4, 1, 2)
    Bc = Bm.astype(f32).reshape(b, nc, cl, G, N)
    Cc = Cm.astype(f32).reshape(b, nc, cl, G, N)
    A_cs = jnp.cumsum(dA, axis=-1)
    Lm = jnp.exp(segsum(dA))
    CB = jnp.einsum('bclgn,bcsgn->bcgls', Cc, Bc)
    y_diag = jnp.einsum('bcgls,bgrcls,bcsgrp->bclgrp', CB, Lm, X)
    decay = jnp.exp(A_cs[..., -1:] - A_cs)
    st = jnp.einsum('bcsgn,bgrcs,bcsgrp->bcgrpn', Bc, decay, X)
    st = jnp.concatenate([h0.astype(f32).reshape(b, 1, G, R, P, N), st], axis=1)
    last = jnp.pad(A_cs[..., -1], [(0, 0), (0, 0), (0, 0), (1, 0)])
    chunk_decay = jnp.exp(segsum(last))
    st = jnp.einsum('bgrzc,bcgrpn->bzgrpn', chunk_decay, st)
    y_off = jnp.einsum('bclgn,bcgrpn,bgrcl->bclgrp', Cc, st[:, :-1], jnp.exp(A_cs))
    y = (y_diag + y_off).reshape(b, nc * cl, SSD_HEADS, P)[:, :l]
    return y, st[:, -1].reshape(b, SSD_HEADS, P, N)


def ssd_branch(z, xbc_raw, dt_raw, conv_buf, h0, conv_w, conv_b, dt_bias, a_log, d_skip, norm_g):
    b, l = z.shape[0], z.shape[1]
    xbc, new_buf = causal_dwconv(conv_buf, xbc_raw, conv_w, conv_b)
    xbc = jax.nn.silu(xbc)
    xs = xbc[..., :SSD_INNER].reshape(b, l, SSD_HEADS, SSD_HEAD_DIM)
    Bm = xbc[..., SSD_INNER:SSD_INNER + SSD_GROUPS * SSD_STATE].reshape(b, l, SSD_GROUPS, SSD_STATE)
    Cm = xbc[..., SSD_INNER + SSD_GROUPS * SSD_STATE:].reshape(b, l, SSD_GROUPS, SSD_STATE)
    dt = jax.nn.softplus(dt_raw.astype(jnp.float32) + dt_bias.astype(jnp.float32))
    A = -jnp.exp(a_log.astype(jnp.float32))
    y, hT = ssd_scan(xs, dt, A, Bm, Cm, h0)
    y = y + d_skip.astype(jnp.float32)[:, None] * xs.astype(jnp.float32)
    yz = y.reshape(b, l, SSD_INNER) * jax.nn.silu(z.astype(jnp.float32))
    yg = yz.reshape(b, l, SSD_GROUPS, SSD_INNER // SSD_GROUPS)
    yg = yg * lax.rsqrt(jnp.mean(yg * yg, axis=-1, keepdims=True) + EPS)
    y = yg.reshape(b, l, SSD_INNER) * norm_g.astype(jnp.float32)
    return y.astype(z.dtype), new_buf, hT.astype(h0.dtype)


def conformer_branch(glu_in, conv_buf, dw_w, dw_b, ln_g, ln_b):
    a, g = glu_in[..., :CF_DIM], glu_in[..., CF_DIM:]
    u = a * jax.nn.sigmoid(g)
    c, new_buf = causal_dwconv(conv_buf, u, dw_w, dw_b)
    return jax.nn.silu(layernorm(c, ln_g, ln_b)), new_buf


def peer(x, w_q, sub_keys, u_tab, v_tab):
    b, l, d = x.shape
    T = b * l
    xt = x.reshape(T, d)
    pad = (-T) % PEER_BLOCK
    xt = jnp.pad(xt, [(0, pad), (0, 0)])

    def block(xb):
        q = (xb @ w_q).reshape(xb.shape[0], PEER_HEADS, 2, PEER_HALF)
        s = jnp.einsum('nhjd,hjkd->nhjk', q, sub_keys).astype(jnp.float32)
        sv, si = lax.top_k(s, PEER_TOPK)
        cand = (sv[:, :, 0, :, None] + sv[:, :, 1, None, :]).reshape(xb.shape[0], PEER_HEADS, -1)
        cidx = (si[:, :, 0, :, None] * PEER_KEYS + si[:, :, 1, None, :]).reshape(xb.shape[0], PEER_HEADS, -1)
        top_s, pos = lax.top_k(cand, PEER_TOPK)
        eidx = jnp.take_along_axis(cidx, pos, axis=-1)
        gate = jax.nn.softmax(top_s, axis=-1)
        pre = jnp.einsum('nd,nhkd->nhk', xb, u_tab[eidx]).astype(jnp.float32)
        act = (jax.nn.gelu(pre) * gate).astype(xb.dtype)
        return jnp.einsum('nhk,nhkd->nd', act, v_tab[eidx])

    out = lax.map(block, xt.reshape(-1, PEER_BLOCK, d)).reshape(-1, d)[:T]
    return out.reshape(b, l, d)


def run_trunk(x, p, ssd_conv0, ssd_h0, cf_conv0, W):
    h = x
    new_h, new_sconv, new_cfconv = [], [], []
    for i in range(DEPTH):
        n = rmsnorm(h, W['g_mix'][i])
        proj = n @ W['w_in'][i]
        z = proj[..., :COL_Z]
        xbc = proj[..., COL_Z:COL_XBC]
        dt_raw = proj[..., COL_XBC:COL_DT]
        glu = proj[..., COL_DT:COL_GLU]
        g_a, g_b = proj[..., COL_GLU:COL_GLU + D_MODEL], proj[..., COL_GLU + D_MODEL:]
        y_a, sconv, hT = ssd_branch(z, xbc, dt_raw, ssd_conv0[i], ssd_h0[i], W['ssd_conv_w'][i],
                                    W['ssd_conv_b'][i], W['ssd_dt_bias'][i], W['ssd_a_log'][i],
                                    W['ssd_d'][i], W['ssd_norm_g'][i])
        y_b, cfconv = conformer_branch(glu, cf_conv0[i], W['cf_dw_w'][i], W['cf_dw_b'][i],
                                       W['cf_ln_g'][i], W['cf_ln_b'][i])
        mix = (jax.nn.sigmoid(g_a) * (y_a @ W['w_ssd_out'][i])
               + jax.nn.sigmoid(g_b) * (y_b @ W['w_cf_out'][i]))
        h = h + mix @ W['w_o'][i]
        h = h + peer(rmsnorm(h, W['g_ffn'][i]), W['peer_wq'][i], W['peer_keys'][i],
                     W['peer_u'][i], W['peer_v'][i])
        gate = jax.nn.sigmoid(rmsnorm(h, W['g_ple'][i]) @ W['w_ple_gate'][i])
        h = h + gate * (p[i] @ W['w_ple_proj'][i])
        new_h.append(hT)
        new_sconv.append(sconv)
        new_cfconv.append(cfconv)
    y = rmsnorm(h, W['g_final'])
    return y, jnp.stack(new_h), jnp.stack(new_sconv), jnp.stack(new_cfconv)


def setup_inputs(seed: int = 0) -> dict:
    key = jax.random.key(seed)
    ks = iter(jax.random.split(key, 48))
    f32 = jnp.float32

    def nrm(shape, scale):
        return jax.random.normal(next(ks), shape, f32) * scale

    def uni(shape, lo, hi):
        return jax.random.uniform(next(ks), shape, f32, lo, hi)

    dt0 = jnp.exp(uni((DEPTH, SSD_HEADS), float(np.log(1e-3)), float(np.log(1e-1))))
    return {
        'x_prompt': nrm((BATCH, SEQ, D_MODEL), 1.0),
        'x_sample': nrm((DEC_BATCH, DEC_SEQ, D_MODEL), 1.0),
        'state_ssd': nrm((DEPTH, DEC_BATCH, SSD_HEADS, SSD_HEAD_DIM, SSD_STATE), 0.1),
        'state_ssd_conv': nrm((DEPTH, DEC_BATCH, SSD_CONV - 1, SSD_CONV_DIM), 1.0),
        'state_cf_conv': nrm((DEPTH, DEC_BATCH, CF_CONV - 1, CF_DIM), 0.5),
        'p_prompt': nrm((DEPTH, BATCH, SEQ, PLE_DIM), 1.0),
        'p_sample': nrm((DEPTH, DEC_BATCH, DEC_SEQ, PLE_DIM), 1.0),
        'g_mix': 1.0 + nrm((DEPTH, D_MODEL), 0.02),
        'w_in': nrm((DEPTH, D_MODEL, IN_COLS), D_MODEL ** -0.5),
        'ssd_conv_w': nrm((DEPTH, SSD_CONV, SSD_CONV_DIM), SSD_CONV ** -0.5),
        'ssd_conv_b': nrm((DEPTH, SSD_CONV_DIM), 0.02),
        'ssd_dt_bias': dt0 + jnp.log(-jnp.expm1(-dt0)),
        'ssd_a_log': jnp.log(uni((DEPTH, SSD_HEADS), 1.0, 16.0)),
        'ssd_d': 1.0 + nrm((DEPTH, SSD_HEADS), 0.02),
        'ssd_norm_g': 1.0 + nrm((DEPTH, SSD_INNER), 0.02),
        'w_ssd_out': nrm((DEPTH, SSD_INNER, D_MODEL), SSD_INNER ** -0.5),
        'cf_dw_w': nrm((DEPTH, CF_CONV, CF_DIM), CF_CONV ** -0.5),
        'cf_dw_b': nrm((DEPTH, CF_DIM), 0.02),
        'cf_ln_g': 1.0 + nrm((DEPTH, CF_DIM), 0.02),
        'cf_ln_b': nrm((DEPTH, CF_DIM), 0.02),
        'w_cf_out': nrm((DEPTH, CF_DIM, D_MODEL), CF_DIM ** -0.5),
        'w_o': nrm((DEPTH, D_MODEL, D_MODEL), 0.5 * D_MODEL ** -0.5),
        'g_ffn': 1.0 + nrm((DEPTH, D_MODEL), 0.02),
        'peer_wq': nrm((DEPTH, D_MODEL, PEER_HEADS * PEER_QDIM), D_MODEL ** -0.5),
        'peer_keys': nrm((DEPTH, PEER_HEADS, 2, PEER_KEYS, PEER_HALF), PEER_HALF ** -0.5),
        'peer_u': nrm((DEPTH, PEER_EXPERTS, D_MODEL), D_MODEL ** -0.5),
        'peer_v': nrm((DEPTH, PEER_EXPERTS, D_MODEL), 0.1),
        'g_ple': 1.0 + nrm((DEPTH, D_MODEL), 0.02),
        'w_ple_gate': nrm((DEPTH, D_MODEL, D_MODEL), D_MODEL ** -0.5),
        'w_ple_proj': nrm((DEPTH, PLE_DIM, D_MODEL), 0.5 * PLE_DIM ** -0.5),
        'g_final': 1.0 + nrm((D_MODEL,), 0.02),
    }


def reference(x_prompt, x_sample, state_ssd, state_ssd_conv, state_cf_conv, p_prompt, p_sample,
              g_mix, w_in, ssd_conv_w, ssd_conv_b, ssd_dt_bias, ssd_a_log, ssd_d, ssd_norm_g,
              w_ssd_out, cf_dw_w, cf_dw_b, cf_ln_g, cf_ln_b, w_cf_out, w_o, g_ffn,
              peer_wq, peer_keys, peer_u, peer_v, g_ple, w_ple_gate, w_ple_proj, g_final):
    W = dict(g_mix=g_mix, w_in=w_in, ssd_conv_w=ssd_conv_w, ssd_conv_b=ssd_conv_b,
             ssd_dt_bias=ssd_dt_bias, ssd_a_log=ssd_a_log, ssd_d=ssd_d, ssd_norm_g=ssd_norm_g,
             w_ssd_out=w_ssd_out, cf_dw_w=cf_dw_w, cf_dw_b=cf_dw_b, cf_ln_g=cf_ln_g,
             cf_ln_b=cf_ln_b, w_cf_out=w_cf_out, w_o=w_o, g_ffn=g_ffn, peer_wq=peer_wq,
             peer_keys=peer_keys, peer_u=peer_u, peer_v=peer_v, g_ple=g_ple,
             w_ple_gate=w_ple_gate, w_ple_proj=w_ple_proj, g_final=g_final)
    bp, dty = x_prompt.shape[0], x_prompt.dtype
    h0_p = jnp.zeros((DEPTH, bp, SSD_HEADS, SSD_HEAD_DIM, SSD_STATE), dty)
    sconv0_p = jnp.zeros((DEPTH, bp, SSD_CONV - 1, SSD_CONV_DIM), dty)
    cfconv0_p = jnp.zeros((DEPTH, bp, CF_CONV - 1, CF_DIM), dty)
    y_prompt, ssd_p, sconv_p, cfconv_p = run_trunk(x_prompt, p_prompt, sconv0_p, h0_p, cfconv0_p, W)
    y_sample, ssd_s, sconv_s, cfconv_s = run_trunk(x_sample, p_sample, state_ssd_conv, state_ssd,
                                                   state_cf_conv, W)
    return (y_prompt, y_sample, ssd_p, sconv_p, cfconv_p, ssd_s, sconv_s, cfconv_s)
```

```python
import types
import numpy as np
import concourse.bass as bass
import concourse.mybir as mybir
from concourse.bass_utils import run_bass_kernel_spmd

F32 = mybir.dt.float32
BF16 = mybir.dt.bfloat16
U32 = mybir.dt.uint32
AF = mybir.ActivationFunctionType
ALU = mybir.AluOpType
AX = mybir.AxisListType

T = 2176
TP = 2048
NTL = 17
NTILES = [(0, 512), (512, 512), (1024, 512), (1536, 512), (2048, 128)]
EPS = 1e-6
COL_Z = 1024
COL_XBC = 3072
COL_DT = 3088
COL_GLU = 5136
G_MIX, G_FFN, G_PLE, G_FIN, SCW, SCB, CFW, CFB, NCV = 0, 8, 16, 24, 32, 96, 112, 360, 368
DTB, ALOG, DSK, NG, LNG, LNB, NRV = 0, 16, 32, 48, 1072, 2096, 3120
IDENT, ONES, TRI, TRIS, BLKS, NEGP, NEGS, SEQM, IO16, IO128, NCONST = 0, 128, 256, 384, 512, 640, 768, 896, 912, 928, 1056
ARENA = 67400

ENGS = ("sp", "act", "dve", "pool", "pe")
DBG = {"s6_stop": 99, "s6_tiles": None, "s6_sub": 99}
PUMP = 2


def _freeze(fn):
    if getattr(fn, "__closure__", None) is None:
        return fn
    cells = []
    for c in fn.__closure__:
        try:
            cells.append(types.CellType(c.cell_contents))
        except ValueError:
            cells.append(c)
    return types.FunctionType(fn.__code__, fn.__globals__, fn.__name__, fn.__defaults__, tuple(cells))


class Prog:
    def __init__(self, nc, n_dma_sems=10):
        self.nc = nc
        self.ops = []
        self.n_dma_sems = n_dma_sems
        self.last_writer = {}
        self.readers = {}
        self.last_on_eng = {}
        self.dmas_since_barrier = []

    def op(self, eng, fn, reads=(), writes=(), dma=False, extra_deps=()):
        idx = len(self.ops)
        deps = set(extra_deps)
        for r in reads:
            w = self.last_writer.get(r)
            if w is not None:
                deps.add(w)
        for w_ in writes:
            w = self.last_writer.get(w_)
            if w is not None:
                deps.add(w)
            for rd in self.readers.get(w_, {}).values():
                deps.add(rd)
        deps.discard(idx)
        for r in reads:
            self.readers.setdefault(r, {})[eng if not dma else ("dma", idx)] = idx
        for w_ in writes:
            self.last_writer[w_] = idx
            self.readers[w_] = {}
        self.ops.append(dict(eng=eng, fn=_freeze(fn), deps=deps, dma=dma, needed=False))
        self.last_on_eng[eng] = idx
        if dma:
            self.dmas_since_barrier.append(idx)
        return idx

    def dma(self, q, out, in_, reads=(), writes=()):
        return self.op(q, lambda e: e.dma_start(out=out, in_=in_), reads, writes, dma=True)

    def barrier(self):
        deps = set(self.last_on_eng.values()) | set(self.dmas_since_barrier)
        self.dmas_since_barrier = []
        for e in ENGS:
            self.op(e, lambda eng: None, extra_deps=deps)

    def emit(self):
        nc = self.nc
        ops = self.ops

        def skip(p, o):
            return (not p["dma"]) and (not o["dma"]) and p["eng"] == o["eng"] == "pe"

        for o in ops:
            for d in o["deps"]:
                p = ops[d]
                if not skip(p, o):
                    p["needed"] = True
        esem = {e: nc.alloc_semaphore(f"s_{e}") for e in ENGS}
        dq = ("sp", "act", "pool")
        dsem = {e: [nc.alloc_semaphore(f"d_{e}{i}") for i in range(self.n_dma_sems)] for e in dq}
        ecount = {e: 0 for e in ENGS}
        dcount = {e: [0] * self.n_dma_sems for e in dq}
        drr = {e: 0 for e in dq}
        for o in ops:
            e = o["eng"]
            if o["dma"]:
                k = drr[e]
                drr[e] = (k + 1) % self.n_dma_sems
                o["prev_target"] = dcount[e][k]
                dcount[e][k] += 16
                o["done"] = (dsem[e][k], dcount[e][k], ("d", e, k))
            elif o["needed"]:
                ecount[e] += 1
                o["done"] = (esem[e], ecount[e], ("e", e))
            else:
                o["done"] = None
        per_eng = {e: [] for e in ENGS}
        for i, o in enumerate(ops):
            per_eng[o["eng"]].append(i)

        def run_engine(ename, eobj):
            waited = {}
            pending_inc = []
            for i in per_eng[ename]:
                o = ops[i]
                need = {}
                for d in o["deps"]:
                    p = ops[d]
                    if p["done"] is None or skip(p, o):
                        continue
                    sem, val, key = p["done"]
                    if need.get(key, (None, 0))[1] < val:
                        need[key] = (sem, val)
                if o["dma"]:
                    sem, val, key = o["done"]
                    pt = o["prev_target"]
                    if pt > 0 and need.get(key, (None, 0))[1] < pt:
                        need[key] = (sem, pt)
                for key, (sem, val) in need.items():
                    if waited.get(key, 0) >= val:
                        continue
                    eobj.wait_ge(sem, val)
                    waited[key] = val
                ins = o["fn"](eobj)
                if o["done"] is not None:
                    sem, val, key = o["done"]
                    if ins is None:
                        ins = eobj.nop()
                    ins.then_inc(sem, 16 if o["dma"] else 1)
            if ename == "sp":
                for e2 in dq:
                    for k in range(self.n_dma_sems):
                        if dcount[e2][k] > 0 and waited.get(("d", e2, k), 0) < dcount[e2][k]:
                            eobj.wait_ge(dsem[e2][k], dcount[e2][k])

        with nc.Block() as block:
            @block.sync
            def _(e):
                run_engine("sp", e)

            @block.scalar
            def _(e):
                run_engine("act", e)

            @block.vector
            def _(e):
                run_engine("dve", e)

            @block.gpsimd
            def _(e):
                run_engine("pool", e)

            @block.tensor
            def _(e):
                run_engine("pe", e)


class Arena:
    def __init__(self, ap, cap=None):
        self.ap = ap
        self.off = 0
        self.cap = ARENA if cap is None else cap

    def reset(self, off=0):
        self.off = off

    def alloc(self, shape, dt):
        n = 1
        for s in shape[1:]:
            n *= s
        ne = n * (2 if dt in (F32, U32) else 1)
        ne = (ne + 15) // 16 * 16
        assert self.off + ne <= self.cap, (self.off, ne)
        v = self.ap[:, self.off:self.off + ne]
        self.off += ne
        if dt in (F32, U32):
            v = v.bitcast(dt)[:, 0:n]
        else:
            v = v[:, 0:n]
        if len(shape) == 3:
            v = v.rearrange("p (a b) -> p a b", a=shape[1])
        elif len(shape) == 4:
            v = v.rearrange("p (a b c) -> p a b c", a=shape[1], b=shape[2])
        return v


def bc(ap, shape):
    return ap.to_broadcast(list(shape))


def build_program(n_layers=2, stages=None, dbg=None):
    nc = bass.Bass("TRN2", target_bir_lowering=False)

    def D(name, shape, dt=F32, kind="ExternalInput"):
        return nc.dram_tensor(name, list(shape), dt, kind=kind).ap()

    xT = D("xT", [8, 128, T])
    pT = D("pT", [2, 2, 128, T])
    cf_hist = D("cf_hist", [2, 8, 128, 16, 30])
    ssd_hist = D("ssd_hist", [2, 16, 128, 16, 3])
    ssd_state = D("ssd_state", [2, 16, 128, 1024])
    consts_d = D("consts", [128, NCONST])
    selall_d = D("selall", [128, 2048])
    cvec_d = D("cvec", [128, 2, NCV])
    rvec_d = D("rvec", [2, NRV])
    w_in = D("w_in", [2, 1024, 7184])
    w_so = D("w_ssd_out", [2, 1024, 1024])
    w_cf = D("w_cf_out", [2, 1024, 1024])
    w_o = D("w_o", [2, 1024, 1024])
    wq = D("peer_wq", [2, 1024, 2048])
    keysT = D("keysT", [2, 16, 128, 128])
    uT = D("uT", [2, 128, 128, 1024])
    vK = D("vK", [2, 128, 128, 1024])
    w_pg = D("w_ple_gate", [2, 1024, 1024])
    w_pp = D("w_ple_proj", [2, 256, 1024])
    yT = D("yT", [8, 128, T], kind="ExternalOutput")
    ssd_out = D("ssd_out", [2, 17, 128, 1024], kind="ExternalOutput")
    sconv_out = D("sconv_out", [2, 16, 128, 17, 3], kind="ExternalOutput")
    cfconv_out = D("cfconv_out", [2, 8, 128, 17, 30], kind="ExternalOutput")
    skind = "ExternalOutput" if dbg else "Internal"
    cT = D("cT", [8, 128, T], kind=skind)
    xbcT = D("xbcT", [16, 128, T], kind=skind)
    yaT_d = D("yaT_d", [8, 128, T], BF16, kind=skind)
    ybT_d = D("ybT_d", [8, 128, T], BF16, kind=skind)
    uTb = D("uTb", [2, 128, 128, 1024], BF16, kind="Internal")
    vKb = D("vKb", [2, 128, 128, 1024], BF16, kind="Internal")
    wqb = D("wqb", [2, 16, 128, 1024], BF16, kind="Internal")
    hD = D("hD", [8, 128, T], kind="Internal")
    mixT_d = D("mixT_d", [8, 128, T], BF16, kind="Internal")
    hdbg = D("hdbg", [4, 8, 128, T], kind="ExternalOutput") if dbg else None

    cs = nc.alloc_sbuf_tensor("cs", [128, NCONST], F32).ap()
    cv = nc.alloc_sbuf_tensor("cv", [128, 2, NCV], F32).ap()
    h = nc.alloc_sbuf_tensor("h", [128, 8, T], F32).ap()
    arena_ap = nc.alloc_sbuf_tensor("arena", [128, ARENA], BF16).ap()
    ps = nc.alloc_psum_tensor("ps", [128, 8, 512], F32).ap()
    AR = Arena(arena_ap)
    P = Prog(nc)

    ident = cs[:, IDENT:IDENT + 128]
    ones = cs[:, ONES:ONES + 128]

    def psb(b, n=512):
        return ps[:, b, 0:n]

    def ps2(b):
        return ps[:, b:b + 2, :].rearrange("p a b -> p (a b)")

    def PK(b):
        return ("ps", b)

    P.dma("sp", cs, consts_d, writes=["cs"])
    P.dma("sp", cv, cvec_d, writes=["cv"])
    for k in range(8):
        P.dma("sp" if k % 2 == 0 else "act", h[:, k, :], xT[k], writes=["h"])

    class Rot:
        def __init__(self, name, shape, dt, n, arena=None):
            self.bufs = [(arena or AR).alloc(shape, dt) for _ in range(n)]
            self.name = name
            self.i = 0

        def next(self):
            i = self.i
            self.i = (i + 1) % len(self.bufs)
            return self.bufs[i], (self.name, i)

    def load_w(rot, w2d, c0, ncols, nk=8):
        buf, key = rot.next()
        src = w2d.rearrange("(k p) c -> p k c", p=128)[:, :, c0:c0 + ncols]
        P.dma("pool", buf[:, 0:nk, 0:ncols], src, writes=[key])
        if pump_state["auto"]:
            pump(pump_state["auto"], layer=0)
        return buf, key

    precast_pending = [(l_, k1_, w_) for l_ in range(n_layers) for k1_ in range(128) for w_ in (0, 1)]
    pump_state = {"auto": 0}

    def pump(n, layer=None):
        cnt = 0
        i = 0
        while cnt < n and i < len(precast_pending):
            l_, k1_, w_ = precast_pending[i]
            if layer is not None and l_ != layer:
                i += 1
                continue
            precast_pending.pop(i)
            if w_ == 0:
                P.dma("pool", uTb[l_, k1_], uT[l_, k1_], writes=[("uTb", l_, k1_)])
            else:
                P.dma("pool", vKb[l_, k1_], vK[l_, k1_], writes=[("vKb", l_, k1_)])
            cnt += 1

    def norm_cols(l, c0, n, gcol, out_fn, out_keys, hsq, rs, psbank, tag, src=None, srckey="h"):
        s3 = h[:, :, c0:c0 + n] if src is None else src[:, :, 0:n]
        P.op("act", lambda e: e.activation(out=hsq[:, :, 0:n], in_=s3, func=AF.Square),
             reads=[srckey], writes=["hsq"])
        for k in range(8):
            P.op("pe", lambda e, k=k: e.matmul(psb(psbank, n), lhsT=ones, rhs=hsq[:, k, 0:n], start=(k == 0), stop=(k == 7)),
                 reads=["hsq", "cs"], writes=[PK(psbank)])
        P.op("act", lambda e: e.activation(out=rs[:, 0:n], in_=psb(psbank, n), func=AF.Sqrt, bias=EPS, scale=1.0 / 1024),
             reads=[PK(psbank)], writes=["rs"])
        P.op("dve", lambda e: e.reciprocal(out=rs[:, 0:n], in_=rs[:, 0:n]), reads=["rs"], writes=["rs"])
        for k in range(8):
            o_ = out_fn(k)
            i_ = s3[:, k, :]
            P.op("dve", lambda e, k=k, o_=o_, i_=i_: e.scalar_tensor_tensor(out=o_, in0=i_, scalar=cv[:, l, gcol + k:gcol + k + 1],
                                                                in1=rs[:, 0:n], op0=ALU.mult, op1=ALU.mult),
                 reads=[srckey, "rs", "cv"], writes=out_keys)

    def proj(psbank, wbuf, wkey, rhs_fn, n, rkeys, nk=8):
        for k in range(nk):
            r_ = rhs_fn(k)
            P.op("pe", lambda e, k=k, r_=r_: e.matmul(psb(psbank, n), lhsT=wbuf[:, k, :], rhs=r_, start=(k == 0), stop=(k == nk - 1)),
                 reads=[wkey] + list(rkeys), writes=[PK(psbank)])

    def conv(l, src_p, src_s, acc, K, wcol, bcol, skeys, akey, acc2=None, a2key=None, acc3=None):
        def views(a):
            return a[:, 0:TP], a[:, TP:T].rearrange("p (s t) -> p s t", t=8)
        ksplit = K if acc2 is None else 23
        tp_, ts_ = views(acc3) if acc3 is not None else (None, None)
        for (eng, a_, ak_, k0, k1_) in (("dve", acc, akey, 0, ksplit), ("pool", acc2, a2key, ksplit, K)):
            if k0 >= k1_:
                continue
            ap_, as_ = views(a_)
            for k in range(k0, k1_):
                w_ = cv[:, l, wcol + k:wcol + k + 1]
                for (a, s_) in ((ap_, src_p[:, k:k + TP]), (as_, src_s[:, :, k:k + 8])):
                    if k == 0:
                        P.op(eng, lambda e, a=a, s_=s_, w_=w_: e.tensor_scalar(out=a, in0=s_, scalar1=w_, scalar2=cv[:, l, bcol:bcol + 1],
                                                                                op0=ALU.mult, op1=ALU.add),
                             reads=list(skeys) + ["cv"], writes=[ak_])
                    elif k == k0:
                        P.op(eng, lambda e, a=a, s_=s_, w_=w_: e.tensor_scalar(out=a, in0=s_, scalar1=w_, scalar2=None, op0=ALU.mult),
                             reads=list(skeys) + ["cv"], writes=[ak_])
                    elif eng == "dve":
                        P.op(eng, lambda e, a=a, s_=s_, w_=w_: e.scalar_tensor_tensor(out=a, in0=s_, scalar=w_, in1=a, op0=ALU.mult, op1=ALU.add),
                             reads=list(skeys) + ["cv", ak_], writes=[ak_])
                    else:
                        t_ = tp_ if a is ap_ else ts_
                        P.op(eng, lambda e, t_=t_, s_=s_, w_=w_: e.tensor_scalar(out=t_, in0=s_, scalar1=w_, scalar2=None, op0=ALU.mult),
                             reads=list(skeys) + ["cv"], writes=["acc3"])
                        P.op(eng, lambda e, a=a, t_=t_: e.tensor_tensor(out=a, in0=a, in1=t_, op=ALU.add), reads=["acc3", ak_], writes=[ak_])
        if acc2 is not None:
            P.op("dve", lambda e: e.tensor_tensor(out=acc, in0=acc, in1=acc2, op=ALU.add), reads=[akey, a2key], writes=[akey])

    def dump_h(i):
        if dbg:
            for k in range(8):
                P.dma("sp", hdbg[i, k], h[:, k, :], reads=["h"])

    want = (lambda s: True) if stages is None else (lambda s: s in stages)

    for l in range(n_layers):
        pump_state["auto"] = 0
        P.barrier()
        AR.reset()
        hn = AR.alloc([128, 8, T], BF16)
        base_off = AR.off
        hsq = AR.alloc([128, 8, 512], F32)
        rs = AR.alloc([128, 512], F32)
        for ti, (c0, n) in enumerate(NTILES):
            norm_cols(l, c0, n, G_MIX, lambda k, c0=c0, n=n: hn[:, k, c0:c0 + n], ["hn"], hsq, rs, ti % 2, "mix")
        P.barrier()

        if want("S1"):
            AR.reset(base_off)
            pump_state["auto"] = 8 if l == 0 else 0
            wr = Rot("w", [128, 8, 128], BF16, 4)
            ubuf_r = Rot("ubuf", [128, 30 + TP], F32, 2)
            ubufs_r = Rot("ubuf_s", [128, 16, 38], F32, 2)
            acc_r1 = Rot("acc1", [128, T], F32, 2)
            sg_r = Rot("sg", [128, 512], F32, 2)
            for ub_ in ubuf_r.bufs:
                P.op("dve", lambda e: e.memset(ub_[:, 0:30], 0.0), writes=[("ubuf", 0), ("ubuf", 1)])
            for j in range(8):
                wa, wak = load_w(wr, w_in[l], COL_DT + j * 128, 128)
                wg, wgk = load_w(wr, w_in[l], COL_DT + 1024 + j * 128, 128)
                ubuf, ubk_ = ubuf_r.next()
                ubuf_s, ubsk_ = ubufs_r.next()
                acc, acck_ = acc_r1.next()
                P.dma("sp", ubuf_s[:, :, 0:30], cf_hist[l, j], writes=[ubsk_])
                for ti, (c0, n) in enumerate(NTILES):
                    ba, bb = 2 * (ti % 2), 2 * (ti % 2) + 1
                    proj(ba, wa, wak, lambda k, c0=c0, n=n: hn[:, k, c0:c0 + n], n, ["hn"])
                    proj(bb, wg, wgk, lambda k, c0=c0, n=n: hn[:, k, c0:c0 + n], n, ["hn"])
                    sg, sgk = sg_r.next()
                    P.op("act", lambda e: e.activation(out=sg[:, 0:n], in_=psb(bb, n), func=AF.Sigmoid),
                         reads=[PK(bb)], writes=[sgk])
                    if c0 < TP:
                        P.op("dve", lambda e: e.tensor_tensor(out=ubuf[:, 30 + c0:30 + c0 + n], in0=psb(ba, n), in1=sg[:, 0:n], op=ALU.mult),
                             reads=[PK(ba), sgk], writes=[ubk_])
                    else:
                        P.op("dve", lambda e: e.tensor_tensor(out=ubuf_s[:, :, 30:38], in0=psb(ba, 128).rearrange("p (s t) -> p s t", t=8),
                                                              in1=sg[:, 0:128].rearrange("p (s t) -> p s t", t=8), op=ALU.mult),
                             reads=[PK(ba), sgk], writes=[ubsk_])
                P.dma("sp", cfconv_out[l, j, :, 0, :], ubuf[:, TP:TP + 30], reads=[ubk_])
                P.dma("sp", cfconv_out[l, j, :, 1:17, :], ubuf_s[:, :, 8:38], reads=[ubsk_])
                conv(l, ubuf, ubuf_s, acc, 31, CFW + j * 31, CFB + j, [ubk_, ubsk_], acck_)
                P.dma("sp", cT[j], acc, reads=[acck_])
            P.barrier()

        pump_state["auto"] = 0
        if want("S2"):
            AR.reset(base_off)
            rv = AR.alloc([128, 2048], F32)
            P.dma("sp", rv, rvec_d[l, LNG:LNG + 2048].partition_broadcast(128), writes=["rv"])
            cin_r = Rot("cin", [128, 8, 128], F32, 2)
            lnb = AR.alloc([128, 1024], F32)
            ybtm = AR.alloc([128, 1024], F32)
            ybst_r = Rot("ybst", [128, 8, 128], BF16, 2)
            st = AR.alloc([128, 8], F32)
            for t in range(NTL):
                cin, cink = cin_r.next()
                P.dma("sp", cin, cT.rearrange("k p t -> p k t")[:, :, t * 128:(t + 1) * 128], writes=[cink])
                b0 = 4 * (t % 2)
                for k in range(8):
                    P.op("pe", lambda e, k=k, b0=b0: e.transpose(out=ps[:, b0 + k // 4, (k % 4) * 128:(k % 4) * 128 + 128], in_=cin[:, k, :], identity=ident),
                         reads=[cink, "cs"], writes=[PK(b0), PK(b0 + 1)])
                ptm = ps2(b0)
                P.op("dve", lambda e: e.memset(st, 0.0), writes=["st"])
                P.op("dve", lambda e, ptm=ptm: e.reduce_sum(out=st[:, 0:1], in_=ptm, axis=AX.X), reads=[PK(b0), PK(b0 + 1), "st"], writes=["st"])
                P.op("act", lambda e, ptm=ptm: e.activation(out=lnb, in_=ptm, func=AF.Square, accum_out=st[:, 1:2]),
                     reads=[PK(b0), PK(b0 + 1), "st"], writes=["lnb", "st"])
                P.op("dve", lambda e: e.tensor_scalar(out=st[:, 2:3], in0=st[:, 0:1], scalar1=1.0 / 1024, scalar2=None, op0=ALU.mult), reads=["st"], writes=["st"])
                P.op("dve", lambda e: e.tensor_tensor(out=st[:, 3:4], in0=st[:, 2:3], in1=st[:, 2:3], op=ALU.mult), reads=["st"], writes=["st"])
                P.op("dve", lambda e: e.scalar_tensor_tensor(out=st[:, 4:5], in0=st[:, 1:2], scalar=1.0 / 1024, in1=st[:, 3:4], op0=ALU.mult, op1=ALU.subtract),
                     reads=["st"], writes=["st"])
                P.op("act", lambda e: e.activation(out=st[:, 5:6], in_=st[:, 4:5], func=AF.Sqrt, bias=EPS, scale=1.0), reads=["st"], writes=["st"])
                P.op("dve", lambda e: e.reciprocal(out=st[:, 6:7], in_=st[:, 5:6]), reads=["st"], writes=["st"])
                P.op("dve", lambda e, ptm=ptm: e.tensor_scalar(out=lnb, in0=ptm, scalar1=st[:, 2:3], scalar2=st[:, 6:7], op0=ALU.subtract, op1=ALU.mult),
                     reads=[PK(b0), PK(b0 + 1), "st", "lnb"], writes=["lnb"])
                P.op("dve", lambda e: e.tensor_tensor(out=lnb, in0=lnb, in1=rv[:, 0:1024], op=ALU.mult), reads=["lnb", "rv"], writes=["lnb"])
                P.op("dve", lambda e: e.tensor_tensor(out=lnb, in0=lnb, in1=rv[:, 1024:2048], op=ALU.add), reads=["lnb", "rv"], writes=["lnb"])
                P.op("act", lambda e: e.activation(out=ybtm, in_=lnb, func=AF.Silu), reads=["lnb"], writes=["ybtm"])
                b2 = b0 + 2
                for k in range(8):
                    P.op("pe", lambda e, k=k, b2=b2: e.transpose(out=ps[:, b2 + k // 4, (k % 4) * 128:(k % 4) * 128 + 128], in_=ybtm[:, k * 128:(k + 1) * 128], identity=ident),
                         reads=["ybtm", "cs"], writes=[PK(b2), PK(b2 + 1)])
                ybst, ybk = ybst_r.next()
                P.op("act", lambda e, b2=b2, ybst=ybst: e.activation(out=ybst, in_=ps2(b2).rearrange("p (k t) -> p k t", k=8), func=AF.Identity),
                     reads=[PK(b2), PK(b2 + 1)], writes=[ybk])
                P.dma("sp", ybT_d.rearrange("k p t -> p k t")[:, :, t * 128:(t + 1) * 128], ybst, reads=[ybk], writes=["ybT_d"])
            P.barrier()

        if want("S3"):
            AR.reset(base_off)
            wr = Rot("w", [128, 8, 128], BF16, 3)
            xbuf_r = Rot("xbuf", [128, 3 + TP], F32, 2)
            xbufs_r = Rot("xbuf_s", [128, 16, 11], F32, 2)
            acc_r = Rot("acc", [128, T], F32, 2)
            for xb_ in xbuf_r.bufs:
                P.op("dve", lambda e: e.memset(xb_[:, 0:3], 0.0), writes=[("xbuf", 0), ("xbuf", 1)])
            for j in range(16):
                w_, wk = load_w(wr, w_in[l], COL_Z + j * 128, 128)
                xbuf, xbk_ = xbuf_r.next()
                xbuf_s, xbsk_ = xbufs_r.next()
                P.dma("sp", xbuf_s[:, :, 0:3], ssd_hist[l, j], writes=[xbsk_])
                for ti, (c0, n) in enumerate(NTILES):
                    ba = ti % 4
                    proj(ba, w_, wk, lambda k, c0=c0, n=n: hn[:, k, c0:c0 + n], n, ["hn"])
                    if c0 < TP:
                        P.op("act", lambda e: e.activation(out=xbuf[:, 3 + c0:3 + c0 + n], in_=psb(ba, n), func=AF.Identity),
                             reads=[PK(ba)], writes=[xbk_])
                    else:
                        P.op("act", lambda e: e.activation(out=xbuf_s[:, :, 3:11], in_=psb(ba, 128).rearrange("p (s t) -> p s t", t=8), func=AF.Identity),
                             reads=[PK(ba)], writes=[xbsk_])
                P.dma("sp", sconv_out[l, j, :, 0, :], xbuf[:, TP:TP + 3], reads=[xbk_])
                P.dma("sp", sconv_out[l, j, :, 1:17, :], xbuf_s[:, :, 8:11], reads=[xbsk_])
                acc, acck = acc_r.next()
                conv(l, xbuf, xbuf_s, acc, 4, SCW + j * 4, SCB + j, [xbk_, xbsk_], acck)
                P.op("act", lambda e: e.activation(out=acc, in_=acc, func=AF.Silu), reads=[acck], writes=[acck])
                P.dma("sp", xbcT[j], acc, reads=[acck], writes=["xbcT"])
            P.barrier()

        if want("S6"):
            AR.reset(base_off)
            rvs = AR.alloc([128, 48], F32)
            rvn = AR.alloc([128, 1024], F32)
            P.dma("sp", rvs, rvec_d[l, 0:48].partition_broadcast(128), writes=["rvs"])
            P.dma("sp", rvn, rvec_d[l, NG:NG + 1024].partition_broadcast(128), writes=["rvn"])
            wdt = AR.alloc([128, 8, 16], BF16)
            P.dma("pool", wdt, w_in[l].rearrange("(k p) c -> p k c", p=128)[:, :, COL_XBC:COL_XBC + 16], writes=["wdt"])
            wz = AR.alloc([128, 8, 1024], BF16)
            for k in range(8):
                P.dma("pool", wz[:, k, :], w_in[l][k * 128:(k + 1) * 128, 0:1024], writes=["wz"])
            dt_tok = AR.alloc([128, NTL, 16], F32)
            dA_tok = AR.alloc([128, NTL, 16], F32)
            Abc = AR.alloc([128, 16], F32)
            tmp16 = AR.alloc([128, 16], F32)
            P.op("act", lambda e: e.activation(out=Abc, in_=rvs[:, ALOG:ALOG + 16], func=AF.Exp), reads=["rvs"], writes=["Abc"])
            P.op("dve", lambda e: e.tensor_scalar(out=Abc, in0=Abc, scalar1=-1.0, scalar2=None, op0=ALU.mult), reads=["Abc"], writes=["Abc"])
            for t in range(NTL):
                bk = t % 2
                for k in range(8):
                    P.op("pe", lambda e, k=k, t=t, bk=bk: e.matmul(ps[:, bk, 0:16], lhsT=hn[:, k, t * 128:(t + 1) * 128], rhs=wdt[:, k, :], start=(k == 0), stop=(k == 7)),
                         reads=["hn", "wdt"], writes=[PK(bk)])
                P.op("dve", lambda e, bk=bk: e.tensor_tensor(out=tmp16, in0=ps[:, bk, 0:16], in1=rvs[:, DTB:DTB + 16], op=ALU.add), reads=[PK(bk), "rvs"], writes=["tmp16"])
                P.op("act", lambda e: e.activation(out=tmp16, in_=tmp16, func=AF.Exp), reads=["tmp16"], writes=["tmp16"])
                P.op("act", lambda e, t=t: e.activation(out=dt_tok[:, t, :], in_=tmp16, func=AF.Ln, bias=1.0, scale=1.0), reads=["tmp16"], writes=["dt_tok"])
                P.op("dve", lambda e, t=t: e.tensor_tensor(out=dA_tok[:, t, :], in0=dt_tok[:, t, :], in1=Abc, op=ALU.mult), reads=["dt_tok", "Abc"], writes=["dA_tok"])
            xin_r = Rot("xin", [128, 16, 128], F32, 1)
            xs_sb = AR.alloc([128, 1024], F32)
            X_bf = AR.alloc([128, 1024], BF16)
            Xd_bf = AR.alloc([128, 1024], BF16)
            Btm_bf = AR.alloc([128, 512], BF16)
            BC_bf = AR.alloc([128, 8, 128], BF16)
            R = AR.alloc([128, 16, 128], F32)
            Lm = R
            M_bf = AR.alloc([128, 16, 128], BF16)
            cb_sb = AR.alloc([128, 512], F32)
            t1 = AR.alloc([128, 1024], F32)
            sz = AR.alloc([128, 1024], F32)
            ST = AR.alloc([128, 1024], F32)
            ST_bf = AR.alloc([128, 1024], BF16)
            sm = AR.alloc([128, 128], F32)
            yast_r = Rot("yast", [128, 8, 128], BF16, 1)
            Acol, tot, dec, expA, cdbc, ss4, rg = sm[:, 0:16], sm[:, 16:32], sm[:, 32:48], sm[:, 48:64], sm[:, 64:80], sm[:, 80:84], sm[:, 84:88]
            Cm_r = Rot("Cm", [128, 16, 128], BF16, 2)
            Ysel = AR.alloc([128, 16, 16], F32)
            S0f_r = Rot("S0f", [128, 1024], F32, 2)
            S0b_r = Rot("S0b", [128, 256], BF16, 4)
            Bm_r = Rot("Bm", [128, 512], BF16, 2)
            cd_all = AR.alloc([128, 16, 16], F32)
            for cm_ in Cm_r.bufs:
                P.op("dve", lambda e, cm_=cm_: e.memset(cm_, 0.0), writes=[("Cm", 0), ("Cm", 1)])
            P.op("dve", lambda e: e.memset(ST, 0.0), writes=["ST"])
            P.op("dve", lambda e: e.memset(ST_bf, 0.0), writes=["ST_bf"])

            def v16(ap):
                return ap.rearrange("p (h q) -> p h q", h=16)

            for t in (DBG["s6_tiles"] if DBG["s6_tiles"] is not None else range(NTL)):
                is_s = (t == NTL - 1)
                if l == 0:
                    pump(8, layer=0)
                triM = cs[:, TRIS:TRIS + 128] if is_s else cs[:, TRI:TRI + 128]
                negM = cs[:, NEGS:NEGS + 128] if is_s else cs[:, NEGP:NEGP + 128]
                blkM = cs[:, BLKS:BLKS + 128] if is_s else ones
                tsl = slice(t * 128, (t + 1) * 128)
                xin, xink = xin_r.next()
                for q4 in range(4):
                    P.dma("sp", xin[:, 4 * q4:4 * q4 + 4, :], xbcT.rearrange("k p t -> p k t")[:, 4 * q4:4 * q4 + 4, tsl], reads=["xbcT"], writes=[xink])
                for k in range(8):
                    P.op("pe", lambda e, k=k, xin=xin: e.transpose(out=ps[:, k // 4, (k % 4) * 128:(k % 4) * 128 + 128], in_=xin[:, k, :], identity=ident),
                         reads=[xink, "cs"], writes=[PK(0), PK(1)])
                for g in range(4):
                    P.op("pe", lambda e, g=g, xin=xin: e.transpose(out=ps[:, 2, g * 128:(g + 1) * 128], in_=xin[:, 8 + g, :], identity=ident),
                         reads=[xink, "cs"], writes=[PK(2)])
                if DBG["s6_sub"] <= 1:
                    continue
                P.op("act", lambda e: e.activation(out=xs_sb, in_=ps2(0), func=AF.Identity), reads=[PK(0), PK(1)], writes=["xs_sb"])
                if DBG["s6_sub"] <= 2:
                    continue
                P.op("dve", lambda e, t=t: e.tensor_tensor(out=v16(X_bf), in0=v16(xs_sb), in1=bc(dt_tok[:, t, :].unsqueeze(2), [128, 16, 64]), op=ALU.mult),
                     reads=["xs_sb", "dt_tok"], writes=["X_bf"])
                if DBG["s6_sub"] <= 3:
                    continue
                P.op("act", lambda e: e.activation(out=Btm_bf, in_=psb(2), func=AF.Identity), reads=[PK(2)], writes=["Btm_bf"])
                if DBG["s6_sub"] <= 4:
                    continue
                P.op("dve", lambda e, xin=xin: e.tensor_copy(out=BC_bf, in_=xin[:, 8:16, :]), reads=[xink], writes=["BC_bf"])
                if DBG["s6_stop"] <= 1:
                    continue
                P.op("pe", lambda e, t=t, triM=triM: e.matmul(ps[:, 3, 0:16], lhsT=triM, rhs=dA_tok[:, t, :], start=True, stop=True), reads=["dA_tok", "cs"], writes=[PK(3)])
                P.op("pe", lambda e, t=t, blkM=blkM: e.matmul(ps[:, 3, 16:32], lhsT=blkM, rhs=dA_tok[:, t, :], start=True, stop=True), reads=["dA_tok", "cs"], writes=[PK(3)])
                P.op("dve", lambda e: e.tensor_copy(out=sm[:, 0:32], in_=ps[:, 3, 0:32]), reads=[PK(3)], writes=["sm_a"])
                P.op("dve", lambda e, t=t, triM=triM: e.tensor_tensor(out=R, in0=bc(triM.unsqueeze(1), [128, 16, 128]), in1=bc(dA_tok[:, t, :].unsqueeze(2), [128, 16, 128]), op=ALU.mult),
                     reads=["cs", "dA_tok"], writes=["R"])
                for i in range(4):
                    P.op("pe", lambda e, i=i: e.matmul(psb(4 + i), lhsT=ones, rhs=R[:, 4 * i:4 * i + 4, :].rearrange("p a b -> p (a b)"), start=True, stop=True),
                         reads=["R", "cs"], writes=[PK(4 + i)])
                for i in range(4):
                    P.op("dve", lambda e, i=i: e.tensor_tensor(out=Lm[:, 4 * i:4 * i + 4, :], in0=ps[:, 4 + i, :].rearrange("p (a b) -> p a b", a=4),
                                                               in1=bc(Acol[:, 4 * i:4 * i + 4].unsqueeze(2), [128, 4, 128]), op=ALU.subtract),
                         reads=[PK(4 + i), "sm_a", "R"], writes=["R", "Lm"])
                P.op("dve", lambda e, negM=negM: e.tensor_tensor(out=Lm, in0=Lm, in1=bc(negM.unsqueeze(1), [128, 16, 128]), op=ALU.add), reads=["Lm", "R", "cs"], writes=["Lm", "R"])
                P.op("act", lambda e: e.activation(out=Lm, in_=Lm, func=AF.Exp), reads=["Lm", "R"], writes=["Lm", "R"])
                if DBG["s6_stop"] <= 2:
                    continue
                for g in range(4):
                    P.op("pe", lambda e, g=g: e.matmul(ps[:, 2, g * 128:(g + 1) * 128], lhsT=BC_bf[:, g, :], rhs=BC_bf[:, 4 + g, :], start=True, stop=True),
                         reads=["BC_bf", "Btm_bf"], writes=[PK(2)])
                P.op("act", lambda e: e.activation(out=cb_sb, in_=psb(2), func=AF.Identity), reads=[PK(2)], writes=["cb_sb"])
                P.op("dve", lambda e: e.tensor_tensor(out=M_bf.rearrange("p (g r) l -> p g r l", g=4), in0=Lm.rearrange("p (g r) l -> p g r l", g=4),
                                                      in1=bc(cb_sb.rearrange("p (g l) -> p g l", g=4).unsqueeze(2), [128, 4, 4, 128]), op=ALU.mult),
                     reads=["Lm", "R", "cb_sb"], writes=["M_bf"])
                if DBG["s6_stop"] <= 3:
                    continue
                for hh in range(16):
                    P.op("pe", lambda e, hh=hh: e.matmul(ps2(0)[:, hh * 64:(hh + 1) * 64], lhsT=M_bf[:, hh, :], rhs=X_bf[:, hh * 64:(hh + 1) * 64], start=True, stop=True),
                         reads=["M_bf", "X_bf", "xs_sb"], writes=[PK(0), PK(1)])
                if DBG["s6_stop"] <= 4:
                    continue
                if not is_s:
                    for g in range(4):
                        P.op("pe", lambda e, g=g: e.matmul(ps2(4)[:, g * 256:(g + 1) * 256], lhsT=BC_bf[:, 4 + g, :], rhs=ST_bf[:, g * 256:(g + 1) * 256], start=True, stop=True),
                             reads=["BC_bf", "ST_bf", "Lm"], writes=[PK(4), PK(5)])
                else:
                    for g in range(4):
                        Cmg, Cmk = Cm_r.next()
                        for sq in range(16):
                            P.op("dve", lambda e, g=g, sq=sq, Cmg=Cmg: e.tensor_copy(out=Cmg[:, sq, 8 * sq:8 * sq + 8], in_=BC_bf[:, 4 + g, 8 * sq:8 * sq + 8]),
                                 reads=["BC_bf", Cmk], writes=[Cmk])
                        for sq in range(16):
                            S0b, S0bk = S0b_r.next()
                            P.dma("pool", S0b, ssd_state[l, sq][:, g * 256:(g + 1) * 256], writes=[S0bk])
                            P.op("pe", lambda e, g=g, sq=sq, Cmg=Cmg, S0b=S0b: e.matmul(ps2(4)[:, g * 256:(g + 1) * 256], lhsT=Cmg[:, sq, :], rhs=S0b,
                                                                                        start=(sq == 0), stop=(sq == 15)),
                                 reads=[Cmk, S0bk, "Lm"], writes=[PK(4), PK(5)])
                if DBG["s6_stop"] <= 5:
                    continue
                P.op("act", lambda e: e.activation(out=expA, in_=Acol, func=AF.Exp), reads=["sm_a"], writes=["sm_e"])
                for i in range(2):
                    P.op("dve", lambda e, i=i: e.tensor_tensor(out=v16(t1)[:, 8 * i:8 * i + 8, :], in0=ps[:, 4 + i, :].rearrange("p (a b) -> p a b", a=8),
                                                               in1=bc(expA[:, 8 * i:8 * i + 8].unsqueeze(2), [128, 8, 64]), op=ALU.mult),
                         reads=[PK(4 + i), "sm_e"], writes=["t1"])
                P.op("dve", lambda e: e.tensor_tensor(out=t1, in0=t1, in1=ps2(0), op=ALU.add), reads=["t1", PK(0), PK(1)], writes=["t1"])
                P.op("dve", lambda e: e.tensor_tensor(out=v16(sz), in0=v16(xs_sb), in1=bc(rvs[:, DSK:DSK + 16].unsqueeze(2), [128, 16, 64]), op=ALU.mult),
                     reads=["xs_sb", "rvs"], writes=["sz"])
                P.op("dve", lambda e: e.tensor_tensor(out=t1, in0=t1, in1=sz, op=ALU.add), reads=["t1", "sz"], writes=["t1"])
                if DBG["s6_stop"] <= 6:
                    continue
                for hf in range(2):
                    for k in range(8):
                        P.op("pe", lambda e, hf=hf, k=k, tsl=tsl: e.matmul(psb(6 + hf), lhsT=hn[:, k, tsl], rhs=wz[:, k, hf * 512:(hf + 1) * 512], start=(k == 0), stop=(k == 7)),
                             reads=["hn", "wz", "Lm"], writes=[PK(6 + hf)])
                P.op("act", lambda e: e.activation(out=sz, in_=ps2(6), func=AF.Silu), reads=[PK(6), PK(7), "t1"], writes=["sz"])
                P.op("dve", lambda e: e.tensor_tensor(out=t1, in0=t1, in1=sz, op=ALU.mult), reads=["t1", "sz"], writes=["t1"])
                if DBG["s6_stop"] <= 7:
                    continue
                P.op("dve", lambda e: e.memset(ss4, 0.0), writes=["ss4"])
                for g in range(4):
                    P.op("act", lambda e, g=g: e.activation(out=sz[:, g * 256:(g + 1) * 256], in_=t1[:, g * 256:(g + 1) * 256], func=AF.Square, accum_out=ss4[:, g:g + 1]),
                         reads=["t1", "ss4"], writes=["sz", "ss4"])
                P.op("act", lambda e: e.activation(out=rg, in_=ss4, func=AF.Sqrt, bias=EPS, scale=1.0 / 256), reads=["ss4"], writes=["rg"])
                P.op("dve", lambda e: e.reciprocal(out=rg, in_=rg), reads=["rg"], writes=["rg"])
                P.op("dve", lambda e: e.tensor_tensor(out=t1.rearrange("p (g c) -> p g c", g=4), in0=t1.rearrange("p (g c) -> p g c", g=4),
                                                      in1=bc(rg.unsqueeze(2), [128, 4, 256]), op=ALU.mult), reads=["t1", "rg"], writes=["t1"])
                P.op("dve", lambda e: e.tensor_tensor(out=t1, in0=t1, in1=rvn, op=ALU.mult), reads=["t1", "rvn"], writes=["t1"])
                for k in range(8):
                    P.op("pe", lambda e, k=k: e.transpose(out=ps[:, k // 4, (k % 4) * 128:(k % 4) * 128 + 128], in_=t1[:, k * 128:(k + 1) * 128], identity=ident),
                         reads=["t1", "cs"], writes=[PK(0), PK(1)])
                if DBG["s6_stop"] <= 8:
                    continue
                yast, yak = yast_r.next()
                P.op("act", lambda e, yast=yast: e.activation(out=yast, in_=ps2(0).rearrange("p (k t) -> p k t", k=8), func=AF.Identity),
                     reads=[PK(0), PK(1)], writes=[yak])
                P.dma("sp", yaT_d.rearrange("k p t -> p k t")[:, :, tsl], yast, reads=[yak], writes=["yaT_d"])
                if DBG["s6_stop"] <= 9:
                    continue
                P.op("dve", lambda e: e.tensor_tensor(out=dec, in0=tot, in1=Acol, op=ALU.subtract), reads=["sm_a"], writes=["sm_d"])
                P.op("act", lambda e: e.activation(out=dec, in_=dec, func=AF.Exp), reads=["sm_d"], writes=["sm_d"])
                P.op("dve", lambda e: e.tensor_tensor(out=v16(Xd_bf), in0=v16(X_bf), in1=bc(dec.unsqueeze(2), [128, 16, 64]), op=ALU.mult),
                     reads=["X_bf", "sm_d"], writes=["Xd_bf"])
                if not is_s:
                    P.op("act", lambda e: e.activation(out=cdbc, in_=tot, func=AF.Exp), reads=["sm_a"], writes=["sm_c"])
                    for g in range(4):
                        P.op("pe", lambda e, g=g: e.matmul(ps2(4)[:, g * 256:(g + 1) * 256], lhsT=Btm_bf[:, g * 128:(g + 1) * 128], rhs=Xd_bf[:, g * 256:(g + 1) * 256], start=True, stop=True),
                             reads=["Btm_bf", "Xd_bf", "t1"], writes=[PK(4), PK(5)])
                    P.op("dve", lambda e: e.tensor_tensor(out=v16(ST), in0=v16(ST), in1=bc(cdbc.unsqueeze(2), [128, 16, 64]), op=ALU.mult), reads=["ST", "sm_c"], writes=["ST"])
                    P.op("dve", lambda e: e.tensor_tensor(out=ST, in0=ST, in1=ps2(4), op=ALU.add), reads=["ST", PK(4), PK(5)], writes=["ST"])
                    P.op("act", lambda e: e.activation(out=ST_bf, in_=ST, func=AF.Identity), reads=["ST"], writes=["ST_bf"])
                    if t == NTL - 2:
                        P.dma("sp", ssd_out[l, 0], ST, reads=["ST"])
                else:
                    P.op("dve", lambda e, t=t: e.tensor_tensor(out=Ysel, in0=bc(dA_tok[:, t, :].unsqueeze(1), [128, 16, 16]), in1=bc(cs[:, SEQM:SEQM + 16].unsqueeze(2), [128, 16, 16]), op=ALU.mult),
                         reads=["dA_tok", "cs"], writes=["Ysel"])
                    P.op("pe", lambda e: e.matmul(ps[:, 3, 0:256], lhsT=ones, rhs=Ysel.rearrange("p s h -> p (s h)"), start=True, stop=True),
                         reads=["Ysel", "cs"], writes=[PK(3)])
                    P.op("act", lambda e: e.activation(out=cd_all, in_=ps[:, 3, 0:256].rearrange("p (s h) -> p s h", s=16), func=AF.Exp), reads=[PK(3)], writes=["cd_all"])
                    for sq in range(16):
                        Bm, Bmk = Bm_r.next()
                        S0f, S0fk = S0f_r.next()
                        pb = 4 + 2 * (sq % 2)
                        P.op("dve", lambda e, sq=sq, Bm=Bm: e.tensor_scalar(out=Bm, in0=Btm_bf, scalar1=cs[:, SEQM + sq:SEQM + sq + 1], scalar2=None, op0=ALU.mult),
                             reads=["Btm_bf", "cs"], writes=[Bmk])
                        for g in range(4):
                            P.op("pe", lambda e, g=g, Bm=Bm, pb=pb: e.matmul(ps2(pb)[:, g * 256:(g + 1) * 256], lhsT=Bm[:, g * 128:(g + 1) * 128], rhs=Xd_bf[:, g * 256:(g + 1) * 256], start=True, stop=True),
                                 reads=[Bmk, "Xd_bf", "t1"], writes=[PK(pb), PK(pb + 1)])
                        P.dma("sp", S0f, ssd_state[l, sq], writes=[S0fk])
                        P.op("dve", lambda e, sq=sq, S0f=S0f: e.tensor_tensor(out=v16(S0f), in0=v16(S0f), in1=bc(cd_all[:, sq, :].unsqueeze(2), [128, 16, 64]), op=ALU.mult),
                             reads=[S0fk, "cd_all"], writes=[S0fk])
                        P.op("dve", lambda e, S0f=S0f, pb=pb: e.tensor_tensor(out=S0f, in0=S0f, in1=ps2(pb), op=ALU.add), reads=[S0fk, PK(pb), PK(pb + 1)], writes=[S0fk])
                        P.dma("sp", ssd_out[l, 1 + sq], S0f, reads=[S0fk])
            P.barrier()

        if want("S7"):
            AR.reset(base_off)
            ya_all = AR.alloc([128, 8, T], BF16)
            yb_all = AR.alloc([128, 8, T], BF16)
            for k in range(8):
                P.dma("sp", ya_all[:, k, :], yaT_d[k], reads=["yaT_d"], writes=["ya_all"])
                P.dma("act", yb_all[:, k, :], ybT_d[k], reads=["ybT_d"], writes=["yb_all"])
            wr = Rot("w", [128, 8, 128], BF16, 8)
            sga_r = Rot("sga", [128, 512], F32, 2)
            sgb_r = Rot("sgb", [128, 512], F32, 2)
            mixc_r = Rot("mixc", [128, 512], BF16, 3)
            for j in range(8):
                w1, k1_ = load_w(wr, w_so[l], j * 128, 128)
                w2, k2_ = load_w(wr, w_cf[l], j * 128, 128)
                w3, k3_ = load_w(wr, w_in[l], COL_GLU + j * 128, 128)
                w4, k4_ = load_w(wr, w_in[l], COL_GLU + 1024 + j * 128, 128)
                for ti, (c0, n) in enumerate(NTILES):
                    pb = 4 * (ti % 2)
                    proj(pb, w1, k1_, lambda k: ya_all[:, k, c0:c0 + n], n, ["ya_all"])
                    proj(pb + 1, w2, k2_, lambda k: yb_all[:, k, c0:c0 + n], n, ["yb_all"])
                    proj(pb + 2, w3, k3_, lambda k: hn[:, k, c0:c0 + n], n, ["hn"])
                    proj(pb + 3, w4, k4_, lambda k: hn[:, k, c0:c0 + n], n, ["hn"])
                    sga, sgak = sga_r.next()
                    sgb, sgbk = sgb_r.next()
                    mixc, mixk = mixc_r.next()
                    P.op("act", lambda e: e.activation(out=sga[:, 0:n], in_=psb(pb + 2, n), func=AF.Sigmoid), reads=[PK(pb + 2)], writes=[sgak])
                    P.op("act", lambda e: e.activation(out=sgb[:, 0:n], in_=psb(pb + 3, n), func=AF.Sigmoid), reads=[PK(pb + 3)], writes=[sgbk])
                    P.op("dve", lambda e: e.tensor_tensor(out=sga[:, 0:n], in0=psb(pb, n), in1=sga[:, 0:n], op=ALU.mult), reads=[PK(pb), sgak], writes=[sgak])
                    P.op("dve", lambda e: e.tensor_tensor(out=sgb[:, 0:n], in0=psb(pb + 1, n), in1=sgb[:, 0:n], op=ALU.mult), reads=[PK(pb + 1), sgbk], writes=[sgbk])
                    P.op("dve", lambda e: e.tensor_tensor(out=mixc[:, 0:n], in0=sga[:, 0:n], in1=sgb[:, 0:n], op=ALU.add), reads=[sgak, sgbk], writes=[mixk])
                    P.dma("sp", mixT_d[j][:, c0:c0 + n], mixc[:, 0:n], reads=[mixk], writes=["mixT_d"])
            P.barrier()
            AR.reset(base_off)
            wo_all = [AR.alloc([128, 8, 128], BF16) for _ in range(8)]
            for j in range(8):
                P.dma("pool", wo_all[j], w_o[l].rearrange("(k p) c -> p k c", p=128)[:, :, j * 128:(j + 1) * 128], writes=[("wo", j)])
            mixt_r = Rot("mixt", [128, 8, 512], BF16, 2)
            for ti, (c0, n) in enumerate(NTILES):
                mixt, mixtk = mixt_r.next()
                P.dma("sp", mixt[:, :, 0:n], mixT_d.rearrange("k p t -> p k t")[:, :, c0:c0 + n], reads=["mixT_d"], writes=[mixtk])
                for j in range(8):
                    pb = j % 8
                    proj(pb, wo_all[j], ("wo", j), lambda k: mixt[:, k, 0:n], n, [mixtk])
                    P.op("dve", lambda e: e.tensor_tensor(out=h[:, j, c0:c0 + n], in0=h[:, j, c0:c0 + n], in1=psb(pb, n), op=ALU.add),
                         reads=["h", PK(pb)], writes=["h"])
            P.barrier()
        dump_h(2 * l)

        pump_state["auto"] = 0
        if want("PEER"):
            P.barrier()
            for k in range(8):
                P.dma("sp" if k % 2 == 0 else "act", hD[k], h[:, k, :], reads=["h"], writes=["hD"])
            P.barrier()
            for b in range(16):
                P.dma("pool", wqb[l, b].rearrange("p (k c) -> p k c", k=8), wq[l].rearrange("(k p) c -> p k c", p=128)[:, :, b * 128:(b + 1) * 128], writes=[("wqb", l, b)])
            AR.reset()
            AR2 = Arena(h.rearrange("p k t -> p (k t)").bitcast(BF16), cap=8 * T * 2)
            hb = [AR2.alloc([128, 8, 256], F32) for _ in range(2)]
            Gs = [AR.alloc([128, 256, 64], BF16), AR2.alloc([128, 256, 64], BF16)]
            hsq = AR.alloc([128, 8, 256], F32)
            rs = AR.alloc([128, 256], F32)
            xnS = [AR.alloc([128, 8, 256], BF16) for _ in range(2)]
            keys_sb = AR.alloc([128, 16, 128], BF16)
            P.dma("pool", keys_sb, keysT[l].rearrange("b d k -> d b k"), writes=["keys_sb"])
            pump(512, layer=l)
            if l + 1 < n_layers:
                pump(512, layer=l + 1)
            wr = Rot("w", [128, 8, 128], BF16, 2)
            qT = AR.alloc([128, 16, 256], BF16)
            S_sb = AR.alloc([128, 16, 128], F32)
            V = AR.alloc([128, 16, 16], F32)
            I = AR.alloc([128, 16, 16], U32)
            If = AR.alloc([128, 16, 16], F32)
            cand = S_sb.rearrange("p (h a) (b c) -> p h (a b) c", h=8, c=16)
            eq = hsq.rearrange("p a (b c) -> p a b c", b=16)
            Tv = AR.alloc([128, 8, 16], F32)
            POS = AR.alloc([128, 8, 16], U32)
            PA = AR.alloc([128, 8, 16], U32)
            PB = AR.alloc([128, 8, 16], U32)
            Af = AR.alloc([128, 8, 16], F32)
            Bf = AR.alloc([128, 8, 16], F32)
            E = AR.alloc([128, 8, 16], F32)
            Z = AR.alloc([128, 8], F32)
            gate = AR.alloc([128, 8, 16], F32)
            i0 = AR.alloc([128, 8, 16], F32)
            i1 = AR.alloc([128, 8, 16], F32)
            idxS = [[AR.alloc([128, 256], BF16) for _ in range(3)] for _ in range(2)]
            io128b = AR.alloc([128, 128], BF16)
            P.op("dve", lambda e: e.tensor_copy(out=io128b, in_=cs[:, IO128:IO128 + 128]), reads=["cs"], writes=["io128b"])
            A_r = Rot("A", [128, 32, 128], BF16, 2)
            B_r = Rot("B", [128, 32, 64], BF16, 2)
            ub_r = Rot("ub", [128, 8, 128], BF16, 8, arena=AR2)
            vb_r = Rot("vb", [128, 1024], BF16, 8)
            ge_r = Rot("ge", [128, 256], F32, 4)
            act_r = Rot("actb", [128, 256], BF16, 5)
            io16 = cs[:, IO16:IO16 + 16]
            blocks = [(c, 256) for c in range(0, TP, 256)] + [(TP, 128)]
            V4 = V.rearrange("p (h j) a -> p h j a", j=2)
            If4 = If.rearrange("p (h j) a -> p h j a", j=2)
            SK16 = [("S", b_) for b_ in range(16)]
            P2 = [PK(2)]

            def tile_phase(bs, c0, n):
                xn_ = xnS[bs]
                i0T_, i1T_, gT_ = idxS[bs]
                xk, ik, hk = ("xn", bs), ("idxT", bs), ("hb", bs)
                hb_ = hb[bs]
                P.dma("sp", hb_[:, :, 0:n], hD.rearrange("k p t -> p k t")[:, :, c0:c0 + n], reads=["hD"], writes=[hk])
                s3 = hb_[:, :, 0:n]
                P.op("act", lambda e: e.activation(out=hsq[:, :, 0:n], in_=s3, func=AF.Square), reads=[hk], writes=["hsq"])
                for k in range(8):
                    P.op("pe", lambda e: e.matmul(psb(2, n), lhsT=ones, rhs=hsq[:, k, 0:n], start=(k == 0), stop=(k == 7)), reads=["hsq", "cs"], writes=P2)
                P.op("act", lambda e: e.activation(out=rs[:, 0:n], in_=psb(2, n), func=AF.Sqrt, bias=EPS, scale=1.0 / 1024), reads=P2, writes=["rs"])
                P.op("dve", lambda e: e.reciprocal(out=rs[:, 0:n], in_=rs[:, 0:n]), reads=["rs"], writes=["rs"])
                yield
                for k in range(8):
                    P.op("dve", lambda e: e.scalar_tensor_tensor(out=xn_[:, k, 0:n], in0=hb_[:, k, 0:n], scalar=cv[:, l, G_FFN + k:G_FFN + k + 1], in1=rs[:, 0:n], op0=ALU.mult, op1=ALU.mult),
                         reads=[hk, "rs", "cv"], writes=[xk])
                    if k % 2 == 1:
                        yield
                for b in range(16):
                    wb, wbk = wr.next()
                    P.dma("sp", wb, wqb[l, b].rearrange("p (k c) -> p k c", k=8), reads=[("wqb", l, b)], writes=[wbk])
                    hf = 2 + b % 2
                    pq = ps[:, hf, 0:n]
                    for k in range(8):
                        P.op("pe", lambda e: e.matmul(pq, lhsT=wb[:, k, :], rhs=xn_[:, k, 0:n], start=(k == 0), stop=(k == 7)), reads=[wbk, xk], writes=[PK(hf)])
                    P.op("act", lambda e: e.activation(out=qT[:, b, 0:n], in_=pq, func=AF.Identity), reads=[PK(hf)], writes=[("qT", b)])
                    yield
                for ti in range(n // 128):
                    tc0 = ti * 128
                    for p_ in range(4):
                        for bb in range(4):
                            b = 4 * p_ + bb
                            P.op("pe", lambda e: e.matmul(ps[:, 2 + p_ % 2, bb * 128:bb * 128 + 128], lhsT=qT[:, b, tc0:tc0 + 128], rhs=keys_sb[:, b, :], start=True, stop=True),
                                 reads=[("qT", b), "keys_sb"], writes=[PK(2 + p_ % 2)])
                        P.op("act", lambda e: e.activation(out=S_sb[:, 4 * p_:4 * p_ + 4, :], in_=ps[:, 2 + p_ % 2, :].rearrange("p (b c) -> p b c", c=128), func=AF.Identity),
                             reads=[PK(2 + p_ % 2)], writes=[("S", b_) for b_ in range(4 * p_, 4 * p_ + 4)])
                        yield
                    for stg in range(5):
                        for b in range(16):
                            sb_ = S_sb[:, b, :]
                            if stg == 0:
                                P.op("dve", lambda e: e.max(out=V[:, b, 0:8], in_=sb_), reads=[("S", b)], writes=[("V", b)])
                            elif stg == 1:
                                P.op("dve", lambda e: e.max_index(out=I[:, b, 0:8], in_max=V[:, b, 0:8], in_values=sb_), reads=[("S", b), ("V", b)], writes=[("I", b)])
                            elif stg == 2:
                                P.op("dve", lambda e: e.match_replace(out=sb_, in_to_replace=V[:, b, 0:8], in_values=sb_, imm_value=-1e30), reads=[("S", b), ("V", b)], writes=[("S", b)])
                            elif stg == 3:
                                P.op("dve", lambda e: e.max(out=V[:, b, 8:16], in_=sb_), reads=[("S", b)], writes=[("V2", b)])
                            else:
                                P.op("dve", lambda e: e.max_index(out=I[:, b, 8:16], in_max=V[:, b, 8:16], in_values=sb_), reads=[("S", b), ("V2", b)], writes=[("I2", b)])
                            if b % 4 == 3:
                                yield
                    P.op("dve", lambda e: e.tensor_copy(out=If, in_=I), reads=[("I", b_) for b_ in range(16)] + [("I2", b_) for b_ in range(16)], writes=["If"])
                    P.op("dve", lambda e: e.tensor_tensor(out=cand, in0=bc(V4[:, :, 0, :].unsqueeze(3), [128, 8, 16, 16]), in1=bc(V4[:, :, 1, :].unsqueeze(2), [128, 8, 16, 16]), op=ALU.add),
                         reads=[("V", b_) for b_ in range(16)] + [("V2", b_) for b_ in range(16)], writes=[("cand", h_) for h_ in range(8)] + SK16)
                    yield
                    for stg in range(5):
                        for hh in range(8):
                            ch = cand[:, hh].rearrange("p a b -> p (a b)")
                            if stg == 0:
                                P.op("dve", lambda e: e.max(out=Tv[:, hh, 0:8], in_=ch), reads=[("cand", hh)], writes=[("Tv", hh)])
                            elif stg == 1:
                                P.op("dve", lambda e: e.max_index(out=POS[:, hh, 0:8], in_max=Tv[:, hh, 0:8], in_values=ch), reads=[("cand", hh), ("Tv", hh)], writes=[("POS", hh)])
                            elif stg == 2:
                                P.op("dve", lambda e: e.match_replace(out=ch, in_to_replace=Tv[:, hh, 0:8], in_values=ch, imm_value=-1e30), reads=[("cand", hh), ("Tv", hh)], writes=[("cand", hh)])
                            elif stg == 3:
                                P.op("dve", lambda e: e.max(out=Tv[:, hh, 8:16], in_=ch), reads=[("cand", hh)], writes=[("Tv2", hh)])
                            else:
                                P.op("dve", lambda e: e.max_index(out=POS[:, hh, 8:16], in_max=Tv[:, hh, 8:16], in_values=ch), reads=[("cand", hh), ("Tv2", hh)], writes=[("POS2", hh)])
                            if hh % 4 == 3:
                                yield
                    TVK = [("Tv", h_) for h_ in range(8)] + [("Tv2", h_) for h_ in range(8)]
                    PSK = [("POS", h_) for h_ in range(8)] + [("POS2", h_) for h_ in range(8)]
                    P.op("dve", lambda e: e.tensor_tensor(out=E, in0=Tv, in1=bc(Tv[:, :, 0:1], [128, 8, 16]), op=ALU.subtract), reads=TVK, writes=["E"])
                    P.op("act", lambda e: e.activation(out=E, in_=E, func=AF.Exp), reads=["E"], writes=["E"])
                    P.op("dve", lambda e: e.tensor_single_scalar(out=PB, in_=POS, scalar=15, op=ALU.bitwise_and), reads=PSK, writes=["PB"])
                    P.op("dve", lambda e: e.tensor_single_scalar(out=PA, in_=POS, scalar=4, op=ALU.logical_shift_right), reads=PSK, writes=["PA"])
                    yield
                    P.op("dve", lambda e: e.reduce_sum(out=Z, in_=E, axis=AX.X), reads=["E"], writes=["Z"])
                    P.op("dve", lambda e: e.tensor_copy(out=Af, in_=PA), reads=["PA"], writes=["Af"])
                    P.op("dve", lambda e: e.reciprocal(out=Z, in_=Z), reads=["Z"], writes=["Z"])
                    P.op("dve", lambda e: e.tensor_copy(out=Bf, in_=PB), reads=["PB"], writes=["Bf"])
                    P.op("dve", lambda e: e.tensor_tensor(out=gate, in0=E, in1=bc(Z.unsqueeze(2), [128, 8, 16]), op=ALU.mult), reads=["E", "Z"], writes=["gate"])
                    yield
                    for (xf, jj, dst, dk) in ((Af, 0, i0, "i0"), (Bf, 1, i1, "i1")):
                        P.op("dve", lambda e: e.tensor_tensor(out=eq, in0=bc(io16.unsqueeze(1).unsqueeze(1), [128, 8, 16, 16]), in1=bc(xf.unsqueeze(3), [128, 8, 16, 16]), op=ALU.is_equal),
                             reads=["cs", "Af", "Bf"], writes=["hsq"])
                        yield
                        P.op("dve", lambda e: e.tensor_tensor(out=eq, in0=eq, in1=bc(If4[:, :, jj, :].unsqueeze(2), [128, 8, 16, 16]), op=ALU.mult), reads=["hsq", "If"], writes=["hsq"])
                        yield
                        P.op("dve", lambda e: e.reduce_sum(out=dst, in_=eq, axis=AX.X), reads=["hsq"], writes=[dk])
                        yield
                    for q_, (src, dstT, sk) in enumerate(((i0, i0T_, "i0"), (i1, i1T_, "i1"), (gate, gT_, "gate"))):
                        P.op("pe", lambda e: e.transpose(out=ps[:, 2, q_ * 128:(q_ + 1) * 128], in_=src.rearrange("p h r -> p (h r)"), identity=ident),
                             reads=[sk, "cs"], writes=P2)
                        P.op("act", lambda e: e.activation(out=dstT[:, tc0:tc0 + 128], in_=ps[:, 2, q_ * 128:(q_ + 1) * 128], func=AF.Identity),
                             reads=P2, writes=[ik])
                    yield

            def g_build(bs, n, half):
                i0T, i1T, gT = idxS[bs]
                ik = ("idxT", bs)
                G = Gs[half]
                gk = ("G", half)

                def pe_part(A_, Ak, B_, Bk, s0):
                    for q_ in range(4):
                        pg = 2 + q_ % 2
                        for tk in range(8):
                            tok = q_ * 8 + tk
                            P.op("pe", lambda e: e.matmul(ps[:, pg, tk * 64:(tk + 1) * 64], lhsT=A_[:, tok, :], rhs=B_[:, tok, :], start=True, stop=True),
                                 reads=[Ak, Bk], writes=[PK(pg)])
                        gdst = G[:, s0 + q_ * 8:s0 + q_ * 8 + 8, :]
                        P.op("act", lambda e: e.activation(out=gdst, in_=ps[:, pg, :].rearrange("p (a b) -> p a b", a=8), func=AF.Identity),
                             reads=[PK(pg)], writes=[gk])
                        yield

                prev = None
                for sub in range(n // 32):
                    s0 = sub * 32
                    A_, Ak = A_r.next()
                    B_, Bk = B_r.next()
                    P.op("dve", lambda e: e.tensor_tensor(out=A_, in0=bc(io128b.unsqueeze(1), [128, 32, 128]), in1=bc(i0T[:, s0:s0 + 32].unsqueeze(2), [128, 32, 128]), op=ALU.is_equal),
                         reads=["io128b", ik], writes=[Ak])
                    yield
                    P.op("dve", lambda e: e.tensor_tensor(out=B_, in0=bc(io128b[:, half * 64:(half + 1) * 64].unsqueeze(1), [128, 32, 64]),
                                                          in1=bc(i1T[:, s0:s0 + 32].unsqueeze(2), [128, 32, 64]), op=ALU.is_equal),
                         reads=["io128b", ik], writes=[Bk])
                    yield
                    P.op("dve", lambda e: e.tensor_tensor(out=B_, in0=B_, in1=bc(gT[:, s0:s0 + 32].unsqueeze(2), [128, 32, 64]), op=ALU.mult),
                         reads=[Bk, ik], writes=[Bk])
                    yield
                    if prev is not None:
                        for _ in pe_part(*prev):
                            yield
                    prev = (A_, Ak, B_, Bk, s0)
                for _ in pe_part(*prev):
                    yield

            def chain(*gens):
                for g in gens:
                    if g is not None:
                        for _ in g:
                            yield

            def pump_gen(g, k):
                if g is None:
                    return
                for _ in range(k):
                    if next(g, "END") == "END":
                        return

            for _ in chain(tile_phase(0, *blocks[0]), g_build(0, blocks[0][1], 0)):
                pass
            for bi, (c0, n) in enumerate(blocks):
                bs = bi % 2
                xn = xnS[bs]
                xk, hk = ("xn", bs), ("hb", bs)
                has_next = bi + 1 < len(blocks)
                gb1 = g_build(bs, n, 1)
                tpn = tile_phase((bi + 1) % 2, *blocks[bi + 1]) if has_next else None
                gb0n = g_build((bi + 1) % 2, blocks[bi + 1][1], 0) if has_next else None
                for half in range(2):
                    G = Gs[half]
                    gk = ("G", half)
                    if half == 0:
                        bg = chain(gb1, tpn)
                    else:
                        for _ in gb1:
                            pass
                        bg = chain(tpn, gb0n)

                    def stage_pre(k1h):
                        k1 = half * 64 + k1h
                        ub, ubk = ub_r.next()
                        vb, vbk = vb_r.next()
                        P.dma("sp", ub, uTb[l, k1].rearrange("p (k c) -> p k c", k=8), reads=[("uTb", l, k1)], writes=[ubk])
                        P.dma("sp", vb, vKb[l, k1], reads=[("vKb", l, k1)], writes=[vbk])
                        pp = k1h % 2
                        for k in range(8):
                            P.op("pe", lambda e: e.matmul(psb(pp, n), lhsT=ub[:, k, :], rhs=xn[:, k, 0:n], start=(k == 0), stop=(k == 7)),
                                 reads=[ubk, xk], writes=[PK(pp)])
                        ge, gek = ge_r.next()
                        ab, abk = act_r.next()
                        P.op("act", lambda e: e.activation(out=ge[:, 0:n], in_=psb(pp, n), func=AF.Gelu_apprx_tanh), reads=[PK(pp)], writes=[gek])
                        P.op("dve", lambda e: e.tensor_tensor(out=ab[:, 0:n], in0=ge[:, 0:n], in1=G[:, 0:n, k1h], op=ALU.mult),
                             reads=[gek, gk], writes=[abk])
                        return vb, vbk, ab, abk

                    def stage_v(k1h, vb, vbk, ab, abk):
                        for dc in range(8):
                            P.op("pe", lambda e: e.matmul(ps[:, 4 + dc // 2, (dc % 2) * 256:(dc % 2) * 256 + n], lhsT=vb[:, dc * 128:(dc + 1) * 128], rhs=ab[:, 0:n],
                                                          start=(half == 0 and k1h == 0), stop=(half == 1 and k1h == 63)),
                                 reads=[vbk, abk], writes=[PK(4 + dc // 2)])

                    if half == 1:
                        pass
                    LOOK = 2
                    pend = [stage_pre(i) for i in range(LOOK)]
                    for k1h in range(64):
                        if k1h + LOOK < 64:
                            pend.append(stage_pre(k1h + LOOK))
                        stage_v(k1h, *pend.pop(0))
                        pump_gen(bg, PUMP)
                for _ in chain(tpn, gb0n):
                    pass
                hb_ = hb[bs]
                for dc in range(8):
                    P.op("dve", lambda e: e.tensor_tensor(out=hb_[:, dc, 0:n], in0=hb_[:, dc, 0:n], in1=ps[:, 4 + dc // 2, (dc % 2) * 256:(dc % 2) * 256 + n], op=ALU.add),
                         reads=[hk, PK(4 + dc // 2)], writes=[hk])
                P.dma("sp", hD.rearrange("k p t -> p k t")[:, :, c0:c0 + n], hb_[:, :, 0:n], reads=[hk], writes=["hD"])
            P.barrier()
            for k in range(8):
                P.dma("sp" if k % 2 == 0 else "act", h[:, k, :], hD[k], reads=["hD"], writes=["h"])
            P.barrier()
        dump_h(2 * l + 1)

        if want("PLE"):
            P.barrier()
            AR.reset()
            hn = AR.alloc([128, 8, T], BF16)
            hsq = AR.alloc([128, 8, 512], F32)
            rs = AR.alloc([128, 512], F32)
            for ti, (c0, n) in enumerate(NTILES):
                norm_cols(l, c0, n, G_PLE, lambda k, c0=c0, n=n: hn[:, k, c0:c0 + n], ["hn"], hsq, rs, ti % 2, "ple")
            wr = Rot("w", [128, 8, 128], BF16, 4)
            pt_all = AR.alloc([128, 2, T], BF16)
            P.dma("pool", pt_all, pT[l].rearrange("k p t -> p k t"), writes=["pt_all"])
            sg_r = Rot("sg", [128, 512], F32, 2)
            for j in range(8):
                w1, k1_ = load_w(wr, w_pg[l], j * 128, 128)
                w2, k2_ = load_w(wr, w_pp[l], j * 128, 128, nk=2)
                for ti, (c0, n) in enumerate(NTILES):
                    pb = 2 * (ti % 4)
                    proj(pb, w1, k1_, lambda k: hn[:, k, c0:c0 + n], n, ["hn"])
                    proj(pb + 1, w2, k2_, lambda k: pt_all[:, k, c0:c0 + n], n, ["pt_all"], nk=2)
                    sg, sgk = sg_r.next()
                    P.op("act", lambda e: e.activation(out=sg[:, 0:n], in_=psb(pb, n), func=AF.Sigmoid), reads=[PK(pb)], writes=[sgk])
                    P.op("dve", lambda e: e.tensor_tensor(out=sg[:, 0:n], in0=sg[:, 0:n], in1=psb(pb + 1, n), op=ALU.mult), reads=[sgk, PK(pb + 1)], writes=[sgk])
                    P.op("dve", lambda e: e.tensor_tensor(out=h[:, j, c0:c0 + n], in0=h[:, j, c0:c0 + n], in1=sg[:, 0:n], op=ALU.add), reads=["h", sgk], writes=["h"])
            P.barrier()

    P.barrier()
    AR.reset()
    hsq = AR.alloc([128, 8, 512], F32)
    rs = AR.alloc([128, 512], F32)
    yo_r = Rot("yo", [128, 8, 512], F32, 2)
    for ti, (c0, n) in enumerate(NTILES):
        yo, yok = yo_r.next()
        norm_cols(0, c0, n, G_FIN, lambda k, yo=yo, n=n: yo[:, k, 0:n], [yok], hsq, rs, ti % 2, "fin")
        P.dma("sp", yT.rearrange("k p t -> p k t")[:, :, c0:c0 + n], yo[:, :, 0:n], reads=[yok])
    P.emit()
    return nc


def make_consts():
    c = np.zeros((128, NCONST), np.float32)
    s = np.arange(128)[:, None]
    l_ = np.arange(128)[None, :]
    c[:, IDENT:IDENT + 128] = (s == l_)
    c[:, ONES:ONES + 128] = 1.0
    tri = (s <= l_)
    same = (s // 8 == l_ // 8)
    c[:, TRI:TRI + 128] = tri
    c[:, TRIS:TRIS + 128] = tri & same
    c[:, BLKS:BLKS + 128] = same
    c[:, NEGP:NEGP + 128] = np.where(tri, 0.0, -1e5)
    c[:, NEGS:NEGS + 128] = np.where(tri & same, 0.0, -1e5)
    c[:, SEQM:SEQM + 16] = (s // 8 == np.arange(16)[None, :])
    c[:, IO16:IO16 + 16] = np.arange(16)[None, :]
    c[:, IO128:IO128 + 128] = np.arange(128)[None, :]
    sel = np.zeros((128, 16, 128), np.float32)
    for q in range(16):
        sel[8 * q, q, :] = 1.0
    return c, sel.reshape(128, 2048)


def cm(v):
    return np.asarray(v).reshape(-1, 128).T


def prepare_inputs(x_prompt, x_sample, state_ssd, state_ssd_conv, state_cf_conv, p_prompt, p_sample,
                   g_mix, w_in, ssd_conv_w, ssd_conv_b, ssd_dt_bias, ssd_a_log, ssd_d, ssd_norm_g,
                   w_ssd_out, cf_dw_w, cf_dw_b, cf_ln_g, cf_ln_b, w_cf_out, w_o, g_ffn,
                   peer_wq, peer_keys, peer_u, peer_v, g_ple, w_ple_gate, w_ple_proj, g_final):
    f = lambda a: np.ascontiguousarray(np.asarray(a, dtype=np.float32))
    consts, selall = make_consts()
    cvec = np.zeros((128, 2, NCV), np.float32)
    rvec = np.zeros((2, NRV), np.float32)
    for l in range(2):
        cvec[:, l, G_MIX:G_MIX + 8] = cm(g_mix[l])
        cvec[:, l, G_FFN:G_FFN + 8] = cm(g_ffn[l])
        cvec[:, l, G_PLE:G_PLE + 8] = cm(g_ple[l])
        cvec[:, l, G_FIN:G_FIN + 8] = cm(g_final)
        cvec[:, l, SCW:SCW + 64] = np.asarray(ssd_conv_w[l]).reshape(4, 16, 128).transpose(2, 1, 0).reshape(128, 64)
        cvec[:, l, SCB:SCB + 16] = cm(ssd_conv_b[l])
        cvec[:, l, CFW:CFW + 248] = np.asarray(cf_dw_w[l]).reshape(31, 8, 128).transpose(2, 1, 0).reshape(128, 248)
        cvec[:, l, CFB:CFB + 8] = cm(cf_dw_b[l])
        rvec[l, DTB:DTB + 16] = ssd_dt_bias[l]
        rvec[l, ALOG:ALOG + 16] = ssd_a_log[l]
        rvec[l, DSK:DSK + 16] = ssd_d[l]
        rvec[l, NG:NG + 1024] = ssd_norm_g[l]
        rvec[l, LNG:LNG + 1024] = cf_ln_g[l]
        rvec[l, LNB:LNB + 1024] = cf_ln_b[l]
    keysT = f(np.asarray(peer_keys).reshape(2, 16, 128, 128).transpose(0, 1, 3, 2))
    uT = f(np.asarray(peer_u).reshape(2, 128, 128, 8, 128).transpose(0, 2, 4, 3, 1).reshape(2, 128, 128, 1024))
    vK = f(np.asarray(peer_v).reshape(2, 128, 128, 1024).transpose(0, 2, 1, 3))
    shared = dict(consts=consts, selall=selall, cvec=cvec, rvec=rvec, w_in=f(w_in), w_ssd_out=f(w_ssd_out), w_cf_out=f(w_cf_out),
                  w_o=f(w_o), peer_wq=f(peer_wq), keysT=keysT, uT=uT, vK=vK, w_ple_gate=f(w_ple_gate), w_ple_proj=f(w_ple_proj))
    in_maps = []
    for c in range(8):
        sq = slice(16 * c, 16 * c + 16)
        xt = np.concatenate([np.asarray(x_prompt[c]), np.asarray(x_sample[sq]).reshape(128, 1024)], axis=0)
        xT = f(xt.T.reshape(8, 128, T))
        pt = np.concatenate([np.asarray(p_prompt[:, c]), np.asarray(p_sample[:, sq]).reshape(2, 128, 256)], axis=1)
        pT = f(pt.transpose(0, 2, 1).reshape(2, 2, 128, T))
        cfh = f(np.asarray(state_cf_conv[:, sq]).transpose(0, 3, 1, 2).reshape(2, 8, 128, 16, 30))
        sh = f(np.asarray(state_ssd_conv[:, sq]).transpose(0, 3, 1, 2).reshape(2, 16, 128, 16, 3))
        ss = f(np.asarray(state_ssd[:, sq]).reshape(2, 16, 1024, 128).transpose(0, 1, 3, 2))
        m = dict(shared)
        m.update(xT=xT, pT=pT, cf_hist=cfh, ssd_hist=sh, ssd_state=ss)
        in_maps.append(m)
    return in_maps


def assemble(results):
    y_p = np.zeros((8, 2048, 1024), np.float32)
    y_s = np.zeros((128, 8, 1024), np.float32)
    ssd_p = np.zeros((2, 8, 16, 64, 128), np.float32)
    sconv_p = np.zeros((2, 8, 3, 2048), np.float32)
    cf_p = np.zeros((2, 8, 30, 1024), np.float32)
    ssd_s = np.zeros((2, 128, 16, 64, 128), np.float32)
    sconv_s = np.zeros((2, 128, 3, 2048), np.float32)
    cf_s = np.zeros((2, 128, 30, 1024), np.float32)
    for c, r in enumerate(results):
        sq = slice(16 * c, 16 * c + 16)
        y = np.asarray(r["yT"]).reshape(1024, T).T
        y_p[c] = y[:2048]
        y_s[sq] = y[2048:].reshape(16, 8, 1024)
        so = np.asarray(r["ssd_out"]).transpose(0, 1, 3, 2).reshape(2, 17, 16, 64, 128)
        ssd_p[:, c] = so[:, 0]
        ssd_s[:, sq] = so[:, 1:]
        sc = np.asarray(r["sconv_out"]).reshape(2, 2048, 17, 3).transpose(0, 2, 3, 1)
        sconv_p[:, c] = sc[:, 0]
        sconv_s[:, sq] = sc[:, 1:]
        cf = np.asarray(r["cfconv_out"]).reshape(2, 1024, 17, 30).transpose(0, 2, 3, 1)
        cf_p[:, c] = cf[:, 0]
        cf_s[:, sq] = cf[:, 1:]
    return (y_p, y_s, ssd_p, sconv_p, cf_p, ssd_s, sconv_s, cf_s)


def kernel(**inputs):
    in_maps = prepare_inputs(**inputs)
    nc = build_program()
    res = run_bass_kernel_spmd(nc, in_maps, core_ids=list(range(8)))
    return assemble(res.results)
```

```python
import types
import numpy as np
import concourse.bass as bass
import concourse.mybir as mybir
from concourse.bass_utils import run_bass_kernel_spmd

F32 = mybir.dt.float32
BF16 = mybir.dt.bfloat16
U32 = mybir.dt.uint32
AF = mybir.ActivationFunctionType
ALU = mybir.AluOpType
AX = mybir.AxisListType

T = 2176
TP = 2048
NTL = 17
NTILES = [(0, 512), (512, 512), (1024, 512), (1536, 512), (2048, 128)]
EPS = 1e-6
COL_Z = 1024
COL_XBC = 3072
COL_DT = 3088
COL_GLU = 5136
G_MIX, G_FFN, G_PLE, G_FIN, SCW, SCB, CFW, CFB, NCV = 0, 8, 16, 24, 32, 96, 112, 360, 368
DTB, ALOG, DSK, NG, LNG, LNB, NRV = 0, 16, 32, 48, 1072, 2096, 3120
IDENT, ONES, TRI, TRIS, BLKS, NEGP, NEGS, SEQM, IO16, IO128, NCONST = 0, 128, 256, 384, 512, 640, 768, 896, 912, 928, 1056
ARENA = 67400

ENGS = ("sp", "act", "dve", "pool", "pe")
DBG = {"s6_stop": 99, "s6_tiles": None, "s6_sub": 99}
PUMP = 2


def _freeze(fn):
    if getattr(fn, "__closure__", None) is None:
        return fn
    cells = []
    for c in fn.__closure__:
        try:
            cells.append(types.CellType(c.cell_contents))
        except ValueError:
            cells.append(c)
    return types.FunctionType(fn.__code__, fn.__globals__, fn.__name__, fn.__defaults__, tuple(cells))


class Prog:
    def __init__(self, nc, n_dma_sems=10):
        self.nc = nc
        self.ops = []
        self.n_dma_sems = n_dma_sems
        self.last_writer = {}
        self.readers = {}
        self.last_on_eng = {}
        self.dmas_since_barrier = []

    def op(self, eng, fn, reads=(), writes=(), dma=False, extra_deps=()):
        idx = len(self.ops)
        deps = set(extra_deps)
        for r in reads:
            w = self.last_writer.get(r)
            if w is not None:
                deps.add(w)
        for w_ in writes:
            w = self.last_writer.get(w_)
            if w is not None:
                deps.add(w)
            for rd in self.readers.get(w_, {}).values():
                deps.add(rd)
        deps.discard(idx)
        for r in reads:
            self.readers.setdefault(r, {})[eng if not dma else ("dma", idx)] = idx
        for w_ in writes:
            self.last_writer[w_] = idx
            self.readers[w_] = {}
        self.ops.append(dict(eng=eng, fn=_freeze(fn), deps=deps, dma=dma, needed=False))
        self.last_on_eng[eng] = idx
        if dma:
            self.dmas_since_barrier.append(idx)
        return idx

    def dma(self, q, out, in_, reads=(), writes=()):
        return self.op(q, lambda e: e.dma_start(out=out, in_=in_), reads, writes, dma=True)

    def barrier(self):
        deps = set(self.last_on_eng.values()) | set(self.dmas_since_barrier)
        self.dmas_since_barrier = []
        for e in ENGS:
            self.op(e, lambda eng: None, extra_deps=deps)

    def emit(self):
        nc = self.nc
        ops = self.ops

        def skip(p, o):
            return (not p["dma"]) and (not o["dma"]) and p["eng"] == o["eng"] == "pe"

        for o in ops:
            for d in o["deps"]:
                p = ops[d]
                if not skip(p, o):
                    p["needed"] = True
        esem = {e: nc.alloc_semaphore(f"s_{e}") for e in ENGS}
        dq = ("sp", "act", "pool")
        dsem = {e: [nc.alloc_semaphore(f"d_{e}{i}") for i in range(self.n_dma_sems)] for e in dq}
        ecount = {e: 0 for e in ENGS}
        dcount = {e: [0] * self.n_dma_sems for e in dq}
        drr = {e: 0 for e in dq}
        for o in ops:
            e = o["eng"]
            if o["dma"]:
                k = drr[e]
                drr[e] = (k + 1) % self.n_dma_sems
                o["prev_target"] = dcount[e][k]
                dcount[e][k] += 16
                o["done"] = (dsem[e][k], dcount[e][k], ("d", e, k))
            elif o["needed"]:
                ecount[e] += 1
                o["done"] = (esem[e], ecount[e], ("e", e))
            else:
                o["done"] = None
        per_eng = {e: [] for e in ENGS}
        for i, o in enumerate(ops):
            per_eng[o["eng"]].append(i)

        def run_engine(ename, eobj):
            waited = {}
            pending_inc = []
            for i in per_eng[ename]:
                o = ops[i]
                need = {}
                for d in o["deps"]:
                    p = ops[d]
                    if p["done"] is None or skip(p, o):
                        continue
                    sem, val, key = p["done"]
                    if need.get(key, (None, 0))[1] < val:
                        need[key] = (sem, val)
                if o["dma"]:
                    sem, val, key = o["done"]
                    pt = o["prev_target"]
                    if pt > 0 and need.get(key, (None, 0))[1] < pt:
                        need[key] = (sem, pt)
                for key, (sem, val) in need.items():
                    if waited.get(key, 0) >= val:
                        continue
                    eobj.wait_ge(sem, val)
                    waited[key] = val
                ins = o["fn"](eobj)
                if o["done"] is not None:
                    sem, val, key = o["done"]
                    if ins is None:
                        ins = eobj.nop()
                    ins.then_inc(sem, 16 if o["dma"] else 1)
            if ename == "sp":
                for e2 in dq:
                    for k in range(self.n_dma_sems):
                        if dcount[e2][k] > 0 and waited.get(("d", e2, k), 0) < dcount[e2][k]:
                            eobj.wait_ge(dsem[e2][k], dcount[e2][k])

        with nc.Block() as block:
            @block.sync
            def _(e):
                run_engine("sp", e)

            @block.scalar
            def _(e):
                run_engine("act", e)

            @block.vector
            def _(e):
                run_engine("dve", e)

            @block.gpsimd
            def _(e):
                run_engine("pool", e)

            @block.tensor
            def _(e):
                run_engine("pe", e)


class Arena:
    def __init__(self, ap, cap=None):
        self.ap = ap
        self.off = 0
        self.cap = ARENA if cap is None else cap

    def reset(self, off=0):
        self.off = off

    def alloc(self, shape, dt):
        n = 1
        for s in shape[1:]:
            n *= s
        ne = n * (2 if dt in (F32, U32) else 1)
        ne = (ne + 15) // 16 * 16
        assert self.off + ne <= self.cap, (self.off, ne)
        v = self.ap[:, self.off:self.off + ne]
        self.off += ne
        if dt in (F32, U32):
            v = v.bitcast(dt)[:, 0:n]
        else:
            v = v[:, 0:n]
        if len(shape) == 3:
            v = v.rearrange("p (a b) -> p a b", a=shape[1])
        elif len(shape) == 4:
            v = v.rearrange("p (a b c) -> p a b c", a=shape[1], b=shape[2])
        return v


def bc(ap, shape):
    return ap.to_broadcast(list(shape))


def build_program(n_layers=2, stages=None, dbg=None):
    nc = bass.Bass("TRN2", target_bir_lowering=False)

    def D(name, shape, dt=F32, kind="ExternalInput"):
        return nc.dram_tensor(name, list(shape), dt, kind=kind).ap()

    xT = D("xT", [8, 128, T])
    pT = D("pT", [2, 2, 128, T])
    cf_hist = D("cf_hist", [2, 8, 128, 16, 30])
    ssd_hist = D("ssd_hist", [2, 16, 128, 16, 3])
    ssd_state = D("ssd_state", [2, 16, 128, 1024])
    consts_d = D("consts", [128, NCONST])
    selall_d = D("selall", [128, 2048])
    cvec_d = D("cvec", [128, 2, NCV])
    rvec_d = D("rvec", [2, NRV])
    w_in = D("w_in", [2, 1024, 7184])
    w_so = D("w_ssd_out", [2, 1024, 1024])
    w_cf = D("w_cf_out", [2, 1024, 1024])
    w_o = D("w_o", [2, 1024, 1024])
    wq = D("peer_wq", [2, 1024, 2048])
    keysT = D("keysT", [2, 16, 128, 128])
    uT = D("uT", [2, 128, 128, 1024])
    vK = D("vK", [2, 128, 128, 1024])
    w_pg = D("w_ple_gate", [2, 1024, 1024])
    w_pp = D("w_ple_proj", [2, 256, 1024])
    yT = D("yT", [8, 128, T], kind="ExternalOutput")
    ssd_out = D("ssd_out", [2, 17, 128, 1024], kind="ExternalOutput")
    sconv_out = D("sconv_out", [2, 16, 128, 17, 3], kind="ExternalOutput")
    cfconv_out = D("cfconv_out", [2, 8, 128, 17, 30], kind="ExternalOutput")
    skind = "ExternalOutput" if dbg else "Internal"
    cT = D("cT", [8, 128, T], kind=skind)
    xbcT = D("xbcT", [16, 128, T], kind=skind)
    yaT_d = D("yaT_d", [8, 128, T], BF16, kind=skind)
    ybT_d = D("ybT_d", [8, 128, T], BF16, kind=skind)
    uTb = D("uTb", [2, 128, 128, 1024], BF16, kind="Internal")
    vKb = D("vKb", [2, 128, 128, 1024], BF16, kind="Internal")
    wqb = D("wqb", [2, 16, 128, 1024], BF16, kind="Internal")
    hD = D("hD", [8, 128, T], kind="Internal")
    mixT_d = D("mixT_d", [8, 128, T], BF16, kind="Internal")
    hdbg = D("hdbg", [4, 8, 128, T], kind="ExternalOutput") if dbg else None

    cs = nc.alloc_sbuf_tensor("cs", [128, NCONST], F32).ap()
    cv = nc.alloc_sbuf_tensor("cv", [128, 2, NCV], F32).ap()
    h = nc.alloc_sbuf_tensor("h", [128, 8, T], F32).ap()
    arena_ap = nc.alloc_sbuf_tensor("arena", [128, ARENA], BF16).ap()
    ps = nc.alloc_psum_tensor("ps", [128, 8, 512], F32).ap()
    AR = Arena(arena_ap)
    P = Prog(nc)

    ident = cs[:, IDENT:IDENT + 128]
    ones = cs[:, ONES:ONES + 128]

    def psb(b, n=512):
        return ps[:, b, 0:n]

    def ps2(b):
        return ps[:, b:b + 2, :].rearrange("p a b -> p (a b)")

    def PK(b):
        return ("ps", b)

    P.dma("sp", cs, consts_d, writes=["cs"])
    P.dma("sp", cv, cvec_d, writes=["cv"])
    for k in range(8):
        P.dma("sp" if k % 2 == 0 else "act", h[:, k, :], xT[k], writes=["h"])

    class Rot:
        def __init__(self, name, shape, dt, n, arena=None):
            self.bufs = [(arena or AR).alloc(shape, dt) for _ in range(n)]
            self.name = name
            self.i = 0

        def next(self):
            i = self.i
            self.i = (i + 1) % len(self.bufs)
            return self.bufs[i], (self.name, i)

    def load_w(rot, w2d, c0, ncols, nk=8):
        buf, key = rot.next()
        src = w2d.rearrange("(k p) c -> p k c", p=128)[:, :, c0:c0 + ncols]
        P.dma("pool", buf[:, 0:nk, 0:ncols], src, writes=[key])
        if pump_state["auto"]:
            pump(pump_state["auto"], layer=0)
        return buf, key

    precast_pending = [(l_, k1_, w_) for l_ in range(n_layers) for k1_ in range(128) for w_ in (0, 1)]
    pump_state = {"auto": 0}

    def pump(n, layer=None):
        cnt = 0
        i = 0
        while cnt < n and i < len(precast_pending):
            l_, k1_, w_ = precast_pending[i]
            if layer is not None and l_ != layer:
                i += 1
                continue
            precast_pending.pop(i)
            if w_ == 0:
                P.dma("pool", uTb[l_, k1_], uT[l_, k1_], writes=[("uTb", l_, k1_)])
            else:
                P.dma("pool", vKb[l_, k1_], vK[l_, k1_], writes=[("vKb", l_, k1_)])
            cnt += 1

    def norm_cols(l, c0, n, gcol, out_fn, out_keys, hsq, rs, psbank, tag, src=None, srckey="h"):
        s3 = h[:, :, c0:c0 + n] if src is None else src[:, :, 0:n]
        P.op("act", lambda e: e.activation(out=hsq[:, :, 0:n], in_=s3, func=AF.Square),
             reads=[srckey], writes=["hsq"])
        for k in range(8):
            P.op("pe", lambda e, k=k: e.matmul(psb(psbank, n), lhsT=ones, rhs=hsq[:, k, 0:n], start=(k == 0), stop=(k == 7)),
                 reads=["hsq", "cs"], writes=[PK(psbank)])
        P.op("act", lambda e: e.activation(out=rs[:, 0:n], in_=psb(psbank, n), func=AF.Sqrt, bias=EPS, scale=1.0 / 1024),
             reads=[PK(psbank)], writes=["rs"])
        P.op("dve", lambda e: e.reciprocal(out=rs[:, 0:n], in_=rs[:, 0:n]), reads=["rs"], writes=["rs"])
        for k in range(8):
            o_ = out_fn(k)
            i_ = s3[:, k, :]
            P.op("dve", lambda e, k=k, o_=o_, i_=i_: e.scalar_tensor_tensor(out=o_, in0=i_, scalar=cv[:, l, gcol + k:gcol + k + 1],
                                                                in1=rs[:, 0:n], op0=ALU.mult, op1=ALU.mult),
                 reads=[srckey, "rs", "cv"], writes=out_keys)

    def proj(psbank, wbuf, wkey, rhs_fn, n, rkeys, nk=8):
        for k in range(nk):
            r_ = rhs_fn(k)
            P.op("pe", lambda e, k=k, r_=r_: e.matmul(psb(psbank, n), lhsT=wbuf[:, k, :], rhs=r_, start=(k == 0), stop=(k == nk - 1)),
                 reads=[wkey] + list(rkeys), writes=[PK(psbank)])

    def conv(l, src_p, src_s, acc, K, wcol, bcol, skeys, akey, acc2=None, a2key=None, acc3=None):
        def views(a):
            return a[:, 0:TP], a[:, TP:T].rearrange("p (s t) -> p s t", t=8)
        ksplit = K if acc2 is None else 23
        tp_, ts_ = views(acc3) if acc3 is not None else (None, None)
        for (eng, a_, ak_, k0, k1_) in (("dve", acc, akey, 0, ksplit), ("pool", acc2, a2key, ksplit, K)):
            if k0 >= k1_:
                continue
            ap_, as_ = views(a_)
            for k in range(k0, k1_):
                w_ = cv[:, l, wcol + k:wcol + k + 1]
                for (a, s_) in ((ap_, src_p[:, k:k + TP]), (as_, src_s[:, :, k:k + 8])):
                    if k == 0:
                        P.op(eng, lambda e, a=a, s_=s_, w_=w_: e.tensor_scalar(out=a, in0=s_, scalar1=w_, scalar2=cv[:, l, bcol:bcol + 1],
                                                                                op0=ALU.mult, op1=ALU.add),
                             reads=list(skeys) + ["cv"], writes=[ak_])
                    elif k == k0:
                        P.op(eng, lambda e, a=a, s_=s_, w_=w_: e.tensor_scalar(out=a, in0=s_, scalar1=w_, scalar2=None, op0=ALU.mult),
                             reads=list(skeys) + ["cv"], writes=[ak_])
                    elif eng == "dve":
                        P.op(eng, lambda e, a=a, s_=s_, w_=w_: e.scalar_tensor_tensor(out=a, in0=s_, scalar=w_, in1=a, op0=ALU.mult, op1=ALU.add),
                             reads=list(skeys) + ["cv", ak_], writes=[ak_])
                    else:
                        t_ = tp_ if a is ap_ else ts_
                        P.op(eng, lambda e, t_=t_, s_=s_, w_=w_: e.tensor_scalar(out=t_, in0=s_, scalar1=w_, scalar2=None, op0=ALU.mult),
                             reads=list(skeys) + ["cv"], writes=["acc3"])
                        P.op(eng, lambda e, a=a, t_=t_: e.tensor_tensor(out=a, in0=a, in1=t_, op=ALU.add), reads=["acc3", ak_], writes=[ak_])
        if acc2 is not None:
            P.op("dve", lambda e: e.tensor_tensor(out=acc, in0=acc, in1=acc2, op=ALU.add), reads=[akey, a2key], writes=[akey])

    def dump_h(i):
        if dbg:
            for k in range(8):
                P.dma("sp", hdbg[i, k], h[:, k, :], reads=["h"])

    want = (lambda s: True) if stages is None else (lambda s: s in stages)

    for l in range(n_layers):
        pump_state["auto"] = 8 if l == 0 else 0
        P.barrier()
        AR.reset()
        hn = AR.alloc([128, 8, T], BF16)
        base_off = AR.off
        hsq = AR.alloc([128, 8, 512], F32)
        rs = AR.alloc([128, 512], F32)
        for ti, (c0, n) in enumerate(NTILES):
            norm_cols(l, c0, n, G_MIX, lambda k, c0=c0, n=n: hn[:, k, c0:c0 + n], ["hn"], hsq, rs, ti % 2, "mix")
        P.barrier()

        if want("S1"):
            AR.reset(base_off)
            wr = Rot("w", [128, 8, 128], BF16, 4)
            ubuf_r = Rot("ubuf", [128, 30 + TP], F32, 2)
            ubufs_r = Rot("ubuf_s", [128, 16, 38], F32, 2)
            acc_r1 = Rot("acc1", [128, T], F32, 2)
            sg_r = Rot("sg", [128, 512], F32, 2)
            for ub_ in ubuf_r.bufs:
                P.op("dve", lambda e: e.memset(ub_[:, 0:30], 0.0), writes=[("ubuf", 0), ("ubuf", 1)])
            for j in range(8):
                wa, wak = load_w(wr, w_in[l], COL_DT + j * 128, 128)
                wg, wgk = load_w(wr, w_in[l], COL_DT + 1024 + j * 128, 128)
                ubuf, ubk_ = ubuf_r.next()
                ubuf_s, ubsk_ = ubufs_r.next()
                acc, acck_ = acc_r1.next()
                P.dma("sp", ubuf_s[:, :, 0:30], cf_hist[l, j], writes=[ubsk_])
                for ti, (c0, n) in enumerate(NTILES):
                    ba, bb = 2 * (ti % 2), 2 * (ti % 2) + 1
                    proj(ba, wa, wak, lambda k, c0=c0, n=n: hn[:, k, c0:c0 + n], n, ["hn"])
                    proj(bb, wg, wgk, lambda k, c0=c0, n=n: hn[:, k, c0:c0 + n], n, ["hn"])
                    sg, sgk = sg_r.next()
                    P.op("act", lambda e: e.activation(out=sg[:, 0:n], in_=psb(bb, n), func=AF.Sigmoid),
                         reads=[PK(bb)], writes=[sgk])
                    if c0 < TP:
                        P.op("dve", lambda e: e.tensor_tensor(out=ubuf[:, 30 + c0:30 + c0 + n], in0=psb(ba, n), in1=sg[:, 0:n], op=ALU.mult),
                             reads=[PK(ba), sgk], writes=[ubk_])
                    else:
                        P.op("dve", lambda e: e.tensor_tensor(out=ubuf_s[:, :, 30:38], in0=psb(ba, 128).rearrange("p (s t) -> p s t", t=8),
                                                              in1=sg[:, 0:128].rearrange("p (s t) -> p s t", t=8), op=ALU.mult),
                             reads=[PK(ba), sgk], writes=[ubsk_])
                P.dma("sp", cfconv_out[l, j, :, 0, :], ubuf[:, TP:TP + 30], reads=[ubk_])
                P.dma("sp", cfconv_out[l, j, :, 1:17, :], ubuf_s[:, :, 8:38], reads=[ubsk_])
                conv(l, ubuf, ubuf_s, acc, 31, CFW + j * 31, CFB + j, [ubk_, ubsk_], acck_)
                P.dma("sp", cT[j], acc, reads=[acck_])
            P.barrier()

        if want("S2"):
            AR.reset(base_off)
            rv = AR.alloc([128, 2048], F32)
            P.dma("sp", rv, rvec_d[l, LNG:LNG + 2048].partition_broadcast(128), writes=["rv"])
            cin_r = Rot("cin", [128, 8, 128], F32, 2)
            lnb = AR.alloc([128, 1024], F32)
            ybtm = AR.alloc([128, 1024], F32)
            ybst_r = Rot("ybst", [128, 8, 128], BF16, 2)
            st = AR.alloc([128, 8], F32)
            for t in range(NTL):
                cin, cink = cin_r.next()
                P.dma("sp", cin, cT.rearrange("k p t -> p k t")[:, :, t * 128:(t + 1) * 128], writes=[cink])
                b0 = 4 * (t % 2)
                for k in range(8):
                    P.op("pe", lambda e, k=k, b0=b0: e.transpose(out=ps[:, b0 + k // 4, (k % 4) * 128:(k % 4) * 128 + 128], in_=cin[:, k, :], identity=ident),
                         reads=[cink, "cs"], writes=[PK(b0), PK(b0 + 1)])
                ptm = ps2(b0)
                P.op("dve", lambda e: e.memset(st, 0.0), writes=["st"])
                P.op("dve", lambda e, ptm=ptm: e.reduce_sum(out=st[:, 0:1], in_=ptm, axis=AX.X), reads=[PK(b0), PK(b0 + 1), "st"], writes=["st"])
                P.op("act", lambda e, ptm=ptm: e.activation(out=lnb, in_=ptm, func=AF.Square, accum_out=st[:, 1:2]),
                     reads=[PK(b0), PK(b0 + 1), "st"], writes=["lnb", "st"])
                P.op("dve", lambda e: e.tensor_scalar(out=st[:, 2:3], in0=st[:, 0:1], scalar1=1.0 / 1024, scalar2=None, op0=ALU.mult), reads=["st"], writes=["st"])
                P.op("dve", lambda e: e.tensor_tensor(out=st[:, 3:4], in0=st[:, 2:3], in1=st[:, 2:3], op=ALU.mult), reads=["st"], writes=["st"])
                P.op("dve", lambda e: e.scalar_tensor_tensor(out=st[:, 4:5], in0=st[:, 1:2], scalar=1.0 / 1024, in1=st[:, 3:4], op0=ALU.mult, op1=ALU.subtract),
                     reads=["st"], writes=["st"])
                P.op("act", lambda e: e.activation(out=st[:, 5:6], in_=st[:, 4:5], func=AF.Sqrt, bias=EPS, scale=1.0), reads=["st"], writes=["st"])
                P.op("dve", lambda e: e.reciprocal(out=st[:, 6:7], in_=st[:, 5:6]), reads=["st"], writes=["st"])
                P.op("dve", lambda e, ptm=ptm: e.tensor_scalar(out=lnb, in0=ptm, scalar1=st[:, 2:3], scalar2=st[:, 6:7], op0=ALU.subtract, op1=ALU.mult),
                     reads=[PK(b0), PK(b0 + 1), "st", "lnb"], writes=["lnb"])
                P.op("dve", lambda e: e.tensor_tensor(out=lnb, in0=lnb, in1=rv[:, 0:1024], op=ALU.mult), reads=["lnb", "rv"], writes=["lnb"])
                P.op("dve", lambda e: e.tensor_tensor(out=lnb, in0=lnb, in1=rv[:, 1024:2048], op=ALU.add), reads=["lnb", "rv"], writes=["lnb"])
                P.op("act", lambda e: e.activation(out=ybtm, in_=lnb, func=AF.Silu), reads=["lnb"], writes=["ybtm"])
                b2 = b0 + 2
                for k in range(8):
                    P.op("pe", lambda e, k=k, b2=b2: e.transpose(out=ps[:, b2 + k // 4, (k % 4) * 128:(k % 4) * 128 + 128], in_=ybtm[:, k * 128:(k + 1) * 128], identity=ident),
                         reads=["ybtm", "cs"], writes=[PK(b2), PK(b2 + 1)])
                ybst, ybk = ybst_r.next()
                P.op("act", lambda e, b2=b2, ybst=ybst: e.activation(out=ybst, in_=ps2(b2).rearrange("p (k t) -> p k t", k=8), func=AF.Identity),
                     reads=[PK(b2), PK(b2 + 1)], writes=[ybk])
                P.dma("sp", ybT_d.rearrange("k p t -> p k t")[:, :, t * 128:(t + 1) * 128], ybst, reads=[ybk], writes=["ybT_d"])
            P.barrier()

        if want("S3"):
            AR.reset(base_off)
            wr = Rot("w", [128, 8, 128], BF16, 3)
            xbuf_r = Rot("xbuf", [128, 3 + TP], F32, 2)
            xbufs_r = Rot("xbuf_s", [128, 16, 11], F32, 2)
            acc_r = Rot("acc", [128, T], F32, 2)
            for xb_ in xbuf_r.bufs:
                P.op("dve", lambda e: e.memset(xb_[:, 0:3], 0.0), writes=[("xbuf", 0), ("xbuf", 1)])
            for j in range(16):
                w_, wk = load_w(wr, w_in[l], COL_Z + j * 128, 128)
                xbuf, xbk_ = xbuf_r.next()
                xbuf_s, xbsk_ = xbufs_r.next()
                P.dma("sp", xbuf_s[:, :, 0:3], ssd_hist[l, j], writes=[xbsk_])
                for ti, (c0, n) in enumerate(NTILES):
                    ba = ti % 4
                    proj(ba, w_, wk, lambda k, c0=c0, n=n: hn[:, k, c0:c0 + n], n, ["hn"])
                    if c0 < TP:
                        P.op("act", lambda e: e.activation(out=xbuf[:, 3 + c0:3 + c0 + n], in_=psb(ba, n), func=AF.Identity),
                             reads=[PK(ba)], writes=[xbk_])
                    else:
                        P.op("act", lambda e: e.activation(out=xbuf_s[:, :, 3:11], in_=psb(ba, 128).rearrange("p (s t) -> p s t", t=8), func=AF.Identity),
                             reads=[PK(ba)], writes=[xbsk_])
                P.dma("sp", sconv_out[l, j, :, 0, :], xbuf[:, TP:TP + 3], reads=[xbk_])
                P.dma("sp", sconv_out[l, j, :, 1:17, :], xbuf_s[:, :, 8:11], reads=[xbsk_])
                acc, acck = acc_r.next()
                conv(l, xbuf, xbuf_s, acc, 4, SCW + j * 4, SCB + j, [xbk_, xbsk_], acck)
                P.op("act", lambda e: e.activation(out=acc, in_=acc, func=AF.Silu), reads=[acck], writes=[acck])
                P.dma("sp", xbcT[j], acc, reads=[acck], writes=["xbcT"])
            P.barrier()

        pump_state["auto"] = 0
        if want("S6"):
            AR.reset(base_off)
            rvs = AR.alloc([128, 48], F32)
            rvn = AR.alloc([128, 1024], F32)
            P.dma("sp", rvs, rvec_d[l, 0:48].partition_broadcast(128), writes=["rvs"])
            P.dma("sp", rvn, rvec_d[l, NG:NG + 1024].partition_broadcast(128), writes=["rvn"])
            wdt = AR.alloc([128, 8, 16], BF16)
            P.dma("pool", wdt, w_in[l].rearrange("(k p) c -> p k c", p=128)[:, :, COL_XBC:COL_XBC + 16], writes=["wdt"])
            wz = AR.alloc([128, 8, 1024], BF16)
            for k in range(8):
                P.dma("pool", wz[:, k, :], w_in[l][k * 128:(k + 1) * 128, 0:1024], writes=["wz"])
            dt_tok = AR.alloc([128, NTL, 16], F32)
            dA_tok = AR.alloc([128, NTL, 16], F32)
            Abc = AR.alloc([128, 16], F32)
            tmp16 = AR.alloc([128, 16], F32)
            P.op("act", lambda e: e.activation(out=Abc, in_=rvs[:, ALOG:ALOG + 16], func=AF.Exp), reads=["rvs"], writes=["Abc"])
            P.op("dve", lambda e: e.tensor_scalar(out=Abc, in0=Abc, scalar1=-1.0, scalar2=None, op0=ALU.mult), reads=["Abc"], writes=["Abc"])
            for t in range(NTL):
                bk = t % 2
                for k in range(8):
                    P.op("pe", lambda e, k=k, t=t, bk=bk: e.matmul(ps[:, bk, 0:16], lhsT=hn[:, k, t * 128:(t + 1) * 128], rhs=wdt[:, k, :], start=(k == 0), stop=(k == 7)),
                         reads=["hn", "wdt"], writes=[PK(bk)])
                P.op("dve", lambda e, bk=bk: e.tensor_tensor(out=tmp16, in0=ps[:, bk, 0:16], in1=rvs[:, DTB:DTB + 16], op=ALU.add), reads=[PK(bk), "rvs"], writes=["tmp16"])
                P.op("act", lambda e: e.activation(out=tmp16, in_=tmp16, func=AF.Exp), reads=["tmp16"], writes=["tmp16"])
                P.op("act", lambda e, t=t: e.activation(out=dt_tok[:, t, :], in_=tmp16, func=AF.Ln, bias=1.0, scale=1.0), reads=["tmp16"], writes=["dt_tok"])
                P.op("dve", lambda e, t=t: e.tensor_tensor(out=dA_tok[:, t, :], in0=dt_tok[:, t, :], in1=Abc, op=ALU.mult), reads=["dt_tok", "Abc"], writes=["dA_tok"])
            xin_r = Rot("xin", [128, 16, 128], F32, 1)
            xs_sb = AR.alloc([128, 1024], F32)
            X_bf = AR.alloc([128, 1024], BF16)
            Xd_bf = AR.alloc([128, 1024], BF16)
            Btm_bf = AR.alloc([128, 512], BF16)
            BC_bf = AR.alloc([128, 8, 128], BF16)
            R = AR.alloc([128, 16, 128], F32)
            Lm = R
            M_bf = AR.alloc([128, 16, 128], BF16)
            cb_sb = AR.alloc([128, 512], F32)
            t1 = AR.alloc([128, 1024], F32)
            sz = AR.alloc([128, 1024], F32)
            ST = AR.alloc([128, 1024], F32)
            ST_bf = AR.alloc([128, 1024], BF16)
            sm = AR.alloc([128, 128], F32)
            yast_r = Rot("yast", [128, 8, 128], BF16, 1)
            Acol, tot, dec, expA, cdbc, ss4, rg = sm[:, 0:16], sm[:, 16:32], sm[:, 32:48], sm[:, 48:64], sm[:, 64:80], sm[:, 80:84], sm[:, 84:88]
            Cm_r = Rot("Cm", [128, 16, 128], BF16, 2)
            Ysel = AR.alloc([128, 16, 16], F32)
            S0f_r = Rot("S0f", [128, 1024], F32, 2)
            S0b_r = Rot("S0b", [128, 256], BF16, 4)
            Bm_r = Rot("Bm", [128, 512], BF16, 2)
            cd_all = AR.alloc([128, 16, 16], F32)
            for cm_ in Cm_r.bufs:
                P.op("dve", lambda e, cm_=cm_: e.memset(cm_, 0.0), writes=[("Cm", 0), ("Cm", 1)])
            P.op("dve", lambda e: e.memset(ST, 0.0), writes=["ST"])
            P.op("dve", lambda e: e.memset(ST_bf, 0.0), writes=["ST_bf"])

            def v16(ap):
                return ap.rearrange("p (h q) -> p h q", h=16)

            for t in (DBG["s6_tiles"] if DBG["s6_tiles"] is not None else range(NTL)):
                is_s = (t == NTL - 1)
                triM = cs[:, TRIS:TRIS + 128] if is_s else cs[:, TRI:TRI + 128]
                negM = cs[:, NEGS:NEGS + 128] if is_s else cs[:, NEGP:NEGP + 128]
                blkM = cs[:, BLKS:BLKS + 128] if is_s else ones
                tsl = slice(t * 128, (t + 1) * 128)
                xin, xink = xin_r.next()
                for q4 in range(4):
                    P.dma("sp", xin[:, 4 * q4:4 * q4 + 4, :], xbcT.rearrange("k p t -> p k t")[:, 4 * q4:4 * q4 + 4, tsl], reads=["xbcT"], writes=[xink])
                for k in range(8):
                    P.op("pe", lambda e, k=k, xin=xin: e.transpose(out=ps[:, k // 4, (k % 4) * 128:(k % 4) * 128 + 128], in_=xin[:, k, :], identity=ident),
                         reads=[xink, "cs"], writes=[PK(0), PK(1)])
                for g in range(4):
                    P.op("pe", lambda e, g=g, xin=xin: e.transpose(out=ps[:, 2, g * 128:(g + 1) * 128], in_=xin[:, 8 + g, :], identity=ident),
                         reads=[xink, "cs"], writes=[PK(2)])
                if DBG["s6_sub"] <= 1:
                    continue
                P.op("act", lambda e: e.activation(out=xs_sb, in_=ps2(0), func=AF.Identity), reads=[PK(0), PK(1)], writes=["xs_sb"])
                if DBG["s6_sub"] <= 2:
                    continue
                P.op("dve", lambda e, t=t: e.tensor_tensor(out=v16(X_bf), in0=v16(xs_sb), in1=bc(dt_tok[:, t, :].unsqueeze(2), [128, 16, 64]), op=ALU.mult),
                     reads=["xs_sb", "dt_tok"], writes=["X_bf"])
                if DBG["s6_sub"] <= 3:
                    continue
                P.op("act", lambda e: e.activation(out=Btm_bf, in_=psb(2), func=AF.Identity), reads=[PK(2)], writes=["Btm_bf"])
                if DBG["s6_sub"] <= 4:
                    continue
                P.op("dve", lambda e, xin=xin: e.tensor_copy(out=BC_bf, in_=xin[:, 8:16, :]), reads=[xink], writes=["BC_bf"])
                if DBG["s6_stop"] <= 1:
                    continue
                P.op("pe", lambda e, t=t, triM=triM: e.matmul(ps[:, 3, 0:16], lhsT=triM, rhs=dA_tok[:, t, :], start=True, stop=True), reads=["dA_tok", "cs"], writes=[PK(3)])
                P.op("pe", lambda e, t=t, blkM=blkM: e.matmul(ps[:, 3, 16:32], lhsT=blkM, rhs=dA_tok[:, t, :], start=True, stop=True), reads=["dA_tok", "cs"], writes=[PK(3)])
                P.op("dve", lambda e: e.tensor_copy(out=sm[:, 0:32], in_=ps[:, 3, 0:32]), reads=[PK(3)], writes=["sm_a"])
                P.op("dve", lambda e, t=t, triM=triM: e.tensor_tensor(out=R, in0=bc(triM.unsqueeze(1), [128, 16, 128]), in1=bc(dA_tok[:, t, :].unsqueeze(2), [128, 16, 128]), op=ALU.mult),
                     reads=["cs", "dA_tok"], writes=["R"])
                for i in range(4):
                    P.op("pe", lambda e, i=i: e.matmul(psb(4 + i), lhsT=ones, rhs=R[:, 4 * i:4 * i + 4, :].rearrange("p a b -> p (a b)"), start=True, stop=True),
                         reads=["R", "cs"], writes=[PK(4 + i)])
                for i in range(4):
                    P.op("dve", lambda e, i=i: e.tensor_tensor(out=Lm[:, 4 * i:4 * i + 4, :], in0=ps[:, 4 + i, :].rearrange("p (a b) -> p a b", a=4),
                                                               in1=bc(Acol[:, 4 * i:4 * i + 4].unsqueeze(2), [128, 4, 128]), op=ALU.subtract),
                         reads=[PK(4 + i), "sm_a", "R"], writes=["R", "Lm"])
                P.op("dve", lambda e, negM=negM: e.tensor_tensor(out=Lm, in0=Lm, in1=bc(negM.unsqueeze(1), [128, 16, 128]), op=ALU.add), reads=["Lm", "R", "cs"], writes=["Lm", "R"])
                P.op("act", lambda e: e.activation(out=Lm, in_=Lm, func=AF.Exp), reads=["Lm", "R"], writes=["Lm", "R"])
                if DBG["s6_stop"] <= 2:
                    continue
                for g in range(4):
                    P.op("pe", lambda e, g=g: e.matmul(ps[:, 2, g * 128:(g + 1) * 128], lhsT=BC_bf[:, g, :], rhs=BC_bf[:, 4 + g, :], start=True, stop=True),
                         reads=["BC_bf", "Btm_bf"], writes=[PK(2)])
                P.op("act", lambda e: e.activation(out=cb_sb, in_=psb(2), func=AF.Identity), reads=[PK(2)], writes=["cb_sb"])
                P.op("dve", lambda e: e.tensor_tensor(out=M_bf.rearrange("p (g r) l -> p g r l", g=4), in0=Lm.rearrange("p (g r) l -> p g r l", g=4),
                                                      in1=bc(cb_sb.rearrange("p (g l) -> p g l", g=4).unsqueeze(2), [128, 4, 4, 128]), op=ALU.mult),
                     reads=["Lm", "R", "cb_sb"], writes=["M_bf"])
                if DBG["s6_stop"] <= 3:
                    continue
                for hh in range(16):
                    P.op("pe", lambda e, hh=hh: e.matmul(ps2(0)[:, hh * 64:(hh + 1) * 64], lhsT=M_bf[:, hh, :], rhs=X_bf[:, hh * 64:(hh + 1) * 64], start=True, stop=True),
                         reads=["M_bf", "X_bf", "xs_sb"], writes=[PK(0), PK(1)])
                if DBG["s6_stop"] <= 4:
                    continue
                if not is_s:
                    for g in range(4):
                        P.op("pe", lambda e, g=g: e.matmul(ps2(4)[:, g * 256:(g + 1) * 256], lhsT=BC_bf[:, 4 + g, :], rhs=ST_bf[:, g * 256:(g + 1) * 256], start=True, stop=True),
                             reads=["BC_bf", "ST_bf", "Lm"], writes=[PK(4), PK(5)])
                else:
                    for g in range(4):
                        Cmg, Cmk = Cm_r.next()
                        for sq in range(16):
                            P.op("dve", lambda e, g=g, sq=sq, Cmg=Cmg: e.tensor_copy(out=Cmg[:, sq, 8 * sq:8 * sq + 8], in_=BC_bf[:, 4 + g, 8 * sq:8 * sq + 8]),
                                 reads=["BC_bf", Cmk], writes=[Cmk])
                        for sq in range(16):
                            S0b, S0bk = S0b_r.next()
                            P.dma("pool", S0b, ssd_state[l, sq][:, g * 256:(g + 1) * 256], writes=[S0bk])
                            P.op("pe", lambda e, g=g, sq=sq, Cmg=Cmg, S0b=S0b: e.matmul(ps2(4)[:, g * 256:(g + 1) * 256], lhsT=Cmg[:, sq, :], rhs=S0b,
                                                                                        start=(sq == 0), stop=(sq == 15)),
                                 reads=[Cmk, S0bk, "Lm"], writes=[PK(4), PK(5)])
                if DBG["s6_stop"] <= 5:
                    continue
                P.op("act", lambda e: e.activation(out=expA, in_=Acol, func=AF.Exp), reads=["sm_a"], writes=["sm_e"])
                for i in range(2):
                    P.op("dve", lambda e, i=i: e.tensor_tensor(out=v16(t1)[:, 8 * i:8 * i + 8, :], in0=ps[:, 4 + i, :].rearrange("p (a b) -> p a b", a=8),
                                                               in1=bc(expA[:, 8 * i:8 * i + 8].unsqueeze(2), [128, 8, 64]), op=ALU.mult),
                         reads=[PK(4 + i), "sm_e"], writes=["t1"])
                P.op("dve", lambda e: e.tensor_tensor(out=t1, in0=t1, in1=ps2(0), op=ALU.add), reads=["t1", PK(0), PK(1)], writes=["t1"])
                P.op("dve", lambda e: e.tensor_tensor(out=v16(sz), in0=v16(xs_sb), in1=bc(rvs[:, DSK:DSK + 16].unsqueeze(2), [128, 16, 64]), op=ALU.mult),
                     reads=["xs_sb", "rvs"], writes=["sz"])
                P.op("dve", lambda e: e.tensor_tensor(out=t1, in0=t1, in1=sz, op=ALU.add), reads=["t1", "sz"], writes=["t1"])
                if DBG["s6_stop"] <= 6:
                    continue
                for hf in range(2):
                    for k in range(8):
                        P.op("pe", lambda e, hf=hf, k=k, tsl=tsl: e.matmul(psb(6 + hf), lhsT=hn[:, k, tsl], rhs=wz[:, k, hf * 512:(hf + 1) * 512], start=(k == 0), stop=(k == 7)),
                             reads=["hn", "wz", "Lm"], writes=[PK(6 + hf)])
                P.op("act", lambda e: e.activation(out=sz, in_=ps2(6), func=AF.Silu), reads=[PK(6), PK(7), "t1"], writes=["sz"])
                P.op("dve", lambda e: e.tensor_tensor(out=t1, in0=t1, in1=sz, op=ALU.mult), reads=["t1", "sz"], writes=["t1"])
                if DBG["s6_stop"] <= 7:
                    continue
                P.op("dve", lambda e: e.memset(ss4, 0.0), writes=["ss4"])
                for g in range(4):
                    P.op("act", lambda e, g=g: e.activation(out=sz[:, g * 256:(g + 1) * 256], in_=t1[:, g * 256:(g + 1) * 256], func=AF.Square, accum_out=ss4[:, g:g + 1]),
                         reads=["t1", "ss4"], writes=["sz", "ss4"])
                P.op("act", lambda e: e.activation(out=rg, in_=ss4, func=AF.Sqrt, bias=EPS, scale=1.0 / 256), reads=["ss4"], writes=["rg"])
                P.op("dve", lambda e: e.reciprocal(out=rg, in_=rg), reads=["rg"], writes=["rg"])
                P.op("dve", lambda e: e.tensor_tensor(out=t1.rearrange("p (g c) -> p g c", g=4), in0=t1.rearrange("p (g c) -> p g c", g=4),
                                                      in1=bc(rg.unsqueeze(2), [128, 4, 256]), op=ALU.mult), reads=["t1", "rg"], writes=["t1"])
                P.op("dve", lambda e: e.tensor_tensor(out=t1, in0=t1, in1=rvn, op=ALU.mult), reads=["t1", "rvn"], writes=["t1"])
                for k in range(8):
                    P.op("pe", lambda e, k=k: e.transpose(out=ps[:, k // 4, (k % 4) * 128:(k % 4) * 128 + 128], in_=t1[:, k * 128:(k + 1) * 128], identity=ident),
                         reads=["t1", "cs"], writes=[PK(0), PK(1)])
                if DBG["s6_stop"] <= 8:
                    continue
                yast, yak = yast_r.next()
                P.op("act", lambda e, yast=yast: e.activation(out=yast, in_=ps2(0).rearrange("p (k t) -> p k t", k=8), func=AF.Identity),
                     reads=[PK(0), PK(1)], writes=[yak])
                P.dma("sp", yaT_d.rearrange("k p t -> p k t")[:, :, tsl], yast, reads=[yak], writes=["yaT_d"])
                if DBG["s6_stop"] <= 9:
                    continue
                P.op("dve", lambda e: e.tensor_tensor(out=dec, in0=tot, in1=Acol, op=ALU.subtract), reads=["sm_a"], writes=["sm_d"])
                P.op("act", lambda e: e.activation(out=dec, in_=dec, func=AF.Exp), reads=["sm_d"], writes=["sm_d"])
                P.op("dve", lambda e: e.tensor_tensor(out=v16(Xd_bf), in0=v16(X_bf), in1=bc(dec.unsqueeze(2), [128, 16, 64]), op=ALU.mult),
                     reads=["X_bf", "sm_d"], writes=["Xd_bf"])
                if not is_s:
                    P.op("act", lambda e: e.activation(out=cdbc, in_=tot, func=AF.Exp), reads=["sm_a"], writes=["sm_c"])
                    for g in range(4):
                        P.op("pe", lambda e, g=g: e.matmul(ps2(4)[:, g * 256:(g + 1) * 256], lhsT=Btm_bf[:, g * 128:(g + 1) * 128], rhs=Xd_bf[:, g * 256:(g + 1) * 256], start=True, stop=True),
                             reads=["Btm_bf", "Xd_bf", "t1"], writes=[PK(4), PK(5)])
                    P.op("dve", lambda e: e.tensor_tensor(out=v16(ST), in0=v16(ST), in1=bc(cdbc.unsqueeze(2), [128, 16, 64]), op=ALU.mult), reads=["ST", "sm_c"], writes=["ST"])
                    P.op("dve", lambda e: e.tensor_tensor(out=ST, in0=ST, in1=ps2(4), op=ALU.add), reads=["ST", PK(4), PK(5)], writes=["ST"])
                    P.op("act", lambda e: e.activation(out=ST_bf, in_=ST, func=AF.Identity), reads=["ST"], writes=["ST_bf"])
                    if t == NTL - 2:
                        P.dma("sp", ssd_out[l, 0], ST, reads=["ST"])
                else:
                    P.op("dve", lambda e, t=t: e.tensor_tensor(out=Ysel, in0=bc(dA_tok[:, t, :].unsqueeze(1), [128, 16, 16]), in1=bc(cs[:, SEQM:SEQM + 16].unsqueeze(2), [128, 16, 16]), op=ALU.mult),
                         reads=["dA_tok", "cs"], writes=["Ysel"])
                    P.op("pe", lambda e: e.matmul(ps[:, 3, 0:256], lhsT=ones, rhs=Ysel.rearrange("p s h -> p (s h)"), start=True, stop=True),
                         reads=["Ysel", "cs"], writes=[PK(3)])
                    P.op("act", lambda e: e.activation(out=cd_all, in_=ps[:, 3, 0:256].rearrange("p (s h) -> p s h", s=16), func=AF.Exp), reads=[PK(3)], writes=["cd_all"])
                    for sq in range(16):
                        Bm, Bmk = Bm_r.next()
                        S0f, S0fk = S0f_r.next()
                        pb = 4 + 2 * (sq % 2)
                        P.op("dve", lambda e, sq=sq, Bm=Bm: e.tensor_scalar(out=Bm, in0=Btm_bf, scalar1=cs[:, SEQM + sq:SEQM + sq + 1], scalar2=None, op0=ALU.mult),
                             reads=["Btm_bf", "cs"], writes=[Bmk])
                        for g in range(4):
                            P.op("pe", lambda e, g=g, Bm=Bm, pb=pb: e.matmul(ps2(pb)[:, g * 256:(g + 1) * 256], lhsT=Bm[:, g * 128:(g + 1) * 128], rhs=Xd_bf[:, g * 256:(g + 1) * 256], start=True, stop=True),
                                 reads=[Bmk, "Xd_bf", "t1"], writes=[PK(pb), PK(pb + 1)])
                        P.dma("sp", S0f, ssd_state[l, sq], writes=[S0fk])
                        P.op("dve", lambda e, sq=sq, S0f=S0f: e.tensor_tensor(out=v16(S0f), in0=v16(S0f), in1=bc(cd_all[:, sq, :].unsqueeze(2), [128, 16, 64]), op=ALU.mult),
                             reads=[S0fk, "cd_all"], writes=[S0fk])
                        P.op("dve", lambda e, S0f=S0f, pb=pb: e.tensor_tensor(out=S0f, in0=S0f, in1=ps2(pb), op=ALU.add), reads=[S0fk, PK(pb), PK(pb + 1)], writes=[S0fk])
                        P.dma("sp", ssd_out[l, 1 + sq], S0f, reads=[S0fk])
            P.barrier()

        if want("S7"):
            AR.reset(base_off)
            ya_all = AR.alloc([128, 8, T], BF16)
            yb_all = AR.alloc([128, 8, T], BF16)
            for k in range(8):
                P.dma("sp", ya_all[:, k, :], yaT_d[k], reads=["yaT_d"], writes=["ya_all"])
                P.dma("act", yb_all[:, k, :], ybT_d[k], reads=["ybT_d"], writes=["yb_all"])
            wr = Rot("w", [128, 8, 128], BF16, 8)
            sga_r = Rot("sga", [128, 512], F32, 2)
            sgb_r = Rot("sgb", [128, 512], F32, 2)
            mixc_r = Rot("mixc", [128, 512], BF16, 3)
            for j in range(8):
                w1, k1_ = load_w(wr, w_so[l], j * 128, 128)
                w2, k2_ = load_w(wr, w_cf[l], j * 128, 128)
                w3, k3_ = load_w(wr, w_in[l], COL_GLU + j * 128, 128)
                w4, k4_ = load_w(wr, w_in[l], COL_GLU + 1024 + j * 128, 128)
                for ti, (c0, n) in enumerate(NTILES):
                    pb = 4 * (ti % 2)
                    proj(pb, w1, k1_, lambda k: ya_all[:, k, c0:c0 + n], n, ["ya_all"])
                    proj(pb + 1, w2, k2_, lambda k: yb_all[:, k, c0:c0 + n], n, ["yb_all"])
                    proj(pb + 2, w3, k3_, lambda k: hn[:, k, c0:c0 + n], n, ["hn"])
                    proj(pb + 3, w4, k4_, lambda k: hn[:, k, c0:c0 + n], n, ["hn"])
                    sga, sgak = sga_r.next()
                    sgb, sgbk = sgb_r.next()
                    mixc, mixk = mixc_r.next()
                    P.op("act", lambda e: e.activation(out=sga[:, 0:n], in_=psb(pb + 2, n), func=AF.Sigmoid), reads=[PK(pb + 2)], writes=[sgak])
                    P.op("act", lambda e: e.activation(out=sgb[:, 0:n], in_=psb(pb + 3, n), func=AF.Sigmoid), reads=[PK(pb + 3)], writes=[sgbk])
                    P.op("dve", lambda e: e.tensor_tensor(out=sga[:, 0:n], in0=psb(pb, n), in1=sga[:, 0:n], op=ALU.mult), reads=[PK(pb), sgak], writes=[sgak])
                    P.op("dve", lambda e: e.tensor_tensor(out=sgb[:, 0:n], in0=psb(pb + 1, n), in1=sgb[:, 0:n], op=ALU.mult), reads=[PK(pb + 1), sgbk], writes=[sgbk])
                    P.op("dve", lambda e: e.tensor_tensor(out=mixc[:, 0:n], in0=sga[:, 0:n], in1=sgb[:, 0:n], op=ALU.add), reads=[sgak, sgbk], writes=[mixk])
                    P.dma("sp", mixT_d[j][:, c0:c0 + n], mixc[:, 0:n], reads=[mixk], writes=["mixT_d"])
            P.barrier()
            AR.reset(base_off)
            wo_all = [AR.alloc([128, 8, 128], BF16) for _ in range(8)]
            for j in range(8):
                P.dma("pool", wo_all[j], w_o[l].rearrange("(k p) c -> p k c", p=128)[:, :, j * 128:(j + 1) * 128], writes=[("wo", j)])
            mixt_r = Rot("mixt", [128, 8, 512], BF16, 2)
            for ti, (c0, n) in enumerate(NTILES):
                mixt, mixtk = mixt_r.next()
                P.dma("sp", mixt[:, :, 0:n], mixT_d.rearrange("k p t -> p k t")[:, :, c0:c0 + n], reads=["mixT_d"], writes=[mixtk])
                for j in range(8):
                    pb = j % 8
                    proj(pb, wo_all[j], ("wo", j), lambda k: mixt[:, k, 0:n], n, [mixtk])
                    P.op("dve", lambda e: e.tensor_tensor(out=h[:, j, c0:c0 + n], in0=h[:, j, c0:c0 + n], in1=psb(pb, n), op=ALU.add),
                         reads=["h", PK(pb)], writes=["h"])
            P.barrier()
        dump_h(2 * l)

        pump_state["auto"] = 0
        if want("PEER"):
            P.barrier()
            for k in range(8):
                P.dma("sp" if k % 2 == 0 else "act", hD[k], h[:, k, :], reads=["h"], writes=["hD"])
            P.barrier()
            for b in range(16):
                P.dma("pool", wqb[l, b].rearrange("p (k c) -> p k c", k=8), wq[l].rearrange("(k p) c -> p k c", p=128)[:, :, b * 128:(b + 1) * 128], writes=[("wqb", l, b)])
            AR.reset()
            AR2 = Arena(h.rearrange("p k t -> p (k t)").bitcast(BF16), cap=8 * T * 2)
            hb = [AR2.alloc([128, 8, 256], F32) for _ in range(2)]
            Gs = [AR.alloc([128, 256, 64], BF16), AR2.alloc([128, 256, 64], BF16)]
            hsq = AR.alloc([128, 8, 256], F32)
            rs = AR.alloc([128, 256], F32)
            xnS = [AR.alloc([128, 8, 256], BF16) for _ in range(2)]
            keys_sb = AR.alloc([128, 16, 128], BF16)
            P.dma("pool", keys_sb, keysT[l].rearrange("b d k -> d b k"), writes=["keys_sb"])
            pump(512, layer=l)
            if l + 1 < n_layers:
                pump(512, layer=l + 1)
            wr = Rot("w", [128, 8, 128], BF16, 2)
            qT = AR.alloc([128, 16, 256], BF16)
            S_sb = AR.alloc([128, 16, 128], F32)
            V = AR.alloc([128, 16, 16], F32)
            I = AR.alloc([128, 16, 16], U32)
            If = AR.alloc([128, 16, 16], F32)
            cand = S_sb.rearrange("p (h a) (b c) -> p h (a b) c", h=8, c=16)
            eq = hsq.rearrange("p a (b c) -> p a b c", b=16)
            Tv = AR.alloc([128, 8, 16], F32)
            POS = AR.alloc([128, 8, 16], U32)
            PA = AR.alloc([128, 8, 16], U32)
            PB = AR.alloc([128, 8, 16], U32)
            Af = AR.alloc([128, 8, 16], F32)
            Bf = AR.alloc([128, 8, 16], F32)
            E = AR.alloc([128, 8, 16], F32)
            Z = AR.alloc([128, 8], F32)
            gate = AR.alloc([128, 8, 16], F32)
            i0 = AR.alloc([128, 8, 16], F32)
            i1 = AR.alloc([128, 8, 16], F32)
            idxS = [[AR.alloc([128, 256], BF16) for _ in range(3)] for _ in range(2)]
            io128b = AR.alloc([128, 128], BF16)
            P.op("dve", lambda e: e.tensor_copy(out=io128b, in_=cs[:, IO128:IO128 + 128]), reads=["cs"], writes=["io128b"])
            A_r = Rot("A", [128, 32, 128], BF16, 2)
            B_r = Rot("B", [128, 32, 64], BF16, 2)
            ub_r = Rot("ub", [128, 8, 128], BF16, 8, arena=AR2)
            vb_r = Rot("vb", [128, 1024], BF16, 8)
            ge_r = Rot("ge", [128, 256], F32, 4)
            act_r = Rot("actb", [128, 256], BF16, 5)
            io16 = cs[:, IO16:IO16 + 16]
            blocks = [(c, 256) for c in range(0, TP, 256)] + [(TP, 128)]
            V4 = V.rearrange("p (h j) a -> p h j a", j=2)
            If4 = If.rearrange("p (h j) a -> p h j a", j=2)
            SK16 = [("S", b_) for b_ in range(16)]
            P2 = [PK(2)]

            def tile_phase(bs, c0, n):
                xn_ = xnS[bs]
                i0T_, i1T_, gT_ = idxS[bs]
                xk, ik, hk = ("xn", bs), ("idxT", bs), ("hb", bs)
                hb_ = hb[bs]
                P.dma("sp", hb_[:, :, 0:n], hD.rearrange("k p t -> p k t")[:, :, c0:c0 + n], reads=["hD"], writes=[hk])
                s3 = hb_[:, :, 0:n]
                P.op("act", lambda e: e.activation(out=hsq[:, :, 0:n], in_=s3, func=AF.Square), reads=[hk], writes=["hsq"])
                for k in range(8):
                    P.op("pe", lambda e: e.matmul(psb(2, n), lhsT=ones, rhs=hsq[:, k, 0:n], start=(k == 0), stop=(k == 7)), reads=["hsq", "cs"], writes=P2)
                P.op("act", lambda e: e.activation(out=rs[:, 0:n], in_=psb(2, n), func=AF.Sqrt, bias=EPS, scale=1.0 / 1024), reads=P2, writes=["rs"])
                P.op("dve", lambda e: e.reciprocal(out=rs[:, 0:n], in_=rs[:, 0:n]), reads=["rs"], writes=["rs"])
                yield
                for k in range(8):
                    P.op("dve", lambda e: e.scalar_tensor_tensor(out=xn_[:, k, 0:n], in0=hb_[:, k, 0:n], scalar=cv[:, l, G_FFN + k:G_FFN + k + 1], in1=rs[:, 0:n], op0=ALU.mult, op1=ALU.mult),
                         reads=[hk, "rs", "cv"], writes=[xk])
                    if k % 2 == 1:
                        yield
                for b in range(16):
                    wb, wbk = wr.next()
                    P.dma("sp", wb, wqb[l, b].rearrange("p (k c) -> p k c", k=8), reads=[("wqb", l, b)], writes=[wbk])
                    hf = 2 + b % 2
                    pq = ps[:, hf, 0:n]
                    for k in range(8):
                        P.op("pe", lambda e: e.matmul(pq, lhsT=wb[:, k, :], rhs=xn_[:, k, 0:n], start=(k == 0), stop=(k == 7)), reads=[wbk, xk], writes=[PK(hf)])
                    P.op("act", lambda e: e.activation(out=qT[:, b, 0:n], in_=pq, func=AF.Identity), reads=[PK(hf)], writes=[("qT", b)])
                    yield
                for ti in range(n // 128):
                    tc0 = ti * 128
                    for p_ in range(4):
                        for bb in range(4):
                            b = 4 * p_ + bb
                            P.op("pe", lambda e: e.matmul(ps[:, 2 + p_ % 2, bb * 128:bb * 128 + 128], lhsT=qT[:, b, tc0:tc0 + 128], rhs=keys_sb[:, b, :], start=True, stop=True),
                                 reads=[("qT", b), "keys_sb"], writes=[PK(2 + p_ % 2)])
                        P.op("act", lambda e: e.activation(out=S_sb[:, 4 * p_:4 * p_ + 4, :], in_=ps[:, 2 + p_ % 2, :].rearrange("p (b c) -> p b c", c=128), func=AF.Identity),
                             reads=[PK(2 + p_ % 2)], writes=[("S", b_) for b_ in range(4 * p_, 4 * p_ + 4)])
                        yield
                    for stg in range(5):
                        for b in range(16):
                            sb_ = S_sb[:, b, :]
                            if stg == 0:
                                P.op("dve", lambda e: e.max(out=V[:, b, 0:8], in_=sb_), reads=[("S", b)], writes=[("V", b)])
                            elif stg == 1:
                                P.op("dve", lambda e: e.max_index(out=I[:, b, 0:8], in_max=V[:, b, 0:8], in_values=sb_), reads=[("S", b), ("V", b)], writes=[("I", b)])
                            elif stg == 2:
                                P.op("dve", lambda e: e.match_replace(out=sb_, in_to_replace=V[:, b, 0:8], in_values=sb_, imm_value=-1e30), reads=[("S", b), ("V", b)], writes=[("S", b)])
                            elif stg == 3:
                                P.op("dve", lambda e: e.max(out=V[:, b, 8:16], in_=sb_), reads=[("S", b)], writes=[("V2", b)])
                            else:
                                P.op("dve", lambda e: e.max_index(out=I[:, b, 8:16], in_max=V[:, b, 8:16], in_values=sb_), reads=[("S", b), ("V2", b)], writes=[("I2", b)])
                            if b % 4 == 3:
                                yield
                    P.op("dve", lambda e: e.tensor_copy(out=If, in_=I), reads=[("I", b_) for b_ in range(16)] + [("I2", b_) for b_ in range(16)], writes=["If"])
                    P.op("dve", lambda e: e.tensor_tensor(out=cand, in0=bc(V4[:, :, 0, :].unsqueeze(3), [128, 8, 16, 16]), in1=bc(V4[:, :, 1, :].unsqueeze(2), [128, 8, 16, 16]), op=ALU.add),
                         reads=[("V", b_) for b_ in range(16)] + [("V2", b_) for b_ in range(16)], writes=[("cand", h_) for h_ in range(8)] + SK16)
                    yield
                    for stg in range(5):
                        for hh in range(8):
                            ch = cand[:, hh].rearrange("p a b -> p (a b)")
                            if stg == 0:
                                P.op("dve", lambda e: e.max(out=Tv[:, hh, 0:8], in_=ch), reads=[("cand", hh)], writes=[("Tv", hh)])
                            elif stg == 1:
                                P.op("dve", lambda e: e.max_index(out=POS[:, hh, 0:8], in_max=Tv[:, hh, 0:8], in_values=ch), reads=[("cand", hh), ("Tv", hh)], writes=[("POS", hh)])
                            elif stg == 2:
                                P.op("dve", lambda e: e.match_replace(out=ch, in_to_replace=Tv[:, hh, 0:8], in_values=ch, imm_value=-1e30), reads=[("cand", hh), ("Tv", hh)], writes=[("cand", hh)])
                            elif stg == 3:
                                P.op("dve", lambda e: e.max(out=Tv[:, hh, 8:16], in_=ch), reads=[("cand", hh)], writes=[("Tv2", hh)])
                            else:
                                P.op("dve", lambda e: e.max_index(out=POS[:, hh, 8:16], in_max=Tv[:, hh, 8:16], in_values=ch), reads=[("cand", hh), ("Tv2", hh)], writes=[("POS2", hh)])
                            if hh % 4 == 3:
                                yield
                    TVK = [("Tv", h_) for h_ in range(8)] + [("Tv2", h_) for h_ in range(8)]
                    PSK = [("POS", h_) for h_ in range(8)] + [("POS2", h_) for h_ in range(8)]
                    P.op("dve", lambda e: e.tensor_tensor(out=E, in0=Tv, in1=bc(Tv[:, :, 0:1], [128, 8, 16]), op=ALU.subtract), reads=TVK, writes=["E"])
                    P.op("act", lambda e: e.activation(out=E, in_=E, func=AF.Exp), reads=["E"], writes=["E"])
                    P.op("dve", lambda e: e.tensor_single_scalar(out=PB, in_=POS, scalar=15, op=ALU.bitwise_and), reads=PSK, writes=["PB"])
                    P.op("dve", lambda e: e.tensor_single_scalar(out=PA, in_=POS, scalar=4, op=ALU.logical_shift_right), reads=PSK, writes=["PA"])
                    yield
                    P.op("dve", lambda e: e.reduce_sum(out=Z, in_=E, axis=AX.X), reads=["E"], writes=["Z"])
                    P.op("dve", lambda e: e.tensor_copy(out=Af, in_=PA), reads=["PA"], writes=["Af"])
                    P.op("dve", lambda e: e.reciprocal(out=Z, in_=Z), reads=["Z"], writes=["Z"])
                    P.op("dve", lambda e: e.tensor_copy(out=Bf, in_=PB), reads=["PB"], writes=["Bf"])
                    P.op("dve", lambda e: e.tensor_tensor(out=gate, in0=E, in1=bc(Z.unsqueeze(2), [128, 8, 16]), op=ALU.mult), reads=["E", "Z"], writes=["gate"])
                    yield
                    for (xf, jj, dst, dk) in ((Af, 0, i0, "i0"), (Bf, 1, i1, "i1")):
                        P.op("dve", lambda e: e.tensor_tensor(out=eq, in0=bc(io16.unsqueeze(1).unsqueeze(1), [128, 8, 16, 16]), in1=bc(xf.unsqueeze(3), [128, 8, 16, 16]), op=ALU.is_equal),
                             reads=["cs", "Af", "Bf"], writes=["hsq"])
                        yield
                        P.op("dve", lambda e: e.tensor_tensor(out=eq, in0=eq, in1=bc(If4[:, :, jj, :].unsqueeze(2), [128, 8, 16, 16]), op=ALU.mult), reads=["hsq", "If"], writes=["hsq"])
                        yield
                        P.op("dve", lambda e: e.reduce_sum(out=dst, in_=eq, axis=AX.X), reads=["hsq"], writes=[dk])
                        yield
                    for q_, (src, dstT, sk) in enumerate(((i0, i0T_, "i0"), (i1, i1T_, "i1"), (gate, gT_, "gate"))):
                        P.op("pe", lambda e: e.transpose(out=ps[:, 2, q_ * 128:(q_ + 1) * 128], in_=src.rearrange("p h r -> p (h r)"), identity=ident),
                             reads=[sk, "cs"], writes=P2)
                        P.op("act", lambda e: e.activation(out=dstT[:, tc0:tc0 + 128], in_=ps[:, 2, q_ * 128:(q_ + 1) * 128], func=AF.Identity),
                             reads=P2, writes=[ik])
                    yield

            def g_build(bs, n, half):
                i0T, i1T, gT = idxS[bs]
                ik = ("idxT", bs)
                G = Gs[half]
                gk = ("G", half)

                def pe_part(A_, Ak, B_, Bk, s0):
                    for q_ in range(4):
                        pg = 2 + q_ % 2
                        for tk in range(8):
                            tok = q_ * 8 + tk
                            P.op("pe", lambda e: e.matmul(ps[:, pg, tk * 64:(tk + 1) * 64], lhsT=A_[:, tok, :], rhs=B_[:, tok, :], start=True, stop=True),
                                 reads=[Ak, Bk], writes=[PK(pg)])
                        gdst = G[:, s0 + q_ * 8:s0 + q_ * 8 + 8, :]
                        P.op("act", lambda e: e.activation(out=gdst, in_=ps[:, pg, :].rearrange("p (a b) -> p a b", a=8), func=AF.Identity),
                             reads=[PK(pg)], writes=[gk])
                        yield

                prev = None
                for sub in range(n // 32):
                    s0 = sub * 32
                    A_, Ak = A_r.next()
                    B_, Bk = B_r.next()
                    P.op("dve", lambda e: e.tensor_tensor(out=A_, in0=bc(io128b.unsqueeze(1), [128, 32, 128]), in1=bc(i0T[:, s0:s0 + 32].unsqueeze(2), [128, 32, 128]), op=ALU.is_equal),
                         reads=["io128b", ik], writes=[Ak])
                    yield
                    P.op("dve", lambda e: e.tensor_tensor(out=B_, in0=bc(io128b[:, half * 64:(half + 1) * 64].unsqueeze(1), [128, 32, 64]),
                                                          in1=bc(i1T[:, s0:s0 + 32].unsqueeze(2), [128, 32, 64]), op=ALU.is_equal),
                         reads=["io128b", ik], writes=[Bk])
                    yield
                    P.op("dve", lambda e: e.tensor_tensor(out=B_, in0=B_, in1=bc(gT[:, s0:s0 + 32].unsqueeze(2), [128, 32, 64]), op=ALU.mult),
                         reads=[Bk, ik], writes=[Bk])
                    yield
                    if prev is not None:
                        for _ in pe_part(*prev):
                            yield
                    prev = (A_, Ak, B_, Bk, s0)
                for _ in pe_part(*prev):
                    yield

            def chain(*gens):
                for g in gens:
                    if g is not None:
                        for _ in g:
                            yield

            def pump_gen(g, k):
                if g is None:
                    return
                for _ in range(k):
                    if next(g, "END") == "END":
                        return

            for _ in chain(tile_phase(0, *blocks[0]), g_build(0, blocks[0][1], 0)):
                pass
            for bi, (c0, n) in enumerate(blocks):
                bs = bi % 2
                xn = xnS[bs]
                xk, hk = ("xn", bs), ("hb", bs)
                has_next = bi + 1 < len(blocks)
                gb1 = g_build(bs, n, 1)
                tpn = tile_phase((bi + 1) % 2, *blocks[bi + 1]) if has_next else None
                gb0n = g_build((bi + 1) % 2, blocks[bi + 1][1], 0) if has_next else None
                for half in range(2):
                    G = Gs[half]
                    gk = ("G", half)
                    if half == 0:
                        bg = chain(gb1, tpn)
                    else:
                        for _ in gb1:
                            pass
                        bg = chain(tpn, gb0n)

                    def stage_pre(k1h):
                        k1 = half * 64 + k1h
                        ub, ubk = ub_r.next()
                        vb, vbk = vb_r.next()
                        P.dma("sp", ub, uTb[l, k1].rearrange("p (k c) -> p k c", k=8), reads=[("uTb", l, k1)], writes=[ubk])
                        P.dma("sp", vb, vKb[l, k1], reads=[("vKb", l, k1)], writes=[vbk])
                        pp = k1h % 2
                        for k in range(8):
                            P.op("pe", lambda e: e.matmul(psb(pp, n), lhsT=ub[:, k, :], rhs=xn[:, k, 0:n], start=(k == 0), stop=(k == 7)),
                                 reads=[ubk, xk], writes=[PK(pp)])
                        ge, gek = ge_r.next()
                        ab, abk = act_r.next()
                        P.op("act", lambda e: e.activation(out=ge[:, 0:n], in_=psb(pp, n), func=AF.Gelu_apprx_tanh), reads=[PK(pp)], writes=[gek])
                        P.op("dve", lambda e: e.tensor_tensor(out=ab[:, 0:n], in0=ge[:, 0:n], in1=G[:, 0:n, k1h], op=ALU.mult),
                             reads=[gek, gk], writes=[abk])
                        return vb, vbk, ab, abk

                    def stage_v(k1h, vb, vbk, ab, abk):
                        for dc in range(8):
                            P.op("pe", lambda e: e.matmul(ps[:, 4 + dc // 2, (dc % 2) * 256:(dc % 2) * 256 + n], lhsT=vb[:, dc * 128:(dc + 1) * 128], rhs=ab[:, 0:n],
                                                          start=(half == 0 and k1h == 0), stop=(half == 1 and k1h == 63)),
                                 reads=[vbk, abk], writes=[PK(4 + dc // 2)])

                    if half == 1:
                        pass
                    LOOK = 2
                    pend = [stage_pre(i) for i in range(LOOK)]
                    for k1h in range(64):
                        if k1h + LOOK < 64:
                            pend.append(stage_pre(k1h + LOOK))
                        stage_v(k1h, *pend.pop(0))
                        pump_gen(bg, PUMP)
                for _ in chain(tpn, gb0n):
                    pass
                hb_ = hb[bs]
                for dc in range(8):
                    P.op("dve", lambda e: e.tensor_tensor(out=hb_[:, dc, 0:n], in0=hb_[:, dc, 0:n], in1=ps[:, 4 + dc // 2, (dc % 2) * 256:(dc % 2) * 256 + n], op=ALU.add),
                         reads=[hk, PK(4 + dc // 2)], writes=[hk])
                P.dma("sp", hD.rearrange("k p t -> p k t")[:, :, c0:c0 + n], hb_[:, :, 0:n], reads=[hk], writes=["hD"])
            P.barrier()
            for k in range(8):
                P.dma("sp" if k % 2 == 0 else "act", h[:, k, :], hD[k], reads=["hD"], writes=["h"])
            P.barrier()
        dump_h(2 * l + 1)

        if want("PLE"):
            P.barrier()
            AR.reset()
            hn = AR.alloc([128, 8, T], BF16)
            hsq = AR.alloc([128, 8, 512], F32)
            rs = AR.alloc([128, 512], F32)
            for ti, (c0, n) in enumerate(NTILES):
                norm_cols(l, c0, n, G_PLE, lambda k, c0=c0, n=n: hn[:, k, c0:c0 + n], ["hn"], hsq, rs, ti % 2, "ple")
            wr = Rot("w", [128, 8, 128], BF16, 4)
            pt_all = AR.alloc([128, 2, T], BF16)
            P.dma("pool", pt_all, pT[l].rearrange("k p t -> p k t"), writes=["pt_all"])
            sg_r = Rot("sg", [128, 512], F32, 2)
            for j in range(8):
                w1, k1_ = load_w(wr, w_pg[l], j * 128, 128)
                w2, k2_ = load_w(wr, w_pp[l], j * 128, 128, nk=2)
                for ti, (c0, n) in enumerate(NTILES):
                    pb = 2 * (ti % 4)
                    proj(pb, w1, k1_, lambda k: hn[:, k, c0:c0 + n], n, ["hn"])
                    proj(pb + 1, w2, k2_, lambda k: pt_all[:, k, c0:c0 + n], n, ["pt_all"], nk=2)
                    sg, sgk = sg_r.next()
                    P.op("act", lambda e: e.activation(out=sg[:, 0:n], in_=psb(pb, n), func=AF.Sigmoid), reads=[PK(pb)], writes=[sgk])
                    P.op("dve", lambda e: e.tensor_tensor(out=sg[:, 0:n], in0=sg[:, 0:n], in1=psb(pb + 1, n), op=ALU.mult), reads=[sgk, PK(pb + 1)], writes=[sgk])
                    P.op("dve", lambda e: e.tensor_tensor(out=h[:, j, c0:c0 + n], in0=h[:, j, c0:c0 + n], in1=sg[:, 0:n], op=ALU.add), reads=["h", sgk], writes=["h"])
            P.barrier()

    P.barrier()
    AR.reset()
    hsq = AR.alloc([128, 8, 512], F32)
    rs = AR.alloc([128, 512], F32)
    yo_r = Rot("yo", [128, 8, 512], F32, 2)
    for ti, (c0, n) in enumerate(NTILES):
        yo, yok = yo_r.next()
        norm_cols(0, c0, n, G_FIN, lambda k, yo=yo, n=n: yo[:, k, 0:n], [yok], hsq, rs, ti % 2, "fin")
        P.dma("sp", yT.rearrange("k p t -> p k t")[:, :, c0:c0 + n], yo[:, :, 0:n], reads=[yok])
    P.emit()
    return nc


def make_consts():
    c = np.zeros((128, NCONST), np.float32)
    s = np.arange(128)[:, None]
    l_ = np.arange(128)[None, :]
    c[:, IDENT:IDENT + 128] = (s == l_)
    c[:, ONES:ONES + 128] = 1.0
    tri = (s <= l_)
    same = (s // 8 == l_ // 8)
    c[:, TRI:TRI + 128] = tri
    c[:, TRIS:TRIS + 128] = tri & same
    c[:, BLKS:BLKS + 128] = same
    c[:, NEGP:NEGP + 128] = np.where(tri, 0.0, -1e5)
    c[:, NEGS:NEGS + 128] = np.where(tri & same, 0.0, -1e5)
    c[:, SEQM:SEQM + 16] = (s // 8 == np.arange(16)[None, :])
    c[:, IO16:IO16 + 16] = np.arange(16)[None, :]
    c[:, IO128:IO128 + 128] = np.arange(128)[None, :]
    sel = np.zeros((128, 16, 128), np.float32)
    for q in range(16):
        sel[8 * q, q, :] = 1.0
    return c, sel.reshape(128, 2048)


def cm(v):
    return np.asarray(v).reshape(-1, 128).T


def prepare_inputs(x_prompt, x_sample, state_ssd, state_ssd_conv, state_cf_conv, p_prompt, p_sample,
                   g_mix, w_in, ssd_conv_w, ssd_conv_b, ssd_dt_bias, ssd_a_log, ssd_d, ssd_norm_g,
                   w_ssd_out, cf_dw_w, cf_dw_b, cf_ln_g, cf_ln_b, w_cf_out, w_o, g_ffn,
                   peer_wq, peer_keys, peer_u, peer_v, g_ple, w_ple_gate, w_ple_proj, g_final):
    f = lambda a: np.ascontiguousarray(np.asarray(a, dtype=np.float32))
    consts, selall = make_consts()
    cvec = np.zeros((128, 2, NCV), np.float32)
    rvec = np.zeros((2, NRV), np.float32)
    for l in range(2):
        cvec[:, l, G_MIX:G_MIX + 8] = cm(g_mix[l])
        cvec[:, l, G_FFN:G_FFN + 8] = cm(g_ffn[l])
        cvec[:, l, G_PLE:G_PLE + 8] = cm(g_ple[l])
        cvec[:, l, G_FIN:G_FIN + 8] = cm(g_final)
        cvec[:, l, SCW:SCW + 64] = np.asarray(ssd_conv_w[l]).reshape(4, 16, 128).transpose(2, 1, 0).reshape(128, 64)
        cvec[:, l, SCB:SCB + 16] = cm(ssd_conv_b[l])
        cvec[:, l, CFW:CFW + 248] = np.asarray(cf_dw_w[l]).reshape(31, 8, 128).transpose(2, 1, 0).reshape(128, 248)
        cvec[:, l, CFB:CFB + 8] = cm(cf_dw_b[l])
        rvec[l, DTB:DTB + 16] = ssd_dt_bias[l]
        rvec[l, ALOG:ALOG + 16] = ssd_a_log[l]
        rvec[l, DSK:DSK + 16] = ssd_d[l]
        rvec[l, NG:NG + 1024] = ssd_norm_g[l]
        rvec[l, LNG:LNG + 1024] = cf_ln_g[l]
        rvec[l, LNB:LNB + 1024] = cf_ln_b[l]
    keysT = f(np.asarray(peer_keys).reshape(2, 16, 128, 128).transpose(0, 1, 3, 2))
    uT = f(np.asarray(peer_u).reshape(2, 128, 128, 8, 128).transpose(0, 2, 4, 3, 1).reshape(2, 128, 128, 1024))
    vK = f(np.asarray(peer_v).reshape(2, 128, 128, 1024).transpose(0, 2, 1, 3))
    shared = dict(consts=consts, selall=selall, cvec=cvec, rvec=rvec, w_in=f(w_in), w_ssd_out=f(w_ssd_out), w_cf_out=f(w_cf_out),
                  w_o=f(w_o), peer_wq=f(peer_wq), keysT=keysT, uT=uT, vK=vK, w_ple_gate=f(w_ple_gate), w_ple_proj=f(w_ple_proj))
    in_maps = []
    for c in range(8):
        sq = slice(16 * c, 16 * c + 16)
        xt = np.concatenate([np.asarray(x_prompt[c]), np.asarray(x_sample[sq]).reshape(128, 1024)], axis=0)
        xT = f(xt.T.reshape(8, 128, T))
        pt = np.concatenate([np.asarray(p_prompt[:, c]), np.asarray(p_sample[:, sq]).reshape(2, 128, 256)], axis=1)
        pT = f(pt.transpose(0, 2, 1).reshape(2, 2, 128, T))
        cfh = f(np.asarray(state_cf_conv[:, sq]).transpose(0, 3, 1, 2).reshape(2, 8, 128, 16, 30))
        sh = f(np.asarray(state_ssd_conv[:, sq]).transpose(0, 3, 1, 2).reshape(2, 16, 128, 16, 3))
        ss = f(np.asarray(state_ssd[:, sq]).reshape(2, 16, 1024, 128).transpose(0, 1, 3, 2))
        m = dict(shared)
        m.update(xT=xT, pT=pT, cf_hist=cfh, ssd_hist=sh, ssd_state=ss)
        in_maps.append(m)
    return in_maps


def assemble(results):
    y_p = np.zeros((8, 2048, 1024), np.float32)
    y_s = np.zeros((128, 8, 1024), np.float32)
    ssd_p = np.zeros((2, 8, 16, 64, 128), np.float32)
    sconv_p = np.zeros((2, 8, 3, 2048), np.float32)
    cf_p = np.zeros((2, 8, 30, 1024), np.float32)
    ssd_s = np.zeros((2, 128, 16, 64, 128), np.float32)
    sconv_s = np.zeros((2, 128, 3, 2048), np.float32)
    cf_s = np.zeros((2, 128, 30, 1024), np.float32)
    for c, r in enumerate(results):
        sq = slice(16 * c, 16 * c + 16)
        y = np.asarray(r["yT"]).reshape(1024, T).T
        y_p[c] = y[:2048]
        y_s[sq] = y[2048:].reshape(16, 8, 1024)
        so = np.asarray(r["ssd_out"]).transpose(0, 1, 3, 2).reshape(2, 17, 16, 64, 128)
        ssd_p[:, c] = so[:, 0]
        ssd_s[:, sq] = so[:, 1:]
        sc = np.asarray(r["sconv_out"]).reshape(2, 2048, 17, 3).transpose(0, 2, 3, 1)
        sconv_p[:, c] = sc[:, 0]
        sconv_s[:, sq] = sc[:, 1:]
        cf = np.asarray(r["cfconv_out"]).reshape(2, 1024, 17, 30).transpose(0, 2, 3, 1)
        cf_p[:, c] = cf[:, 0]
        cf_s[:, sq] = cf[:, 1:]
    return (y_p, y_s, ssd_p, sconv_p, cf_p, ssd_s, sconv_s, cf_s)


def kernel(**inputs):
    in_maps = prepare_inputs(**inputs)
    nc = build_program()
    res = run_bass_kernel_spmd(nc, in_maps, core_ids=list(range(8)))
    return assemble(res.results)
```

```python
import types
import numpy as np
import concourse.bass as bass
import concourse.mybir as mybir
from concourse.bass_utils import run_bass_kernel_spmd

F32 = mybir.dt.float32
BF16 = mybir.dt.bfloat16
U32 = mybir.dt.uint32
AF = mybir.ActivationFunctionType
ALU = mybir.AluOpType
AX = mybir.AxisListType

T = 2176
TP = 2048
NTL = 17
NTILES = [(0, 512), (512, 512), (1024, 512), (1536, 512), (2048, 128)]
EPS = 1e-6
COL_Z = 1024
COL_XBC = 3072
COL_DT = 3088
COL_GLU = 5136
G_MIX, G_FFN, G_PLE, G_FIN, SCW, SCB, CFW, CFB, NCV = 0, 8, 16, 24, 32, 96, 112, 360, 368
DTB, ALOG, DSK, NG, LNG, LNB, NRV = 0, 16, 32, 48, 1072, 2096, 3120
IDENT, ONES, TRI, TRIS, BLKS, NEGP, NEGS, SEQM, IO16, IO128, NCONST = 0, 128, 256, 384, 512, 640, 768, 896, 912, 928, 1056
ARENA = 67400

ENGS = ("sp", "act", "dve", "pool", "pe")
DBG = {"s6_stop": 99, "s6_tiles": None, "s6_sub": 99}
PUMP = 2


def _freeze(fn):
    if getattr(fn, "__closure__", None) is None:
        return fn
    cells = []
    for c in fn.__closure__:
        try:
            cells.append(types.CellType(c.cell_contents))
        except ValueError:
            cells.append(c)
    return types.FunctionType(fn.__code__, fn.__globals__, fn.__name__, fn.__defaults__, tuple(cells))


class Prog:
    def __init__(self, nc, n_dma_sems=10):
        self.nc = nc
        self.ops = []
        self.n_dma_sems = n_dma_sems
        self.last_writer = {}
        self.readers = {}
        self.last_on_eng = {}
        self.dmas_since_barrier = []

    def op(self, eng, fn, reads=(), writes=(), dma=False, extra_deps=()):
        idx = len(self.ops)
        deps = set(extra_deps)
        for r in reads:
            w = self.last_writer.get(r)
            if w is not None:
                deps.add(w)
        for w_ in writes:
            w = self.last_writer.get(w_)
            if w is not None:
                deps.add(w)
            for rd in self.readers.get(w_, {}).values():
                deps.add(rd)
        deps.discard(idx)
        for r in reads:
            self.readers.setdefault(r, {})[eng if not dma else ("dma", idx)] = idx
        for w_ in writes:
            self.last_writer[w_] = idx
            self.readers[w_] = {}
        self.ops.append(dict(eng=eng, fn=_freeze(fn), deps=deps, dma=dma, needed=False))
        self.last_on_eng[eng] = idx
        if dma:
            self.dmas_since_barrier.append(idx)
        return idx

    def dma(self, q, out, in_, reads=(), writes=()):
        return self.op(q, lambda e: e.dma_start(out=out, in_=in_), reads, writes, dma=True)

    def barrier(self):
        deps = set(self.last_on_eng.values()) | set(self.dmas_since_barrier)
        self.dmas_since_barrier = []
        for e in ENGS:
            self.op(e, lambda eng: None, extra_deps=deps)

    def emit(self):
        nc = self.nc
        ops = self.ops

        def skip(p, o):
            return (not p["dma"]) and (not o["dma"]) and p["eng"] == o["eng"] == "pe"

        for o in ops:
            for d in o["deps"]:
                p = ops[d]
                if not skip(p, o):
                    p["needed"] = True
        esem = {e: nc.alloc_semaphore(f"s_{e}") for e in ENGS}
        dq = ("sp", "act", "pool")
        dsem = {e: [nc.alloc_semaphore(f"d_{e}{i}") for i in range(self.n_dma_sems)] for e in dq}
        ecount = {e: 0 for e in ENGS}
        dcount = {e: [0] * self.n_dma_sems for e in dq}
        drr = {e: 0 for e in dq}
        for o in ops:
            e = o["eng"]
            if o["dma"]:
                k = drr[e]
                drr[e] = (k + 1) % self.n_dma_sems
                o["prev_target"] = dcount[e][k]
                dcount[e][k] += 16
                o["done"] = (dsem[e][k], dcount[e][k], ("d", e, k))
            elif o["needed"]:
                ecount[e] += 1
                o["done"] = (esem[e], ecount[e], ("e", e))
            else:
                o["done"] = None
        per_eng = {e: [] for e in ENGS}
        for i, o in enumerate(ops):
            per_eng[o["eng"]].append(i)

        def run_engine(ename, eobj):
            waited = {}
            pending_inc = []
            for i in per_eng[ename]:
                o = ops[i]
                need = {}
                for d in o["deps"]:
                    p = ops[d]
                    if p["done"] is None or skip(p, o):
                        continue
                    sem, val, key = p["done"]
                    if need.get(key, (None, 0))[1] < val:
                        need[key] = (sem, val)
                if o["dma"]:
                    sem, val, key = o["done"]
                    pt = o["prev_target"]
                    if pt > 0 and need.get(key, (None, 0))[1] < pt:
                        need[key] = (sem, pt)
                for key, (sem, val) in need.items():
                    if waited.get(key, 0) >= val:
                        continue
                    eobj.wait_ge(sem, val)
                    waited[key] = val
                ins = o["fn"](eobj)
                if o["done"] is not None:
                    sem, val, key = o["done"]
                    if ins is None:
                        ins = eobj.nop()
                    ins.then_inc(sem, 16 if o["dma"] else 1)
            if ename == "sp":
                for e2 in dq:
                    for k in range(self.n_dma_sems):
                        if dcount[e2][k] > 0 and waited.get(("d", e2, k), 0) < dcount[e2][k]:
                            eobj.wait_ge(dsem[e2][k], dcount[e2][k])

        with nc.Block() as block:
            @block.sync
            def _(e):
                run_engine("sp", e)

            @block.scalar
            def _(e):
                run_engine("act", e)

            @block.vector
            def _(e):
                run_engine("dve", e)

            @block.gpsimd
            def _(e):
                run_engine("pool", e)

            @block.tensor
            def _(e):
                run_engine("pe", e)


class Arena:
    def __init__(self, ap, cap=None):
        self.ap = ap
        self.off = 0
        self.cap = ARENA if cap is None else cap

    def reset(self, off=0):
        self.off = off

    def alloc(self, shape, dt):
        n = 1
        for s in shape[1:]:
            n *= s
        ne = n * (2 if dt in (F32, U32) else 1)
        ne = (ne + 15) // 16 * 16
        assert self.off + ne <= self.cap, (self.off, ne)
        v = self.ap[:, self.off:self.off + ne]
        self.off += ne
        if dt in (F32, U32):
            v = v.bitcast(dt)[:, 0:n]
        else:
            v = v[:, 0:n]
        if len(shape) == 3:
            v = v.rearrange("p (a b) -> p a b", a=shape[1])
        elif len(shape) == 4:
            v = v.rearrange("p (a b c) -> p a b c", a=shape[1], b=shape[2])
        return v


def bc(ap, shape):
    return ap.to_broadcast(list(shape))


def build_program(n_layers=2, stages=None, dbg=None):
    nc = bass.Bass("TRN2", target_bir_lowering=False)

    def D(name, shape, dt=F32, kind="ExternalInput"):
        return nc.dram_tensor(name, list(shape), dt, kind=kind).ap()

    xT = D("xT", [8, 128, T])
    pT = D("pT", [2, 2, 128, T])
    cf_hist = D("cf_hist", [2, 8, 128, 16, 30])
    ssd_hist = D("ssd_hist", [2, 16, 128, 16, 3])
    ssd_state = D("ssd_state", [2, 16, 128, 1024])
    consts_d = D("consts", [128, NCONST])
    selall_d = D("selall", [128, 2048])
    cvec_d = D("cvec", [128, 2, NCV])
    rvec_d = D("rvec", [2, NRV])
    w_in = D("w_in", [2, 1024, 7184])
    w_so = D("w_ssd_out", [2, 1024, 1024])
    w_cf = D("w_cf_out", [2, 1024, 1024])
    w_o = D("w_o", [2, 1024, 1024])
    wq = D("peer_wq", [2, 1024, 2048])
    keysT = D("keysT", [2, 16, 128, 128])
    uT = D("uT", [2, 128, 128, 1024])
    vK = D("vK", [2, 128, 128, 1024])
    w_pg = D("w_ple_gate", [2, 1024, 1024])
    w_pp = D("w_ple_proj", [2, 256, 1024])
    yT = D("yT", [8, 128, T], kind="ExternalOutput")
    ssd_out = D("ssd_out", [2, 17, 128, 1024], kind="ExternalOutput")
    sconv_out = D("sconv_out", [2, 16, 128, 17, 3], kind="ExternalOutput")
    cfconv_out = D("cfconv_out", [2, 8, 128, 17, 30], kind="ExternalOutput")
    skind = "ExternalOutput" if dbg else "Internal"
    cT = D("cT", [8, 128, T], kind=skind)
    xbcT = D("xbcT", [16, 128, T], kind=skind)
    yaT_d = D("yaT_d", [8, 128, T], BF16, kind=skind)
    ybT_d = D("ybT_d", [8, 128, T], BF16, kind=skind)
    uTb = D("uTb", [2, 128, 128, 1024], BF16, kind="Internal")
    vKb = D("vKb", [2, 128, 128, 1024], BF16, kind="Internal")
    wqb = D("wqb", [2, 16, 128, 1024], BF16, kind="Internal")
    hD = D("hD", [8, 128, T], kind="Internal")
    mixT_d = D("mixT_d", [8, 128, T], BF16, kind="Internal")
    hdbg = D("hdbg", [4, 8, 128, T], kind="ExternalOutput") if dbg else None

    cs = nc.alloc_sbuf_tensor("cs", [128, NCONST], F32).ap()
    cv = nc.alloc_sbuf_tensor("cv", [128, 2, NCV], F32).ap()
    h = nc.alloc_sbuf_tensor("h", [128, 8, T], F32).ap()
    arena_ap = nc.alloc_sbuf_tensor("arena", [128, ARENA], BF16).ap()
    ps = nc.alloc_psum_tensor("ps", [128, 8, 512], F32).ap()
    AR = Arena(arena_ap)
    P = Prog(nc)

    ident = cs[:, IDENT:IDENT + 128]
    ones = cs[:, ONES:ONES + 128]

    def psb(b, n=512):
        return ps[:, b, 0:n]

    def ps2(b):
        return ps[:, b:b + 2, :].rearrange("p a b -> p (a b)")

    def PK(b):
        return ("ps", b)

    P.dma("sp", cs, consts_d, writes=["cs"])
    P.dma("sp", cv, cvec_d, writes=["cv"])
    for k in range(8):
        P.dma("sp" if k % 2 == 0 else "act", h[:, k, :], xT[k], writes=["h"])

    class Rot:
        def __init__(self, name, shape, dt, n, arena=None):
            self.bufs = [(arena or AR).alloc(shape, dt) for _ in range(n)]
            self.name = name
            self.i = 0

        def next(self):
            i = self.i
            self.i = (i + 1) % len(self.bufs)
            return self.bufs[i], (self.name, i)

    def load_w(rot, w2d, c0, ncols, nk=8):
        buf, key = rot.next()
        src = w2d.rearrange("(k p) c -> p k c", p=128)[:, :, c0:c0 + ncols]
        P.dma("pool", buf[:, 0:nk, 0:ncols], src, writes=[key])
        if pump_state["auto"]:
            pump(pump_state["auto"], layer=0)
        return buf, key

    precast_pending = [(l_, k1_, w_) for l_ in range(n_layers) for k1_ in range(128) for w_ in (0, 1)]
    pump_state = {"auto": 0}

    def pump(n, layer=None):
        cnt = 0
        i = 0
        while cnt < n and i < len(precast_pending):
            l_, k1_, w_ = precast_pending[i]
            if layer is not None and l_ != layer:
                i += 1
                continue
            precast_pending.pop(i)
            if w_ == 0:
                P.dma("pool", uTb[l_, k1_], uT[l_, k1_], writes=[("uTb", l_, k1_)])
            else:
                P.dma("pool", vKb[l_, k1_], vK[l_, k1_], writes=[("vKb", l_, k1_)])
            cnt += 1

    def norm_cols(l, c0, n, gcol, out_fn, out_keys, hsq, rs, psbank, tag, src=None, srckey="h"):
        s3 = h[:, :, c0:c0 + n] if src is None else src[:, :, 0:n]
        P.op("act", lambda e: e.activation(out=hsq[:, :, 0:n], in_=s3, func=AF.Square),
             reads=[srckey], writes=["hsq"])
        for k in range(8):
            P.op("pe", lambda e, k=k: e.matmul(psb(psbank, n), lhsT=ones, rhs=hsq[:, k, 0:n], start=(k == 0), stop=(k == 7)),
                 reads=["hsq", "cs"], writes=[PK(psbank)])
        P.op("act", lambda e: e.activation(out=rs[:, 0:n], in_=psb(psbank, n), func=AF.Sqrt, bias=EPS, scale=1.0 / 1024),
             reads=[PK(psbank)], writes=["rs"])
        P.op("dve", lambda e: e.reciprocal(out=rs[:, 0:n], in_=rs[:, 0:n]), reads=["rs"], writes=["rs"])
        for k in range(8):
            o_ = out_fn(k)
            i_ = s3[:, k, :]
            P.op("dve", lambda e, k=k, o_=o_, i_=i_: e.scalar_tensor_tensor(out=o_, in0=i_, scalar=cv[:, l, gcol + k:gcol + k + 1],
                                                                in1=rs[:, 0:n], op0=ALU.mult, op1=ALU.mult),
                 reads=[srckey, "rs", "cv"], writes=out_keys)

    def proj(psbank, wbuf, wkey, rhs_fn, n, rkeys, nk=8):
        for k in range(nk):
            r_ = rhs_fn(k)
            P.op("pe", lambda e, k=k, r_=r_: e.matmul(psb(psbank, n), lhsT=wbuf[:, k, :], rhs=r_, start=(k == 0), stop=(k == nk - 1)),
                 reads=[wkey] + list(rkeys), writes=[PK(psbank)])

    def conv(l, src_p, src_s, acc, K, wcol, bcol, skeys, akey, acc2=None, a2key=None, acc3=None):
        def views(a):
            return a[:, 0:TP], a[:, TP:T].rearrange("p (s t) -> p s t", t=8)
        ksplit = K if acc2 is None else 23
        tp_, ts_ = views(acc3) if acc3 is not None else (None, None)
        for (eng, a_, ak_, k0, k1_) in (("dve", acc, akey, 0, ksplit), ("pool", acc2, a2key, ksplit, K)):
            if k0 >= k1_:
                continue
            ap_, as_ = views(a_)
            for k in range(k0, k1_):
                w_ = cv[:, l, wcol + k:wcol + k + 1]
                for (a, s_) in ((ap_, src_p[:, k:k + TP]), (as_, src_s[:, :, k:k + 8])):
                    if k == 0:
                        P.op(eng, lambda e, a=a, s_=s_, w_=w_: e.tensor_scalar(out=a, in0=s_, scalar1=w_, scalar2=cv[:, l, bcol:bcol + 1],
                                                                                op0=ALU.mult, op1=ALU.add),
                             reads=list(skeys) + ["cv"], writes=[ak_])
                    elif k == k0:
                        P.op(eng, lambda e, a=a, s_=s_, w_=w_: e.tensor_scalar(out=a, in0=s_, scalar1=w_, scalar2=None, op0=ALU.mult),
                             reads=list(skeys) + ["cv"], writes=[ak_])
                    elif eng == "dve":
                        P.op(eng, lambda e, a=a, s_=s_, w_=w_: e.scalar_tensor_tensor(out=a, in0=s_, scalar=w_, in1=a, op0=ALU.mult, op1=ALU.add),
                             reads=list(skeys) + ["cv", ak_], writes=[ak_])
                    else:
                        t_ = tp_ if a is ap_ else ts_
                        P.op(eng, lambda e, t_=t_, s_=s_, w_=w_: e.tensor_scalar(out=t_, in0=s_, scalar1=w_, scalar2=None, op0=ALU.mult),
                             reads=list(skeys) + ["cv"], writes=["acc3"])
                        P.op(eng, lambda e, a=a, t_=t_: e.tensor_tensor(out=a, in0=a, in1=t_, op=ALU.add), reads=["acc3", ak_], writes=[ak_])
        if acc2 is not None:
            P.op("dve", lambda e: e.tensor_tensor(out=acc, in0=acc, in1=acc2, op=ALU.add), reads=[akey, a2key], writes=[akey])

    def dump_h(i):
        if dbg:
            for k in range(8):
                P.dma("sp", hdbg[i, k], h[:, k, :], reads=["h"])

    want = (lambda s: True) if stages is None else (lambda s: s in stages)

    for l in range(n_layers):
        pump_state["auto"] = 4 if l == 0 else 0
        P.barrier()
        AR.reset()
        hn = AR.alloc([128, 8, T], BF16)
        base_off = AR.off
        hsq = AR.alloc([128, 8, 512], F32)
        rs = AR.alloc([128, 512], F32)
        for ti, (c0, n) in enumerate(NTILES):
            norm_cols(l, c0, n, G_MIX, lambda k, c0=c0, n=n: hn[:, k, c0:c0 + n], ["hn"], hsq, rs, ti % 2, "mix")
        P.barrier()

        if want("S1"):
            AR.reset(base_off)
            wr = Rot("w", [128, 8, 128], BF16, 4)
            ubuf_r = Rot("ubuf", [128, 30 + TP], F32, 2)
            ubufs_r = Rot("ubuf_s", [128, 16, 38], F32, 2)
            acc_r1 = Rot("acc1", [128, T], F32, 2)
            sg_r = Rot("sg", [128, 512], F32, 2)
            for ub_ in ubuf_r.bufs:
                P.op("dve", lambda e: e.memset(ub_[:, 0:30], 0.0), writes=[("ubuf", 0), ("ubuf", 1)])
            for j in range(8):
                wa, wak = load_w(wr, w_in[l], COL_DT + j * 128, 128)
                wg, wgk = load_w(wr, w_in[l], COL_DT + 1024 + j * 128, 128)
                ubuf, ubk_ = ubuf_r.next()
                ubuf_s, ubsk_ = ubufs_r.next()
                acc, acck_ = acc_r1.next()
                P.dma("sp", ubuf_s[:, :, 0:30], cf_hist[l, j], writes=[ubsk_])
                for ti, (c0, n) in enumerate(NTILES):
                    ba, bb = 2 * (ti % 2), 2 * (ti % 2) + 1
                    proj(ba, wa, wak, lambda k, c0=c0, n=n: hn[:, k, c0:c0 + n], n, ["hn"])
                    proj(bb, wg, wgk, lambda k, c0=c0, n=n: hn[:, k, c0:c0 + n], n, ["hn"])
                    sg, sgk = sg_r.next()
                    P.op("act", lambda e: e.activation(out=sg[:, 0:n], in_=psb(bb, n), func=AF.Sigmoid),
                         reads=[PK(bb)], writes=[sgk])
                    if c0 < TP:
                        P.op("dve", lambda e: e.tensor_tensor(out=ubuf[:, 30 + c0:30 + c0 + n], in0=psb(ba, n), in1=sg[:, 0:n], op=ALU.mult),
                             reads=[PK(ba), sgk], writes=[ubk_])
                    else:
                        P.op("dve", lambda e: e.tensor_tensor(out=ubuf_s[:, :, 30:38], in0=psb(ba, 128).rearrange("p (s t) -> p s t", t=8),
                                                              in1=sg[:, 0:128].rearrange("p (s t) -> p s t", t=8), op=ALU.mult),
                             reads=[PK(ba), sgk], writes=[ubsk_])
                P.dma("sp", cfconv_out[l, j, :, 0, :], ubuf[:, TP:TP + 30], reads=[ubk_])
                P.dma("sp", cfconv_out[l, j, :, 1:17, :], ubuf_s[:, :, 8:38], reads=[ubsk_])
                conv(l, ubuf, ubuf_s, acc, 31, CFW + j * 31, CFB + j, [ubk_, ubsk_], acck_)
                P.dma("sp", cT[j], acc, reads=[acck_])
            P.barrier()

        if want("S2"):
            AR.reset(base_off)
            rv = AR.alloc([128, 2048], F32)
            P.dma("sp", rv, rvec_d[l, LNG:LNG + 2048].partition_broadcast(128), writes=["rv"])
            cin_r = Rot("cin", [128, 8, 128], F32, 2)
            lnb = AR.alloc([128, 1024], F32)
            ybtm = AR.alloc([128, 1024], F32)
            ybst_r = Rot("ybst", [128, 8, 128], BF16, 2)
            st = AR.alloc([128, 8], F32)
            for t in range(NTL):
                cin, cink = cin_r.next()
                P.dma("sp", cin, cT.rearrange("k p t -> p k t")[:, :, t * 128:(t + 1) * 128], writes=[cink])
                b0 = 4 * (t % 2)
                for k in range(8):
                    P.op("pe", lambda e, k=k, b0=b0: e.transpose(out=ps[:, b0 + k // 4, (k % 4) * 128:(k % 4) * 128 + 128], in_=cin[:, k, :], identity=ident),
                         reads=[cink, "cs"], writes=[PK(b0), PK(b0 + 1)])
                ptm = ps2(b0)
                P.op("dve", lambda e: e.memset(st, 0.0), writes=["st"])
                P.op("dve", lambda e, ptm=ptm: e.reduce_sum(out=st[:, 0:1], in_=ptm, axis=AX.X), reads=[PK(b0), PK(b0 + 1), "st"], writes=["st"])
                P.op("act", lambda e, ptm=ptm: e.activation(out=lnb, in_=ptm, func=AF.Square, accum_out=st[:, 1:2]),
                     reads=[PK(b0), PK(b0 + 1), "st"], writes=["lnb", "st"])
                P.op("dve", lambda e: e.tensor_scalar(out=st[:, 2:3], in0=st[:, 0:1], scalar1=1.0 / 1024, scalar2=None, op0=ALU.mult), reads=["st"], writes=["st"])
                P.op("dve", lambda e: e.tensor_tensor(out=st[:, 3:4], in0=st[:, 2:3], in1=st[:, 2:3], op=ALU.mult), reads=["st"], writes=["st"])
                P.op("dve", lambda e: e.scalar_tensor_tensor(out=st[:, 4:5], in0=st[:, 1:2], scalar=1.0 / 1024, in1=st[:, 3:4], op0=ALU.mult, op1=ALU.subtract),
                     reads=["st"], writes=["st"])
                P.op("act", lambda e: e.activation(out=st[:, 5:6], in_=st[:, 4:5], func=AF.Sqrt, bias=EPS, scale=1.0), reads=["st"], writes=["st"])
                P.op("dve", lambda e: e.reciprocal(out=st[:, 6:7], in_=st[:, 5:6]), reads=["st"], writes=["st"])
                P.op("dve", lambda e, ptm=ptm: e.tensor_scalar(out=lnb, in0=ptm, scalar1=st[:, 2:3], scalar2=st[:, 6:7], op0=ALU.subtract, op1=ALU.mult),
                     reads=[PK(b0), PK(b0 + 1), "st", "lnb"], writes=["lnb"])
                P.op("dve", lambda e: e.tensor_tensor(out=lnb, in0=lnb, in1=rv[:, 0:1024], op=ALU.mult), reads=["lnb", "rv"], writes=["lnb"])
                P.op("dve", lambda e: e.tensor_tensor(out=lnb, in0=lnb, in1=rv[:, 1024:2048], op=ALU.add), reads=["lnb", "rv"], writes=["lnb"])
                P.op("act", lambda e: e.activation(out=ybtm, in_=lnb, func=AF.Silu), reads=["lnb"], writes=["ybtm"])
                b2 = b0 + 2
                for k in range(8):
                    P.op("pe", lambda e, k=k, b2=b2: e.transpose(out=ps[:, b2 + k // 4, (k % 4) * 128:(k % 4) * 128 + 128], in_=ybtm[:, k * 128:(k + 1) * 128], identity=ident),
                         reads=["ybtm", "cs"], writes=[PK(b2), PK(b2 + 1)])
                ybst, ybk = ybst_r.next()
                P.op("act", lambda e, b2=b2, ybst=ybst: e.activation(out=ybst, in_=ps2(b2).rearrange("p (k t) -> p k t", k=8), func=AF.Identity),
                     reads=[PK(b2), PK(b2 + 1)], writes=[ybk])
                P.dma("sp", ybT_d.rearrange("k p t -> p k t")[:, :, t * 128:(t + 1) * 128], ybst, reads=[ybk], writes=["ybT_d"])
            P.barrier()

        if want("S3"):
            AR.reset(base_off)
            wr = Rot("w", [128, 8, 128], BF16, 3)
            xbuf_r = Rot("xbuf", [128, 3 + TP], F32, 2)
            xbufs_r = Rot("xbuf_s", [128, 16, 11], F32, 2)
            acc_r = Rot("acc", [128, T], F32, 2)
            for xb_ in xbuf_r.bufs:
                P.op("dve", lambda e: e.memset(xb_[:, 0:3], 0.0), writes=[("xbuf", 0), ("xbuf", 1)])
            for j in range(16):
                w_, wk = load_w(wr, w_in[l], COL_Z + j * 128, 128)
                xbuf, xbk_ = xbuf_r.next()
                xbuf_s, xbsk_ = xbufs_r.next()
                P.dma("sp", xbuf_s[:, :, 0:3], ssd_hist[l, j], writes=[xbsk_])
                for ti, (c0, n) in enumerate(NTILES):
                    ba = ti % 4
                    proj(ba, w_, wk, lambda k, c0=c0, n=n: hn[:, k, c0:c0 + n], n, ["hn"])
                    if c0 < TP:
                        P.op("act", lambda e: e.activation(out=xbuf[:, 3 + c0:3 + c0 + n], in_=psb(ba, n), func=AF.Identity),
                             reads=[PK(ba)], writes=[xbk_])
                    else:
                        P.op("act", lambda e: e.activation(out=xbuf_s[:, :, 3:11], in_=psb(ba, 128).rearrange("p (s t) -> p s t", t=8), func=AF.Identity),
                             reads=[PK(ba)], writes=[xbsk_])
                P.dma("sp", sconv_out[l, j, :, 0, :], xbuf[:, TP:TP + 3], reads=[xbk_])
                P.dma("sp", sconv_out[l, j, :, 1:17, :], xbuf_s[:, :, 8:11], reads=[xbsk_])
                acc, acck = acc_r.next()
                conv(l, xbuf, xbuf_s, acc, 4, SCW + j * 4, SCB + j, [xbk_, xbsk_], acck)
                P.op("act", lambda e: e.activation(out=acc, in_=acc, func=AF.Silu), reads=[acck], writes=[acck])
                P.dma("sp", xbcT[j], acc, reads=[acck], writes=["xbcT"])
            P.barrier()

        if want("S6"):
            AR.reset(base_off)
            rvs = AR.alloc([128, 48], F32)
            rvn = AR.alloc([128, 1024], F32)
            P.dma("sp", rvs, rvec_d[l, 0:48].partition_broadcast(128), writes=["rvs"])
            P.dma("sp", rvn, rvec_d[l, NG:NG + 1024].partition_broadcast(128), writes=["rvn"])
            wdt = AR.alloc([128, 8, 16], BF16)
            P.dma("pool", wdt, w_in[l].rearrange("(k p) c -> p k c", p=128)[:, :, COL_XBC:COL_XBC + 16], writes=["wdt"])
            wz = AR.alloc([128, 8, 1024], BF16)
            for k in range(8):
                P.dma("pool", wz[:, k, :], w_in[l][k * 128:(k + 1) * 128, 0:1024], writes=["wz"])
            dt_tok = AR.alloc([128, NTL, 16], F32)
            dA_tok = AR.alloc([128, NTL, 16], F32)
            Abc = AR.alloc([128, 16], F32)
            tmp16 = AR.alloc([128, 16], F32)
            P.op("act", lambda e: e.activation(out=Abc, in_=rvs[:, ALOG:ALOG + 16], func=AF.Exp), reads=["rvs"], writes=["Abc"])
            P.op("dve", lambda e: e.tensor_scalar(out=Abc, in0=Abc, scalar1=-1.0, scalar2=None, op0=ALU.mult), reads=["Abc"], writes=["Abc"])
            for t in range(NTL):
                bk = t % 2
                for k in range(8):
                    P.op("pe", lambda e, k=k, t=t, bk=bk: e.matmul(ps[:, bk, 0:16], lhsT=hn[:, k, t * 128:(t + 1) * 128], rhs=wdt[:, k, :], start=(k == 0), stop=(k == 7)),
                         reads=["hn", "wdt"], writes=[PK(bk)])
                P.op("dve", lambda e, bk=bk: e.tensor_tensor(out=tmp16, in0=ps[:, bk, 0:16], in1=rvs[:, DTB:DTB + 16], op=ALU.add), reads=[PK(bk), "rvs"], writes=["tmp16"])
                P.op("act", lambda e: e.activation(out=tmp16, in_=tmp16, func=AF.Exp), reads=["tmp16"], writes=["tmp16"])
                P.op("act", lambda e, t=t: e.activation(out=dt_tok[:, t, :], in_=tmp16, func=AF.Ln, bias=1.0, scale=1.0), reads=["tmp16"], writes=["dt_tok"])
                P.op("dve", lambda e, t=t: e.tensor_tensor(out=dA_tok[:, t, :], in0=dt_tok[:, t, :], in1=Abc, op=ALU.mult), reads=["dt_tok", "Abc"], writes=["dA_tok"])
            xin_r = Rot("xin", [128, 16, 128], F32, 1)
            xs_sb = AR.alloc([128, 1024], F32)
            X_bf = AR.alloc([128, 1024], BF16)
            Xd_bf = AR.alloc([128, 1024], BF16)
            Btm_bf = AR.alloc([128, 512], BF16)
            BC_bf = AR.alloc([128, 8, 128], BF16)
            R = AR.alloc([128, 16, 128], F32)
            Lm = R
            M_bf = AR.alloc([128, 16, 128], BF16)
            cb_sb = AR.alloc([128, 512], F32)
            t1 = AR.alloc([128, 1024], F32)
            sz = AR.alloc([128, 1024], F32)
            ST = AR.alloc([128, 1024], F32)
            ST_bf = AR.alloc([128, 1024], BF16)
            sm = AR.alloc([128, 128], F32)
            yast_r = Rot("yast", [128, 8, 128], BF16, 1)
            Acol, tot, dec, expA, cdbc, ss4, rg = sm[:, 0:16], sm[:, 16:32], sm[:, 32:48], sm[:, 48:64], sm[:, 64:80], sm[:, 80:84], sm[:, 84:88]
            Cm_r = Rot("Cm", [128, 16, 128], BF16, 2)
            Ysel = AR.alloc([128, 16, 16], F32)
            S0f_r = Rot("S0f", [128, 1024], F32, 2)
            S0b_r = Rot("S0b", [128, 256], BF16, 4)
            Bm_r = Rot("Bm", [128, 512], BF16, 2)
            cd_all = AR.alloc([128, 16, 16], F32)
            for cm_ in Cm_r.bufs:
                P.op("dve", lambda e, cm_=cm_: e.memset(cm_, 0.0), writes=[("Cm", 0), ("Cm", 1)])
            P.op("dve", lambda e: e.memset(ST, 0.0), writes=["ST"])
            P.op("dve", lambda e: e.memset(ST_bf, 0.0), writes=["ST_bf"])

            def v16(ap):
                return ap.rearrange("p (h q) -> p h q", h=16)

            for t in (DBG["s6_tiles"] if DBG["s6_tiles"] is not None else range(NTL)):
                is_s = (t == NTL - 1)
                triM = cs[:, TRIS:TRIS + 128] if is_s else cs[:, TRI:TRI + 128]
                negM = cs[:, NEGS:NEGS + 128] if is_s else cs[:, NEGP:NEGP + 128]
                blkM = cs[:, BLKS:BLKS + 128] if is_s else ones
                tsl = slice(t * 128, (t + 1) * 128)
                xin, xink = xin_r.next()
                for q4 in range(4):
                    P.dma("sp", xin[:, 4 * q4:4 * q4 + 4, :], xbcT.rearrange("k p t -> p k t")[:, 4 * q4:4 * q4 + 4, tsl], reads=["xbcT"], writes=[xink])
                for k in range(8):
                    P.op("pe", lambda e, k=k, xin=xin: e.transpose(out=ps[:, k // 4, (k % 4) * 128:(k % 4) * 128 + 128], in_=xin[:, k, :], identity=ident),
                         reads=[xink, "cs"], writes=[PK(0), PK(1)])
                for g in range(4):
                    P.op("pe", lambda e, g=g, xin=xin: e.transpose(out=ps[:, 2, g * 128:(g + 1) * 128], in_=xin[:, 8 + g, :], identity=ident),
                         reads=[xink, "cs"], writes=[PK(2)])
                if DBG["s6_sub"] <= 1:
                    continue
                P.op("act", lambda e: e.activation(out=xs_sb, in_=ps2(0), func=AF.Identity), reads=[PK(0), PK(1)], writes=["xs_sb"])
                if DBG["s6_sub"] <= 2:
                    continue
                P.op("dve", lambda e, t=t: e.tensor_tensor(out=v16(X_bf), in0=v16(xs_sb), in1=bc(dt_tok[:, t, :].unsqueeze(2), [128, 16, 64]), op=ALU.mult),
                     reads=["xs_sb", "dt_tok"], writes=["X_bf"])
                if DBG["s6_sub"] <= 3:
                    continue
                P.op("act", lambda e: e.activation(out=Btm_bf, in_=psb(2), func=AF.Identity), reads=[PK(2)], writes=["Btm_bf"])
                if DBG["s6_sub"] <= 4:
                    continue
                P.op("dve", lambda e, xin=xin: e.tensor_copy(out=BC_bf, in_=xin[:, 8:16, :]), reads=[xink], writes=["BC_bf"])
                if DBG["s6_stop"] <= 1:
                    continue
                P.op("pe", lambda e, t=t, triM=triM: e.matmul(ps[:, 3, 0:16], lhsT=triM, rhs=dA_tok[:, t, :], start=True, stop=True), reads=["dA_tok", "cs"], writes=[PK(3)])
                P.op("pe", lambda e, t=t, blkM=blkM: e.matmul(ps[:, 3, 16:32], lhsT=blkM, rhs=dA_tok[:, t, :], start=True, stop=True), reads=["dA_tok", "cs"], writes=[PK(3)])
                P.op("dve", lambda e: e.tensor_copy(out=sm[:, 0:32], in_=ps[:, 3, 0:32]), reads=[PK(3)], writes=["sm_a"])
                P.op("dve", lambda e, t=t, triM=triM: e.tensor_tensor(out=R, in0=bc(triM.unsqueeze(1), [128, 16, 128]), in1=bc(dA_tok[:, t, :].unsqueeze(2), [128, 16, 128]), op=ALU.mult),
                     reads=["cs", "dA_tok"], writes=["R"])
                for i in range(4):
                    P.op("pe", lambda e, i=i: e.matmul(psb(4 + i), lhsT=ones, rhs=R[:, 4 * i:4 * i + 4, :].rearrange("p a b -> p (a b)"), start=True, stop=True),
                         reads=["R", "cs"], writes=[PK(4 + i)])
                for i in range(4):
                    P.op("dve", lambda e, i=i: e.tensor_tensor(out=Lm[:, 4 * i:4 * i + 4, :], in0=ps[:, 4 + i, :].rearrange("p (a b) -> p a b", a=4),
                                                               in1=bc(Acol[:, 4 * i:4 * i + 4].unsqueeze(2), [128, 4, 128]), op=ALU.subtract),
                         reads=[PK(4 + i), "sm_a", "R"], writes=["R", "Lm"])
                P.op("dve", lambda e, negM=negM: e.tensor_tensor(out=Lm, in0=Lm, in1=bc(negM.unsqueeze(1), [128, 16, 128]), op=ALU.add), reads=["Lm", "R", "cs"], writes=["Lm", "R"])
                P.op("act", lambda e: e.activation(out=Lm, in_=Lm, func=AF.Exp), reads=["Lm", "R"], writes=["Lm", "R"])
                if DBG["s6_stop"] <= 2:
                    continue
                for g in range(4):
                    P.op("pe", lambda e, g=g: e.matmul(ps[:, 2, g * 128:(g + 1) * 128], lhsT=BC_bf[:, g, :], rhs=BC_bf[:, 4 + g, :], start=True, stop=True),
                         reads=["BC_bf", "Btm_bf"], writes=[PK(2)])
                P.op("act", lambda e: e.activation(out=cb_sb, in_=psb(2), func=AF.Identity), reads=[PK(2)], writes=["cb_sb"])
                P.op("dve", lambda e: e.tensor_tensor(out=M_bf.rearrange("p (g r) l -> p g r l", g=4), in0=Lm.rearrange("p (g r) l -> p g r l", g=4),
                                                      in1=bc(cb_sb.rearrange("p (g l) -> p g l", g=4).unsqueeze(2), [128, 4, 4, 128]), op=ALU.mult),
                     reads=["Lm", "R", "cb_sb"], writes=["M_bf"])
                if DBG["s6_stop"] <= 3:
                    continue
                for hh in range(16):
                    P.op("pe", lambda e, hh=hh: e.matmul(ps2(0)[:, hh * 64:(hh + 1) * 64], lhsT=M_bf[:, hh, :], rhs=X_bf[:, hh * 64:(hh + 1) * 64], start=True, stop=True),
                         reads=["M_bf", "X_bf", "xs_sb"], writes=[PK(0), PK(1)])
                if DBG["s6_stop"] <= 4:
                    continue
                if not is_s:
                    for g in range(4):
                        P.op("pe", lambda e, g=g: e.matmul(ps2(4)[:, g * 256:(g + 1) * 256], lhsT=BC_bf[:, 4 + g, :], rhs=ST_bf[:, g * 256:(g + 1) * 256], start=True, stop=True),
                             reads=["BC_bf", "ST_bf", "Lm"], writes=[PK(4), PK(5)])
                else:
                    for g in range(4):
                        Cmg, Cmk = Cm_r.next()
                        for sq in range(16):
                            P.op("dve", lambda e, g=g, sq=sq, Cmg=Cmg: e.tensor_copy(out=Cmg[:, sq, 8 * sq:8 * sq + 8], in_=BC_bf[:, 4 + g, 8 * sq:8 * sq + 8]),
                                 reads=["BC_bf", Cmk], writes=[Cmk])
                        for sq in range(16):
                            S0b, S0bk = S0b_r.next()
                            P.dma("pool", S0b, ssd_state[l, sq][:, g * 256:(g + 1) * 256], writes=[S0bk])
                            P.op("pe", lambda e, g=g, sq=sq, Cmg=Cmg, S0b=S0b: e.matmul(ps2(4)[:, g * 256:(g + 1) * 256], lhsT=Cmg[:, sq, :], rhs=S0b,
                                                                                        start=(sq == 0), stop=(sq == 15)),
                                 reads=[Cmk, S0bk, "Lm"], writes=[PK(4), PK(5)])
                if DBG["s6_stop"] <= 5:
                    continue
                P.op("act", lambda e: e.activation(out=expA, in_=Acol, func=AF.Exp), reads=["sm_a"], writes=["sm_e"])
                for i in range(2):
                    P.op("dve", lambda e, i=i: e.tensor_tensor(out=v16(t1)[:, 8 * i:8 * i + 8, :], in0=ps[:, 4 + i, :].rearrange("p (a b) -> p a b", a=8),
                                                               in1=bc(expA[:, 8 * i:8 * i + 8].unsqueeze(2), [128, 8, 64]), op=ALU.mult),
                         reads=[PK(4 + i), "sm_e"], writes=["t1"])
                P.op("dve", lambda e: e.tensor_tensor(out=t1, in0=t1, in1=ps2(0), op=ALU.add), reads=["t1", PK(0), PK(1)], writes=["t1"])
                P.op("dve", lambda e: e.tensor_tensor(out=v16(sz), in0=v16(xs_sb), in1=bc(rvs[:, DSK:DSK + 16].unsqueeze(2), [128, 16, 64]), op=ALU.mult),
                     reads=["xs_sb", "rvs"], writes=["sz"])
                P.op("dve", lambda e: e.tensor_tensor(out=t1, in0=t1, in1=sz, op=ALU.add), reads=["t1", "sz"], writes=["t1"])
                if DBG["s6_stop"] <= 6:
                    continue
                for hf in range(2):
                    for k in range(8):
                        P.op("pe", lambda e, hf=hf, k=k, tsl=tsl: e.matmul(psb(6 + hf), lhsT=hn[:, k, tsl], rhs=wz[:, k, hf * 512:(hf + 1) * 512], start=(k == 0), stop=(k == 7)),
                             reads=["hn", "wz", "Lm"], writes=[PK(6 + hf)])
                P.op("act", lambda e: e.activation(out=sz, in_=ps2(6), func=AF.Silu), reads=[PK(6), PK(7), "t1"], writes=["sz"])
                P.op("dve", lambda e: e.tensor_tensor(out=t1, in0=t1, in1=sz, op=ALU.mult), reads=["t1", "sz"], writes=["t1"])
                if DBG["s6_stop"] <= 7:
                    continue
                P.op("dve", lambda e: e.memset(ss4, 0.0), writes=["ss4"])
                for g in range(4):
                    P.op("act", lambda e, g=g: e.activation(out=sz[:, g * 256:(g + 1) * 256], in_=t1[:, g * 256:(g + 1) * 256], func=AF.Square, accum_out=ss4[:, g:g + 1]),
                         reads=["t1", "ss4"], writes=["sz", "ss4"])
                P.op("act", lambda e: e.activation(out=rg, in_=ss4, func=AF.Sqrt, bias=EPS, scale=1.0 / 256), reads=["ss4"], writes=["rg"])
                P.op("dve", lambda e: e.reciprocal(out=rg, in_=rg), reads=["rg"], writes=["rg"])
                P.op("dve", lambda e: e.tensor_tensor(out=t1.rearrange("p (g c) -> p g c", g=4), in0=t1.rearrange("p (g c) -> p g c", g=4),
                                                      in1=bc(rg.unsqueeze(2), [128, 4, 256]), op=ALU.mult), reads=["t1", "rg"], writes=["t1"])
                P.op("dve", lambda e: e.tensor_tensor(out=t1, in0=t1, in1=rvn, op=ALU.mult), reads=["t1", "rvn"], writes=["t1"])
                for k in range(8):
                    P.op("pe", lambda e, k=k: e.transpose(out=ps[:, k // 4, (k % 4) * 128:(k % 4) * 128 + 128], in_=t1[:, k * 128:(k + 1) * 128], identity=ident),
                         reads=["t1", "cs"], writes=[PK(0), PK(1)])
                if DBG["s6_stop"] <= 8:
                    continue
                yast, yak = yast_r.next()
                P.op("act", lambda e, yast=yast: e.activation(out=yast, in_=ps2(0).rearrange("p (k t) -> p k t", k=8), func=AF.Identity),
                     reads=[PK(0), PK(1)], writes=[yak])
                P.dma("sp", yaT_d.rearrange("k p t -> p k t")[:, :, tsl], yast, reads=[yak], writes=["yaT_d"])
                if DBG["s6_stop"] <= 9:
                    continue
                P.op("dve", lambda e: e.tensor_tensor(out=dec, in0=tot, in1=Acol, op=ALU.subtract), reads=["sm_a"], writes=["sm_d"])
                P.op("act", lambda e: e.activation(out=dec, in_=dec, func=AF.Exp), reads=["sm_d"], writes=["sm_d"])
                P.op("dve", lambda e: e.tensor_tensor(out=v16(Xd_bf), in0=v16(X_bf), in1=bc(dec.unsqueeze(2), [128, 16, 64]), op=ALU.mult),
                     reads=["X_bf", "sm_d"], writes=["Xd_bf"])
                if not is_s:
                    P.op("act", lambda e: e.activation(out=cdbc, in_=tot, func=AF.Exp), reads=["sm_a"], writes=["sm_c"])
                    for g in range(4):
                        P.op("pe", lambda e, g=g: e.matmul(ps2(4)[:, g * 256:(g + 1) * 256], lhsT=Btm_bf[:, g * 128:(g + 1) * 128], rhs=Xd_bf[:, g * 256:(g + 1) * 256], start=True, stop=True),
                             reads=["Btm_bf", "Xd_bf", "t1"], writes=[PK(4), PK(5)])
                    P.op("dve", lambda e: e.tensor_tensor(out=v16(ST), in0=v16(ST), in1=bc(cdbc.unsqueeze(2), [128, 16, 64]), op=ALU.mult), reads=["ST", "sm_c"], writes=["ST"])
                    P.op("dve", lambda e: e.tensor_tensor(out=ST, in0=ST, in1=ps2(4), op=ALU.add), reads=["ST", PK(4), PK(5)], writes=["ST"])
                    P.op("act", lambda e: e.activation(out=ST_bf, in_=ST, func=AF.Identity), reads=["ST"], writes=["ST_bf"])
                    if t == NTL - 2:
                        P.dma("sp", ssd_out[l, 0], ST, reads=["ST"])
                else:
                    P.op("dve", lambda e, t=t: e.tensor_tensor(out=Ysel, in0=bc(dA_tok[:, t, :].unsqueeze(1), [128, 16, 16]), in1=bc(cs[:, SEQM:SEQM + 16].unsqueeze(2), [128, 16, 16]), op=ALU.mult),
                         reads=["dA_tok", "cs"], writes=["Ysel"])
                    P.op("pe", lambda e: e.matmul(ps[:, 3, 0:256], lhsT=ones, rhs=Ysel.rearrange("p s h -> p (s h)"), start=True, stop=True),
                         reads=["Ysel", "cs"], writes=[PK(3)])
                    P.op("act", lambda e: e.activation(out=cd_all, in_=ps[:, 3, 0:256].rearrange("p (s h) -> p s h", s=16), func=AF.Exp), reads=[PK(3)], writes=["cd_all"])
                    for sq in range(16):
                        Bm, Bmk = Bm_r.next()
                        S0f, S0fk = S0f_r.next()
                        pb = 4 + 2 * (sq % 2)
                        P.op("dve", lambda e, sq=sq, Bm=Bm: e.tensor_scalar(out=Bm, in0=Btm_bf, scalar1=cs[:, SEQM + sq:SEQM + sq + 1], scalar2=None, op0=ALU.mult),
                             reads=["Btm_bf", "cs"], writes=[Bmk])
                        for g in range(4):
                            P.op("pe", lambda e, g=g, Bm=Bm, pb=pb: e.matmul(ps2(pb)[:, g * 256:(g + 1) * 256], lhsT=Bm[:, g * 128:(g + 1) * 128], rhs=Xd_bf[:, g * 256:(g + 1) * 256], start=True, stop=True),
                                 reads=[Bmk, "Xd_bf", "t1"], writes=[PK(pb), PK(pb + 1)])
                        P.dma("sp", S0f, ssd_state[l, sq], writes=[S0fk])
                        P.op("dve", lambda e, sq=sq, S0f=S0f: e.tensor_tensor(out=v16(S0f), in0=v16(S0f), in1=bc(cd_all[:, sq, :].unsqueeze(2), [128, 16, 64]), op=ALU.mult),
                             reads=[S0fk, "cd_all"], writes=[S0fk])
                        P.op("dve", lambda e, S0f=S0f, pb=pb: e.tensor_tensor(out=S0f, in0=S0f, in1=ps2(pb), op=ALU.add), reads=[S0fk, PK(pb), PK(pb + 1)], writes=[S0fk])
                        P.dma("sp", ssd_out[l, 1 + sq], S0f, reads=[S0fk])
            P.barrier()

        if want("S7"):
            AR.reset(base_off)
            ya_all = AR.alloc([128, 8, T], BF16)
            yb_all = AR.alloc([128, 8, T], BF16)
            for k in range(8):
                P.dma("sp", ya_all[:, k, :], yaT_d[k], reads=["yaT_d"], writes=["ya_all"])
                P.dma("act", yb_all[:, k, :], ybT_d[k], reads=["ybT_d"], writes=["yb_all"])
            wr = Rot("w", [128, 8, 128], BF16, 8)
            sga_r = Rot("sga", [128, 512], F32, 2)
            sgb_r = Rot("sgb", [128, 512], F32, 2)
            mixc_r = Rot("mixc", [128, 512], BF16, 3)
            for j in range(8):
                w1, k1_ = load_w(wr, w_so[l], j * 128, 128)
                w2, k2_ = load_w(wr, w_cf[l], j * 128, 128)
                w3, k3_ = load_w(wr, w_in[l], COL_GLU + j * 128, 128)
                w4, k4_ = load_w(wr, w_in[l], COL_GLU + 1024 + j * 128, 128)
                for ti, (c0, n) in enumerate(NTILES):
                    pb = 4 * (ti % 2)
                    proj(pb, w1, k1_, lambda k: ya_all[:, k, c0:c0 + n], n, ["ya_all"])
                    proj(pb + 1, w2, k2_, lambda k: yb_all[:, k, c0:c0 + n], n, ["yb_all"])
                    proj(pb + 2, w3, k3_, lambda k: hn[:, k, c0:c0 + n], n, ["hn"])
                    proj(pb + 3, w4, k4_, lambda k: hn[:, k, c0:c0 + n], n, ["hn"])
                    sga, sgak = sga_r.next()
                    sgb, sgbk = sgb_r.next()
                    mixc, mixk = mixc_r.next()
                    P.op("act", lambda e: e.activation(out=sga[:, 0:n], in_=psb(pb + 2, n), func=AF.Sigmoid), reads=[PK(pb + 2)], writes=[sgak])
                    P.op("act", lambda e: e.activation(out=sgb[:, 0:n], in_=psb(pb + 3, n), func=AF.Sigmoid), reads=[PK(pb + 3)], writes=[sgbk])
                    P.op("dve", lambda e: e.tensor_tensor(out=sga[:, 0:n], in0=psb(pb, n), in1=sga[:, 0:n], op=ALU.mult), reads=[PK(pb), sgak], writes=[sgak])
                    P.op("dve", lambda e: e.tensor_tensor(out=sgb[:, 0:n], in0=psb(pb + 1, n), in1=sgb[:, 0:n], op=ALU.mult), reads=[PK(pb + 1), sgbk], writes=[sgbk])
                    P.op("dve", lambda e: e.tensor_tensor(out=mixc[:, 0:n], in0=sga[:, 0:n], in1=sgb[:, 0:n], op=ALU.add), reads=[sgak, sgbk], writes=[mixk])
                    P.dma("sp", mixT_d[j][:, c0:c0 + n], mixc[:, 0:n], reads=[mixk], writes=["mixT_d"])
            P.barrier()
            AR.reset(base_off)
            wo_all = [AR.alloc([128, 8, 128], BF16) for _ in range(8)]
            for j in range(8):
                P.dma("pool", wo_all[j], w_o[l].rearrange("(k p) c -> p k c", p=128)[:, :, j * 128:(j + 1) * 128], writes=[("wo", j)])
            mixt_r = Rot("mixt", [128, 8, 512], BF16, 2)
            for ti, (c0, n) in enumerate(NTILES):
                mixt, mixtk = mixt_r.next()
                P.dma("sp", mixt[:, :, 0:n], mixT_d.rearrange("k p t -> p k t")[:, :, c0:c0 + n], reads=["mixT_d"], writes=[mixtk])
                for j in range(8):
                    pb = j % 8
                    proj(pb, wo_all[j], ("wo", j), lambda k: mixt[:, k, 0:n], n, [mixtk])
                    P.op("dve", lambda e: e.tensor_tensor(out=h[:, j, c0:c0 + n], in0=h[:, j, c0:c0 + n], in1=psb(pb, n), op=ALU.add),
                         reads=["h", PK(pb)], writes=["h"])
            P.barrier()
        dump_h(2 * l)

        pump_state["auto"] = 0
        if want("PEER"):
            P.barrier()
            for k in range(8):
                P.dma("sp" if k % 2 == 0 else "act", hD[k], h[:, k, :], reads=["h"], writes=["hD"])
            P.barrier()
            for b in range(16):
                P.dma("pool", wqb[l, b].rearrange("p (k c) -> p k c", k=8), wq[l].rearrange("(k p) c -> p k c", p=128)[:, :, b * 128:(b + 1) * 128], writes=[("wqb", l, b)])
            AR.reset()
            AR2 = Arena(h.rearrange("p k t -> p (k t)").bitcast(BF16), cap=8 * T * 2)
            hb = [AR2.alloc([128, 8, 256], F32) for _ in range(2)]
            Gs = [AR.alloc([128, 256, 64], BF16), AR2.alloc([128, 256, 64], BF16)]
            hsq = AR.alloc([128, 8, 256], F32)
            rs = AR.alloc([128, 256], F32)
            xnS = [AR.alloc([128, 8, 256], BF16) for _ in range(2)]
            keys_sb = AR.alloc([128, 16, 128], BF16)
            P.dma("pool", keys_sb, keysT[l].rearrange("b d k -> d b k"), writes=["keys_sb"])
            pump(512, layer=l)
            if l + 1 < n_layers:
                pump(512, layer=l + 1)
            wr = Rot("w", [128, 8, 128], BF16, 2)
            qT = AR.alloc([128, 16, 256], BF16)
            S_sb = AR.alloc([128, 16, 128], F32)
            V = AR.alloc([128, 16, 16], F32)
            I = AR.alloc([128, 16, 16], U32)
            If = AR.alloc([128, 16, 16], F32)
            cand = S_sb.rearrange("p (h a) (b c) -> p h (a b) c", h=8, c=16)
            eq = hsq.rearrange("p a (b c) -> p a b c", b=16)
            Tv = AR.alloc([128, 8, 16], F32)
            POS = AR.alloc([128, 8, 16], U32)
            PA = AR.alloc([128, 8, 16], U32)
            PB = AR.alloc([128, 8, 16], U32)
            Af = AR.alloc([128, 8, 16], F32)
            Bf = AR.alloc([128, 8, 16], F32)
            E = AR.alloc([128, 8, 16], F32)
            Z = AR.alloc([128, 8], F32)
            gate = AR.alloc([128, 8, 16], F32)
            i0 = AR.alloc([128, 8, 16], F32)
            i1 = AR.alloc([128, 8, 16], F32)
            idxS = [[AR.alloc([128, 256], BF16) for _ in range(3)] for _ in range(2)]
            io128b = AR.alloc([128, 128], BF16)
            P.op("dve", lambda e: e.tensor_copy(out=io128b, in_=cs[:, IO128:IO128 + 128]), reads=["cs"], writes=["io128b"])
            A_r = Rot("A", [128, 32, 128], BF16, 2)
            B_r = Rot("B", [128, 32, 64], BF16, 2)
            ub_r = Rot("ub", [128, 8, 128], BF16, 8, arena=AR2)
            vb_r = Rot("vb", [128, 1024], BF16, 8)
            ge_r = Rot("ge", [128, 256], F32, 4)
            act_r = Rot("actb", [128, 256], BF16, 5)
            io16 = cs[:, IO16:IO16 + 16]
            blocks = [(c, 256) for c in range(0, TP, 256)] + [(TP, 128)]
            V4 = V.rearrange("p (h j) a -> p h j a", j=2)
            If4 = If.rearrange("p (h j) a -> p h j a", j=2)
            SK16 = [("S", b_) for b_ in range(16)]
            P2 = [PK(2)]

            def tile_phase(bs, c0, n):
                xn_ = xnS[bs]
                i0T_, i1T_, gT_ = idxS[bs]
                xk, ik, hk = ("xn", bs), ("idxT", bs), ("hb", bs)
                hb_ = hb[bs]
                P.dma("sp", hb_[:, :, 0:n], hD.rearrange("k p t -> p k t")[:, :, c0:c0 + n], reads=["hD"], writes=[hk])
                s3 = hb_[:, :, 0:n]
                P.op("act", lambda e: e.activation(out=hsq[:, :, 0:n], in_=s3, func=AF.Square), reads=[hk], writes=["hsq"])
                for k in range(8):
                    P.op("pe", lambda e: e.matmul(psb(2, n), lhsT=ones, rhs=hsq[:, k, 0:n], start=(k == 0), stop=(k == 7)), reads=["hsq", "cs"], writes=P2)
                P.op("act", lambda e: e.activation(out=rs[:, 0:n], in_=psb(2, n), func=AF.Sqrt, bias=EPS, scale=1.0 / 1024), reads=P2, writes=["rs"])
                P.op("dve", lambda e: e.reciprocal(out=rs[:, 0:n], in_=rs[:, 0:n]), reads=["rs"], writes=["rs"])
                yield
                for k in range(8):
                    P.op("dve", lambda e: e.scalar_tensor_tensor(out=xn_[:, k, 0:n], in0=hb_[:, k, 0:n], scalar=cv[:, l, G_FFN + k:G_FFN + k + 1], in1=rs[:, 0:n], op0=ALU.mult, op1=ALU.mult),
                         reads=[hk, "rs", "cv"], writes=[xk])
                    if k % 2 == 1:
                        yield
                for b in range(16):
                    wb, wbk = wr.next()
                    P.dma("sp", wb, wqb[l, b].rearrange("p (k c) -> p k c", k=8), reads=[("wqb", l, b)], writes=[wbk])
                    hf = 2 + b % 2
                    pq = ps[:, hf, 0:n]
                    for k in range(8):
                        P.op("pe", lambda e: e.matmul(pq, lhsT=wb[:, k, :], rhs=xn_[:, k, 0:n], start=(k == 0), stop=(k == 7)), reads=[wbk, xk], writes=[PK(hf)])
                    P.op("act", lambda e: e.activation(out=qT[:, b, 0:n], in_=pq, func=AF.Identity), reads=[PK(hf)], writes=[("qT", b)])
                    yield
                for ti in range(n // 128):
                    tc0 = ti * 128
                    for p_ in range(4):
                        for bb in range(4):
                            b = 4 * p_ + bb
                            P.op("pe", lambda e: e.matmul(ps[:, 2 + p_ % 2, bb * 128:bb * 128 + 128], lhsT=qT[:, b, tc0:tc0 + 128], rhs=keys_sb[:, b, :], start=True, stop=True),
                                 reads=[("qT", b), "keys_sb"], writes=[PK(2 + p_ % 2)])
                        P.op("act", lambda e: e.activation(out=S_sb[:, 4 * p_:4 * p_ + 4, :], in_=ps[:, 2 + p_ % 2, :].rearrange("p (b c) -> p b c", c=128), func=AF.Identity),
                             reads=[PK(2 + p_ % 2)], writes=[("S", b_) for b_ in range(4 * p_, 4 * p_ + 4)])
                        yield
                    for stg in range(5):
                        for b in range(16):
                            sb_ = S_sb[:, b, :]
                            if stg == 0:
                                P.op("dve", lambda e: e.max(out=V[:, b, 0:8], in_=sb_), reads=[("S", b)], writes=[("V", b)])
                            elif stg == 1:
                                P.op("dve", lambda e: e.max_index(out=I[:, b, 0:8], in_max=V[:, b, 0:8], in_values=sb_), reads=[("S", b), ("V", b)], writes=[("I", b)])
                            elif stg == 2:
                                P.op("dve", lambda e: e.match_replace(out=sb_, in_to_replace=V[:, b, 0:8], in_values=sb_, imm_value=-1e30), reads=[("S", b), ("V", b)], writes=[("S", b)])
                            elif stg == 3:
                                P.op("dve", lambda e: e.max(out=V[:, b, 8:16], in_=sb_), reads=[("S", b)], writes=[("V2", b)])
                            else:
                                P.op("dve", lambda e: e.max_index(out=I[:, b, 8:16], in_max=V[:, b, 8:16], in_values=sb_), reads=[("S", b), ("V2", b)], writes=[("I2", b)])
                            if b % 4 == 3:
                                yield
                    P.op("dve", lambda e: e.tensor_copy(out=If, in_=I), reads=[("I", b_) for b_ in range(16)] + [("I2", b_) for b_ in range(16)], writes=["If"])
                    P.op("dve", lambda e: e.tensor_tensor(out=cand, in0=bc(V4[:, :, 0, :].unsqueeze(3), [128, 8, 16, 16]), in1=bc(V4[:, :, 1, :].unsqueeze(2), [128, 8, 16, 16]), op=ALU.add),
                         reads=[("V", b_) for b_ in range(16)] + [("V2", b_) for b_ in range(16)], writes=[("cand", h_) for h_ in range(8)] + SK16)
                    yield
                    for stg in range(5):
                        for hh in range(8):
                            ch = cand[:, hh].rearrange("p a b -> p (a b)")
                            if stg == 0:
                                P.op("dve", lambda e: e.max(out=Tv[:, hh, 0:8], in_=ch), reads=[("cand", hh)], writes=[("Tv", hh)])
                            elif stg == 1:
                                P.op("dve", lambda e: e.max_index(out=POS[:, hh, 0:8], in_max=Tv[:, hh, 0:8], in_values=ch), reads=[("cand", hh), ("Tv", hh)], writes=[("POS", hh)])
                            elif stg == 2:
                                P.op("dve", lambda e: e.match_replace(out=ch, in_to_replace=Tv[:, hh, 0:8], in_values=ch, imm_value=-1e30), reads=[("cand", hh), ("Tv", hh)], writes=[("cand", hh)])
                            elif stg == 3:
                                P.op("dve", lambda e: e.max(out=Tv[:, hh, 8:16], in_=ch), reads=[("cand", hh)], writes=[("Tv2", hh)])
                            else:
                                P.op("dve", lambda e: e.max_index(out=POS[:, hh, 8:16], in_max=Tv[:, hh, 8:16], in_values=ch), reads=[("cand", hh), ("Tv2", hh)], writes=[("POS2", hh)])
                            if hh % 4 == 3:
                                yield
                    TVK = [("Tv", h_) for h_ in range(8)] + [("Tv2", h_) for h_ in range(8)]
                    PSK = [("POS", h_) for h_ in range(8)] + [("POS2", h_) for h_ in range(8)]
                    P.op("dve", lambda e: e.tensor_tensor(out=E, in0=Tv, in1=bc(Tv[:, :, 0:1], [128, 8, 16]), op=ALU.subtract), reads=TVK, writes=["E"])
                    P.op("act", lambda e: e.activation(out=E, in_=E, func=AF.Exp), reads=["E"], writes=["E"])
                    P.op("dve", lambda e: e.tensor_single_scalar(out=PB, in_=POS, scalar=15, op=ALU.bitwise_and), reads=PSK, writes=["PB"])
                    P.op("dve", lambda e: e.tensor_single_scalar(out=PA, in_=POS, scalar=4, op=ALU.logical_shift_right), reads=PSK, writes=["PA"])
                    yield
                    P.op("dve", lambda e: e.reduce_sum(out=Z, in_=E, axis=AX.X), reads=["E"], writes=["Z"])
                    P.op("dve", lambda e: e.tensor_copy(out=Af, in_=PA), reads=["PA"], writes=["Af"])
                    P.op("dve", lambda e: e.reciprocal(out=Z, in_=Z), reads=["Z"], writes=["Z"])
                    P.op("dve", lambda e: e.tensor_copy(out=Bf, in_=PB), reads=["PB"], writes=["Bf"])
                    P.op("dve", lambda e: e.tensor_tensor(out=gate, in0=E, in1=bc(Z.unsqueeze(2), [128, 8, 16]), op=ALU.mult), reads=["E", "Z"], writes=["gate"])
                    yield
                    for (xf, jj, dst, dk) in ((Af, 0, i0, "i0"), (Bf, 1, i1, "i1")):
                        P.op("dve", lambda e: e.tensor_tensor(out=eq, in0=bc(io16.unsqueeze(1).unsqueeze(1), [128, 8, 16, 16]), in1=bc(xf.unsqueeze(3), [128, 8, 16, 16]), op=ALU.is_equal),
                             reads=["cs", "Af", "Bf"], writes=["hsq"])
                        yield
                        P.op("dve", lambda e: e.tensor_tensor(out=eq, in0=eq, in1=bc(If4[:, :, jj, :].unsqueeze(2), [128, 8, 16, 16]), op=ALU.mult), reads=["hsq", "If"], writes=["hsq"])
                        yield
                        P.op("dve", lambda e: e.reduce_sum(out=dst, in_=eq, axis=AX.X), reads=["hsq"], writes=[dk])
                        yield
                    for q_, (src, dstT, sk) in enumerate(((i0, i0T_, "i0"), (i1, i1T_, "i1"), (gate, gT_, "gate"))):
                        P.op("pe", lambda e: e.transpose(out=ps[:, 2, q_ * 128:(q_ + 1) * 128], in_=src.rearrange("p h r -> p (h r)"), identity=ident),
                             reads=[sk, "cs"], writes=P2)
                        P.op("act", lambda e: e.activation(out=dstT[:, tc0:tc0 + 128], in_=ps[:, 2, q_ * 128:(q_ + 1) * 128], func=AF.Identity),
                             reads=P2, writes=[ik])
                    yield

            def g_build(bs, n, half):
                i0T, i1T, gT = idxS[bs]
                ik = ("idxT", bs)
                G = Gs[half]
                gk = ("G", half)

                def pe_part(A_, Ak, B_, Bk, s0):
                    for q_ in range(4):
                        pg = 2 + q_ % 2
                        for tk in range(8):
                            tok = q_ * 8 + tk
                            P.op("pe", lambda e: e.matmul(ps[:, pg, tk * 64:(tk + 1) * 64], lhsT=A_[:, tok, :], rhs=B_[:, tok, :], start=True, stop=True),
                                 reads=[Ak, Bk], writes=[PK(pg)])
                        gdst = G[:, s0 + q_ * 8:s0 + q_ * 8 + 8, :]
                        P.op("act", lambda e: e.activation(out=gdst, in_=ps[:, pg, :].rearrange("p (a b) -> p a b", a=8), func=AF.Identity),
                             reads=[PK(pg)], writes=[gk])
                        yield

                prev = None
                for sub in range(n // 32):
                    s0 = sub * 32
                    A_, Ak = A_r.next()
                    B_, Bk = B_r.next()
                    P.op("dve", lambda e: e.tensor_tensor(out=A_, in0=bc(io128b.unsqueeze(1), [128, 32, 128]), in1=bc(i0T[:, s0:s0 + 32].unsqueeze(2), [128, 32, 128]), op=ALU.is_equal),
                         reads=["io128b", ik], writes=[Ak])
                    yield
                    P.op("dve", lambda e: e.tensor_tensor(out=B_, in0=bc(io128b[:, half * 64:(half + 1) * 64].unsqueeze(1), [128, 32, 64]),
                                                          in1=bc(i1T[:, s0:s0 + 32].unsqueeze(2), [128, 32, 64]), op=ALU.is_equal),
                         reads=["io128b", ik], writes=[Bk])
                    yield
                    P.op("dve", lambda e: e.tensor_tensor(out=B_, in0=B_, in1=bc(gT[:, s0:s0 + 32].unsqueeze(2), [128, 32, 64]), op=ALU.mult),
                         reads=[Bk, ik], writes=[Bk])
                    yield
                    if prev is not None:
                        for _ in pe_part(*prev):
                            yield
                    prev = (A_, Ak, B_, Bk, s0)
                for _ in pe_part(*prev):
                    yield

            def chain(*gens):
                for g in gens:
                    if g is not None:
                        for _ in g:
                            yield

            def pump_gen(g, k):
                if g is None:
                    return
                for _ in range(k):
                    if next(g, "END") == "END":
                        return

            for _ in chain(tile_phase(0, *blocks[0]), g_build(0, blocks[0][1], 0)):
                pass
            for bi, (c0, n) in enumerate(blocks):
                bs = bi % 2
                xn = xnS[bs]
                xk, hk = ("xn", bs), ("hb", bs)
                has_next = bi + 1 < len(blocks)
                gb1 = g_build(bs, n, 1)
                tpn = tile_phase((bi + 1) % 2, *blocks[bi + 1]) if has_next else None
                gb0n = g_build((bi + 1) % 2, blocks[bi + 1][1], 0) if has_next else None
                for half in range(2):
                    G = Gs[half]
                    gk = ("G", half)
                    if half == 0:
                        bg = chain(gb1, tpn)
                    else:
                        for _ in gb1:
                            pass
                        bg = chain(tpn, gb0n)

                    def stage_pre(k1h):
                        k1 = half * 64 + k1h
                        ub, ubk = ub_r.next()
                        vb, vbk = vb_r.next()
                        P.dma("sp", ub, uTb[l, k1].rearrange("p (k c) -> p k c", k=8), reads=[("uTb", l, k1)], writes=[ubk])
                        P.dma("sp", vb, vKb[l, k1], reads=[("vKb", l, k1)], writes=[vbk])
                        pp = k1h % 2
                        for k in range(8):
                            P.op("pe", lambda e: e.matmul(psb(pp, n), lhsT=ub[:, k, :], rhs=xn[:, k, 0:n], start=(k == 0), stop=(k == 7)),
                                 reads=[ubk, xk], writes=[PK(pp)])
                        ge, gek = ge_r.next()
                        ab, abk = act_r.next()
                        P.op("act", lambda e: e.activation(out=ge[:, 0:n], in_=psb(pp, n), func=AF.Gelu_apprx_tanh), reads=[PK(pp)], writes=[gek])
                        P.op("pool", lambda e: e.tensor_tensor(out=ab[:, 0:n], in0=ge[:, 0:n], in1=G[:, 0:n, k1h], op=ALU.mult),
                             reads=[gek, gk], writes=[abk])
                        return vb, vbk, ab, abk

                    def stage_v(k1h, vb, vbk, ab, abk):
                        for dc in range(8):
                            P.op("pe", lambda e: e.matmul(ps[:, 4 + dc // 2, (dc % 2) * 256:(dc % 2) * 256 + n], lhsT=vb[:, dc * 128:(dc + 1) * 128], rhs=ab[:, 0:n],
                                                          start=(half == 0 and k1h == 0), stop=(half == 1 and k1h == 63)),
                                 reads=[vbk, abk], writes=[PK(4 + dc // 2)])

                    if half == 1:
                        pass
                    LOOK = 2
                    pend = [stage_pre(i) for i in range(LOOK)]
                    for k1h in range(64):
                        if k1h + LOOK < 64:
                            pend.append(stage_pre(k1h + LOOK))
                        stage_v(k1h, *pend.pop(0))
                        pump_gen(bg, PUMP)
                for _ in chain(tpn, gb0n):
                    pass
                hb_ = hb[bs]
                for dc in range(8):
                    P.op("dve", lambda e: e.tensor_tensor(out=hb_[:, dc, 0:n], in0=hb_[:, dc, 0:n], in1=ps[:, 4 + dc // 2, (dc % 2) * 256:(dc % 2) * 256 + n], op=ALU.add),
                         reads=[hk, PK(4 + dc // 2)], writes=[hk])
                P.dma("sp", hD.rearrange("k p t -> p k t")[:, :, c0:c0 + n], hb_[:, :, 0:n], reads=[hk], writes=["hD"])
            P.barrier()
            for k in range(8):
                P.dma("sp" if k % 2 == 0 else "act", h[:, k, :], hD[k], reads=["hD"], writes=["h"])
            P.barrier()
        dump_h(2 * l + 1)

        if want("PLE"):
            P.barrier()
            AR.reset()
            hn = AR.alloc([128, 8, T], BF16)
            hsq = AR.alloc([128, 8, 512], F32)
            rs = AR.alloc([128, 512], F32)
            for ti, (c0, n) in enumerate(NTILES):
                norm_cols(l, c0, n, G_PLE, lambda k, c0=c0, n=n: hn[:, k, c0:c0 + n], ["hn"], hsq, rs, ti % 2, "ple")
            wr = Rot("w", [128, 8, 128], BF16, 4)
            pt_all = AR.alloc([128, 2, T], BF16)
            P.dma("pool", pt_all, pT[l].rearrange("k p t -> p k t"), writes=["pt_all"])
            sg_r = Rot("sg", [128, 512], F32, 2)
            for j in range(8):
                w1, k1_ = load_w(wr, w_pg[l], j * 128, 128)
                w2, k2_ = load_w(wr, w_pp[l], j * 128, 128, nk=2)
                for ti, (c0, n) in enumerate(NTILES):
                    pb = 2 * (ti % 4)
                    proj(pb, w1, k1_, lambda k: hn[:, k, c0:c0 + n], n, ["hn"])
                    proj(pb + 1, w2, k2_, lambda k: pt_all[:, k, c0:c0 + n], n, ["pt_all"], nk=2)
                    sg, sgk = sg_r.next()
                    P.op("act", lambda e: e.activation(out=sg[:, 0:n], in_=psb(pb, n), func=AF.Sigmoid), reads=[PK(pb)], writes=[sgk])
                    P.op("dve", lambda e: e.tensor_tensor(out=sg[:, 0:n], in0=sg[:, 0:n], in1=psb(pb + 1, n), op=ALU.mult), reads=[sgk, PK(pb + 1)], writes=[sgk])
                    P.op("dve", lambda e: e.tensor_tensor(out=h[:, j, c0:c0 + n], in0=h[:, j, c0:c0 + n], in1=sg[:, 0:n], op=ALU.add), reads=["h", sgk], writes=["h"])
            P.barrier()

    P.barrier()
    AR.reset()
    hsq = AR.alloc([128, 8, 512], F32)
    rs = AR.alloc([128, 512], F32)
    yo_r = Rot("yo", [128, 8, 512], F32, 2)
    for ti, (c0, n) in enumerate(NTILES):
        yo, yok = yo_r.next()
        norm_cols(0, c0, n, G_FIN, lambda k, yo=yo, n=n: yo[:, k, 0:n], [yok], hsq, rs, ti % 2, "fin")
        P.dma("sp", yT.rearrange("k p t -> p k t")[:, :, c0:c0 + n], yo[:, :, 0:n], reads=[yok])
    P.emit()
    return nc


def make_consts():
    c = np.zeros((128, NCONST), np.float32)
    s = np.arange(128)[:, None]
    l_ = np.arange(128)[None, :]
    c[:, IDENT:IDENT + 128] = (s == l_)
    c[:, ONES:ONES + 128] = 1.0
    tri = (s <= l_)
    same = (s // 8 == l_ // 8)
    c[:, TRI:TRI + 128] = tri
    c[:, TRIS:TRIS + 128] = tri & same
    c[:, BLKS:BLKS + 128] = same
    c[:, NEGP:NEGP + 128] = np.where(tri, 0.0, -1e5)
    c[:, NEGS:NEGS + 128] = np.where(tri & same, 0.0, -1e5)
    c[:, SEQM:SEQM + 16] = (s // 8 == np.arange(16)[None, :])
    c[:, IO16:IO16 + 16] = np.arange(16)[None, :]
    c[:, IO128:IO128 + 128] = np.arange(128)[None, :]
    sel = np.zeros((128, 16, 128), np.float32)
    for q in range(16):
        sel[8 * q, q, :] = 1.0
    return c, sel.reshape(128, 2048)


def cm(v):
    return np.asarray(v).reshape(-1, 128).T


def prepare_inputs(x_prompt, x_sample, state_ssd, state_ssd_conv, state_cf_conv, p_prompt, p_sample,
                   g_mix, w_in, ssd_conv_w, ssd_conv_b, ssd_dt_bias, ssd_a_log, ssd_d, ssd_norm_g,
                   w_ssd_out, cf_dw_w, cf_dw_b, cf_ln_g, cf_ln_b, w_cf_out, w_o, g_ffn,
                   peer_wq, peer_keys, peer_u, peer_v, g_ple, w_ple_gate, w_ple_proj, g_final):
    f = lambda a: np.ascontiguousarray(np.asarray(a, dtype=np.float32))
    consts, selall = make_consts()
    cvec = np.zeros((128, 2, NCV), np.float32)
    rvec = np.zeros((2, NRV), np.float32)
    for l in range(2):
        cvec[:, l, G_MIX:G_MIX + 8] = cm(g_mix[l])
        cvec[:, l, G_FFN:G_FFN + 8] = cm(g_ffn[l])
        cvec[:, l, G_PLE:G_PLE + 8] = cm(g_ple[l])
        cvec[:, l, G_FIN:G_FIN + 8] = cm(g_final)
        cvec[:, l, SCW:SCW + 64] = np.asarray(ssd_conv_w[l]).reshape(4, 16, 128).transpose(2, 1, 0).reshape(128, 64)
        cvec[:, l, SCB:SCB + 16] = cm(ssd_conv_b[l])
        cvec[:, l, CFW:CFW + 248] = np.asarray(cf_dw_w[l]).reshape(31, 8, 128).transpose(2, 1, 0).reshape(128, 248)
        cvec[:, l, CFB:CFB + 8] = cm(cf_dw_b[l])
        rvec[l, DTB:DTB + 16] = ssd_dt_bias[l]
        rvec[l, ALOG:ALOG + 16] = ssd_a_log[l]
        rvec[l, DSK:DSK + 16] = ssd_d[l]
        rvec[l, NG:NG + 1024] = ssd_norm_g[l]
        rvec[l, LNG:LNG + 1024] = cf_ln_g[l]
        rvec[l, LNB:LNB + 1024] = cf_ln_b[l]
    keysT = f(np.asarray(peer_keys).reshape(2, 16, 128, 128).transpose(0, 1, 3, 2))
    uT = f(np.asarray(peer_u).reshape(2, 128, 128, 8, 128).transpose(0, 2, 4, 3, 1).reshape(2, 128, 128, 1024))
    vK = f(np.asarray(peer_v).reshape(2, 128, 128, 1024).transpose(0, 2, 1, 3))
    shared = dict(consts=consts, selall=selall, cvec=cvec, rvec=rvec, w_in=f(w_in), w_ssd_out=f(w_ssd_out), w_cf_out=f(w_cf_out),
                  w_o=f(w_o), peer_wq=f(peer_wq), keysT=keysT, uT=uT, vK=vK, w_ple_gate=f(w_ple_gate), w_ple_proj=f(w_ple_proj))
    in_maps = []
    for c in range(8):
        sq = slice(16 * c, 16 * c + 16)
        xt = np.concatenate([np.asarray(x_prompt[c]), np.asarray(x_sample[sq]).reshape(128, 1024)], axis=0)
        xT = f(xt.T.reshape(8, 128, T))
        pt = np.concatenate([np.asarray(p_prompt[:, c]), np.asarray(p_sample[:, sq]).reshape(2, 128, 256)], axis=1)
        pT = f(pt.transpose(0, 2, 1).reshape(2, 2, 128, T))
        cfh = f(np.asarray(state_cf_conv[:, sq]).transpose(0, 3, 1, 2).reshape(2, 8, 128, 16, 30))
        sh = f(np.asarray(state_ssd_conv[:, sq]).transpose(0, 3, 1, 2).reshape(2, 16, 128, 16, 3))
        ss = f(np.asarray(state_ssd[:, sq]).reshape(2, 16, 1024, 128).transpose(0, 1, 3, 2))
        m = dict(shared)
        m.update(xT=xT, pT=pT, cf_hist=cfh, ssd_hist=sh, ssd_state=ss)
        in_maps.append(m)
    return in_maps


def assemble(results):
    y_p = np.zeros((8, 2048, 1024), np.float32)
    y_s = np.zeros((128, 8, 1024), np.float32)
    ssd_p = np.zeros((2, 8, 16, 64, 128), np.float32)
    sconv_p = np.zeros((2, 8, 3, 2048), np.float32)
    cf_p = np.zeros((2, 8, 30, 1024), np.float32)
    ssd_s = np.zeros((2, 128, 16, 64, 128), np.float32)
    sconv_s = np.zeros((2, 128, 3, 2048), np.float32)
    cf_s = np.zeros((2, 128, 30, 1024), np.float32)
    for c, r in enumerate(results):
        sq = slice(16 * c, 16 * c + 16)
        y = np.asarray(r["yT"]).reshape(1024, T).T
        y_p[c] = y[:2048]
        y_s[sq] = y[2048:].reshape(16, 8, 1024)
        so = np.asarray(r["ssd_out"]).transpose(0, 1, 3, 2).reshape(2, 17, 16, 64, 128)
        ssd_p[:, c] = so[:, 0]
        ssd_s[:, sq] = so[:, 1:]
        sc = np.asarray(r["sconv_out"]).reshape(2, 2048, 17, 3).transpose(0, 2, 3, 1)
        sconv_p[:, c] = sc[:, 0]
        sconv_s[:, sq] = sc[:, 1:]
        cf = np.asarray(r["cfconv_out"]).reshape(2, 1024, 17, 30).transpose(0, 2, 3, 1)
        cf_p[:, c] = cf[:, 0]
        cf_s[:, sq] = cf[:, 1:]
    return (y_p, y_s, ssd_p, sconv_p, cf_p, ssd_s, sconv_s, cf_s)


def kernel(**inputs):
    in_maps = prepare_inputs(**inputs)
    nc = build_program()
    res = run_bass_kernel_spmd(nc, in_maps, core_ids=list(range(8)))
    return assemble(res.results)
```
